# Optimizing a Trainium2 kernel written in Bass

```python
import math
import jax, jax.numpy as jnp
from jax import lax
import numpy as np

D_MODEL = 2048
BATCH = 8
SEQ = 2048
DEPTH = 4
DEC_BATCH = 8
DEC_SEQ = 16
PAST_LEN = 4096

CHUNK = 64
N_MIXERS = 3
HEAD_DIM = 128
MIX_WIDTH = 3 * D_MODEL // 4
MIX_HEADS = MIX_WIDTH // HEAD_DIM
MEM_WIDTH = D_MODEL // 4
MEM_HEADS = MEM_WIDTH // HEAD_DIM
GATE_WIDTH = MIX_WIDTH + MEM_WIDTH
N_MEM = 256
Q_BLOCK = 128
A_KV_HEADS = 4
A_WINDOW = 128
A_PREV_CHUNKS = A_WINDOW // CHUNK
T5_BUCKETS = 32
T5_MAX_DIST = 128
B_Q_RANK = 512
B_KV_RANK = 256
B_NOPE = 128
B_ROPE = 64
B_V = 128
ROPE_THETA = 10000.0
C_PREV_CHUNKS = 8
C_REL_CLIP = 128
N_A = (DEPTH + 2) // 3
N_B = (DEPTH + 1) // 3
N_C = DEPTH // 3
EPS = 1e-6
NEG_INF = -1e30
A_SPLITS = (MIX_WIDTH, A_KV_HEADS * HEAD_DIM, A_KV_HEADS * HEAD_DIM, MEM_WIDTH, GATE_WIDTH)
B_SPLITS = (B_Q_RANK, B_KV_RANK, B_ROPE, MEM_WIDTH, GATE_WIDTH)
C_SPLITS = (MIX_WIDTH, MIX_WIDTH, MIX_WIDTH, MEM_WIDTH, GATE_WIDTH)

kernel_name = "hybrid_streaming_encoder_step"


def rmsnorm(x, g):
    xf = x.astype(jnp.float32)
    y = xf * lax.rsqrt(jnp.mean(xf * xf, axis=-1, keepdims=True) + EPS)
    return (y * g.astype(jnp.float32)).astype(x.dtype)


def split_cols(z, sizes):
    idx = np.cumsum(np.array(sizes))[:-1].tolist()
    return jnp.split(z, idx, axis=-1)


def attend(q, k, v, bias, valid, sink):
    b, nq, h, d = q.shape
    kvh = k.shape[2]
    g = h // kvh
    qg = q.reshape(b, nq, kvh, g, d)
    s = jnp.einsum("bqhgd,bkhd->bhgqk", qg, k, preferred_element_type=jnp.float32) * (d ** -0.5)
    if bias is not None:
        s = s + bias.astype(jnp.float32).reshape(kvh, g, nq, bias.shape[-1])
    if valid is not None:
        s = jnp.where(valid, s, NEG_INF)
    if sink is None:
        p = jax.nn.softmax(s, axis=-1)
    else:
        sk = sink.astype(jnp.float32).reshape(kvh, g, 1, 1)
        m = jnp.maximum(jnp.max(s, axis=-1, keepdims=True), sk)
        e = jnp.exp(s - m)
        p = e / (jnp.sum(e, axis=-1, keepdims=True) + jnp.exp(sk - m))
    o = jnp.einsum("bhgqk,bkhd->bqhgd", p.astype(v.dtype), v)
    return o.reshape(b, nq, h, v.shape[-1])


def t5_bucket(rel):
    half = T5_BUCKETS // 2
    max_exact = half // 2
    ret = jnp.where(rel < 0, half, 0)
    n = jnp.abs(rel)
    nf = jnp.maximum(n, 1).astype(jnp.float32)
    large = max_exact + (jnp.log(nf / max_exact) / math.log(T5_MAX_DIST / max_exact)
                         * (half - max_exact)).astype(jnp.int32)
    large = jnp.minimum(large, half - 1)
    return ret + jnp.where(n < max_exact, n, large)


def t5_rel_bias(table, rel):
    return jnp.moveaxis(jnp.take(table, t5_bucket(rel), axis=0), -1, 0)


def clipped_rel_bias(table, rel):
    return table[:, jnp.clip(rel, -C_REL_CLIP, C_REL_CLIP) + C_REL_CLIP]


def band_attention_prompt(q, k, v, n_prev, bias, sink):
    b, s, h, d = q.shape
    pad = n_prev * CHUNK
    band = pad + CHUNK
    n_chunks = s // CHUNK
    kp = jnp.pad(k, ((0, 0), (pad, 0), (0, 0), (0, 0)))
    vp = jnp.pad(v, ((0, 0), (pad, 0), (0, 0), (0, 0)))
    qc = jnp.moveaxis(q.reshape(b, n_chunks, CHUNK, h, d), 1, 0)
    k_off = jnp.arange(band) - pad

    def one(args):
        c, qb = args
        start = c * CHUNK
        kb = lax.dynamic_slice_in_dim(kp, start, band, axis=1)
        vb = lax.dynamic_slice_in_dim(vp, start, band, axis=1)
        valid = (start + k_off >= 0)[None, :]
        return attend(qb, kb, vb, bias, valid, sink)

    out = lax.map(one, (jnp.arange(n_chunks), qc))
    return jnp.moveaxis(out, 0, 1).reshape(b, s, h, v.shape[-1])


def band_mix(q, k, v, past_k, past_v, n_prev, bias_fn, sink):
    s = q.shape[1]
    if past_k is None:
        pad = n_prev * CHUNK
        rel = jnp.arange(CHUNK)[:, None] - (jnp.arange(pad + CHUNK) - pad)[None, :]
        out = band_attention_prompt(q, k, v, n_prev, bias_fn(rel), sink)
        keep = min(pad, s)
        return out, k[:, s - keep:], v[:, s - keep:]
    p = past_k.shape[1]
    rel = (p + jnp.arange(s))[:, None] - jnp.arange(p + s)[None, :]
    out = attend(q, jnp.concatenate([past_k, k], axis=1), jnp.concatenate([past_v, v], axis=1),
                 bias_fn(rel), None, sink)
    return out, k, v


def rope_cos_sin(pos):
    half = B_ROPE // 2
    inv = ROPE_THETA ** (-jnp.arange(half, dtype=jnp.float32) / half)
    ang = pos.astype(jnp.float32)[:, None] * inv[None, :]
    return jnp.cos(ang), jnp.sin(ang)


def apply_rope(x, cos, sin):
    xf = x.astype(jnp.float32)
    x1, x2 = jnp.split(xf, 2, axis=-1)
    return jnp.concatenate([x1 * cos - x2 * sin, x1 * sin + x2 * cos], axis=-1).astype(x.dtype)


def mla_keys(c_kv, k_rope, w_kv_b, g_k):
    b, n, _ = c_kv.shape
    kv = (c_kv @ w_kv_b).reshape(b, n, MIX_HEADS, B_NOPE + B_V)
    k_nope, v = kv[..., :B_NOPE], kv[..., B_NOPE:]
    k = jnp.concatenate([k_nope, jnp.broadcast_to(k_rope[:, :, None, :], (b, n, MIX_HEADS, B_ROPE))], axis=-1)
    return rmsnorm(k, g_k), v


def mla_prompt_attend(q, k, v):
    b, s, h, d = q.shape
    nb = s // Q_BLOCK
    qb = jnp.moveaxis(q.reshape(b, nb, Q_BLOCK, h, d), 1, 0)
    k_chunk = jnp.arange(s) // CHUNK

    def one(args):
        i, qblk = args
        q_chunk = (i * Q_BLOCK + jnp.arange(Q_BLOCK)) // CHUNK
        valid = k_chunk[None, :] <= q_chunk[:, None]
        return attend(qblk, k, v, None, valid, None)

    out = lax.map(one, (jnp.arange(nb), qb))
    return jnp.moveaxis(out, 0, 1).reshape(b, s, h, v.shape[-1])


def memory_kv(mem, g_mem, w_mem_kv, g_xk):
    b, n, _ = mem.shape
    kv = rmsnorm(mem, g_mem) @ w_mem_kv
    k = rmsnorm(kv[..., :MEM_WIDTH].reshape(b, n, MEM_HEADS, HEAD_DIM), g_xk)
    v = kv[..., MEM_WIDTH:].reshape(b, n, MEM_HEADS, HEAD_DIM)
    return k, v


def merge_and_project(x, mix_out, xq, gate, mem_k, mem_v, g_xq, w_out):
    b, s, _ = x.shape
    xq = rmsnorm(xq.reshape(b, s, MEM_HEADS, HEAD_DIM), g_xq)
    x_out = attend(xq, mem_k, mem_v, None, None, None)
    o = jnp.concatenate([mix_out.reshape(b, s, MIX_WIDTH), x_out.reshape(b, s, MEM_WIDTH)], axis=-1)
    return x + (o * jax.nn.silu(gate)) @ w_out


def layer_a(x, past_k, past_v, g_norm, w_in, g_q, g_k, sink, t5_table, mem_k, mem_v, g_xq, w_out):
    b, s, _ = x.shape
    q, k, v, xq, gate = split_cols(rmsnorm(x, g_norm) @ w_in, A_SPLITS)
    q = rmsnorm(q.reshape(b, s, MIX_HEADS, HEAD_DIM), g_q)
    k = rmsnorm(k.reshape(b, s, A_KV_HEADS, HEAD_DIM), g_k)
    v = v.reshape(b, s, A_KV_HEADS, HEAD_DIM)
    out, new_k, new_v = band_mix(q, k, v, past_k, past_v, A_PREV_CHUNKS,
                                 lambda rel: t5_rel_bias(t5_table, rel), sink)
    y = merge_and_project(x, out, xq, gate, mem_k, mem_v, g_xq, w_out)
    return y, new_k, new_v


def layer_b(x, pos, past_ckv, past_krope, g_norm, w_in, g_cq, w_q_b, g_ckv, w_kv_b, g_q, g_k,
            mem_k, mem_v, g_xq, w_out):
    b, s, _ = x.shape
    c_q, c_kv, k_rope, xq, gate = split_cols(rmsnorm(x, g_norm) @ w_in, B_SPLITS)
    cos, sin = rope_cos_sin(pos)
    q = (rmsnorm(c_q, g_cq) @ w_q_b).reshape(b, s, MIX_HEADS, B_NOPE + B_ROPE)
    q = jnp.concatenate([q[..., :B_NOPE], apply_rope(q[..., B_NOPE:], cos[:, None, :], sin[:, None, :])], axis=-1)
    q = rmsnorm(q, g_q)
    c_kv = rmsnorm(c_kv, g_ckv)
    k_rope = apply_rope(k_rope, cos, sin)
    if past_ckv is None:
        k, v = mla_keys(c_kv, k_rope, w_kv_b, g_k)
        out = mla_prompt_attend(q, k, v)
    else:
        k, v = mla_keys(jnp.concatenate([past_ckv, c_kv], axis=1),
                        jnp.concatenate([past_krope, k_rope], axis=1), w_kv_b, g_k)
        out = attend(q, k, v, None, None, None)
    y = merge_and_project(x, out, xq, gate, mem_k, mem_v, g_xq, w_out)
    return y, c_kv, k_rope


def layer_c(x, past_k, past_v, g_norm, w_in, g_q, g_k, rel_table, mem_k, mem_v, g_xq, w_out):
    b, s, _ = x.shape
    q, k, v, xq, gate = split_cols(rmsnorm(x, g_norm) @ w_in, C_SPLITS)
    q = rmsnorm(q.reshape(b, s, MIX_HEADS, HEAD_DIM), g_q)
    k = rmsnorm(k.reshape(b, s, MIX_HEADS, HEAD_DIM), g_k)
    v = v.reshape(b, s, MIX_HEADS, HEAD_DIM)
    out, new_k, new_v = band_mix(q, k, v, past_k, past_v, C_PREV_CHUNKS,
                                 lambda rel: clipped_rel_bias(rel_table, rel), None)
    y = merge_and_project(x, out, xq, gate, mem_k, mem_v, g_xq, w_out)
    return y, new_k, new_v


def setup_inputs(seed: int = 0) -> dict:
    key = jax.random.key(seed)
    keys = iter(jax.random.split(key, 40))

    def nrm(shape, scale):
        return jax.random.normal(next(keys), shape, jnp.float32) * scale

    def gain(shape):
        return 1.0 + nrm(shape, 0.02)

    a_rows = min(A_WINDOW, PAST_LEN)
    c_rows = min(C_PREV_CHUNKS * CHUNK, PAST_LEN)
    d_in = D_MODEL ** -0.5
    return {
        "x_prompt": nrm((BATCH, SEQ, D_MODEL), 1.0),
        "x_sample": nrm((DEC_BATCH, DEC_SEQ, D_MODEL), 1.0),
        "mem_prompt": nrm((BATCH, N_MEM, D_MODEL), 1.0),
        "cache_a_k": nrm((N_A, DEC_BATCH, a_rows, A_KV_HEADS, HEAD_DIM), 1.0),
        "cache_a_v": nrm((N_A, DEC_BATCH, a_rows, A_KV_HEADS, HEAD_DIM), 1.0),
        "cache_b_ckv": nrm((N_B, DEC_BATCH, PAST_LEN, B_KV_RANK), 1.0),
        "cache_b_krope": nrm((N_B, DEC_BATCH, PAST_LEN, B_ROPE), 1.0),
        "cache_c_k": nrm((N_C, DEC_BATCH, c_rows, MIX_HEADS, HEAD_DIM), 1.0),
        "cache_c_v": nrm((N_C, DEC_BATCH, c_rows, MIX_HEADS, HEAD_DIM), 1.0),
        "cache_mem_k": nrm((DEPTH, DEC_BATCH, N_MEM, MEM_HEADS, HEAD_DIM), 1.0),
        "cache_mem_v": nrm((DEPTH, DEC_BATCH, N_MEM, MEM_HEADS, HEAD_DIM), 1.0),
        "t5_bias": nrm((T5_BUCKETS, MIX_HEADS), 0.5),
        "norm_g": gain((DEPTH, D_MODEL)),
        "w_out": nrm((DEPTH, GATE_WIDTH, D_MODEL), GATE_WIDTH ** -0.5),
        "mem_norm_g": gain((DEPTH, D_MODEL)),
        "w_mem_kv": nrm((DEPTH, D_MODEL, 2 * MEM_WIDTH), d_in),
        "xq_norm_g": gain((DEPTH, HEAD_DIM)),
        "xk_norm_g": gain((DEPTH, HEAD_DIM)),
        "a_w_in": nrm((N_A, D_MODEL, sum(A_SPLITS)), d_in),
        "a_q_norm_g": gain((N_A, HEAD_DIM)),
        "a_k_norm_g": gain((N_A, HEAD_DIM)),
        "a_sink": nrm((N_A, MIX_HEADS), 0.5),
        "b_w_in": nrm((N_B, D_MODEL, sum(B_SPLITS)), d_in),
        "b_cq_norm_g": gain((N_B, B_Q_RANK)),
        "b_w_q_b": nrm((N_B, B_Q_RANK, MIX_HEADS * (B_NOPE + B_ROPE)), B_Q_RANK ** -0.5),
        "b_ckv_norm_g": gain((N_B, B_KV_RANK)),
        "b_w_kv_b": nrm((N_B, B_KV_RANK, MIX_HEADS * (B_NOPE + B_V)), B_KV_RANK ** -0.5),
        "b_q_norm_g": gain((N_B, B_NOPE + B_ROPE)),
        "b_k_norm_g": gain((N_B, B_NOPE + B_ROPE)),
        "c_w_in": nrm((N_C, D_MODEL, sum(C_SPLITS)), d_in),
        "c_q_norm_g": gain((N_C, HEAD_DIM)),
        "c_k_norm_g": gain((N_C, HEAD_DIM)),
        "c_rel_bias": nrm((N_C, MIX_HEADS, 2 * C_REL_CLIP + 1), 0.5),
    }


def reference(x_prompt, x_sample, mem_prompt, cache_a_k, cache_a_v, cache_b_ckv, cache_b_krope,
              cache_c_k, cache_c_v, cache_mem_k, cache_mem_v, t5_bias, norm_g, w_out, mem_norm_g,
              w_mem_kv, xq_norm_g, xk_norm_g, a_w_in, a_q_norm_g, a_k_norm_g, a_sink, b_w_in,
              b_cq_norm_g, b_w_q_b, b_ckv_norm_g, b_w_kv_b, b_q_norm_g, b_k_norm_g, c_w_in,
              c_q_norm_g, c_k_norm_g, c_rel_bias):
    past = cache_b_ckv.shape[2]
    pos_p = jnp.arange(x_prompt.shape[1])
    pos_s = past + jnp.arange(x_sample.shape[1])
    yp, ys = x_prompt, x_sample
    a_kp, a_vp, a_ks, a_vs = [], [], [], []
    b_cp, b_rp, b_cs, b_rs = [], [], [], []
    c_kp, c_vp, c_ks, c_vs = [], [], [], []
    m_k, m_v = [], []
    for i in range(DEPTH):
        kind, j = i % N_MIXERS, i // N_MIXERS
        mk, mv = memory_kv(mem_prompt, mem_norm_g[i], w_mem_kv[i], xk_norm_g[i])
        m_k.append(mk)
        m_v.append(mv)
        if kind == 0:
            yp, kn, vn = layer_a(yp, None, None, norm_g[i], a_w_in[j], a_q_norm_g[j], a_k_norm_g[j],
                                 a_sink[j], t5_bias, mk, mv, xq_norm_g[i], w_out[i])
            a_kp.append(kn)
            a_vp.append(vn)
            ys, kn, vn = layer_a(ys, cache_a_k[j], cache_a_v[j], norm_g[i], a_w_in[j], a_q_norm_g[j],
                                 a_k_norm_g[j], a_sink[j], t5_bias, cache_mem_k[i], cache_mem_v[i],
                                 xq_norm_g[i], w_out[i])
            a_ks.append(kn)
            a_vs.append(vn)
        elif kind == 1:
            yp, cn, rn = layer_b(yp, pos_p, None, None, norm_g[i], b_w_in[j], b_cq_norm_g[j], b_w_q_b[j],
                                 b_ckv_norm_g[j], b_w_kv_b[j], b_q_norm_g[j], b_k_norm_g[j],
                                 mk, mv, xq_norm_g[i], w_out[i])
            b_cp.append(cn)
            b_rp.append(rn)
            ys, cn, rn = layer_b(ys, pos_s, cache_b_ckv[j], cache_b_krope[j], norm_g[i], b_w_in[j],
                                 b_cq_norm_g[j], b_w_q_b[j], b_ckv_norm_g[j], b_w_kv_b[j],
                                 b_q_norm_g[j], b_k_norm_g[j], cache_mem_k[i], cache_mem_v[i],
                                 xq_norm_g[i], w_out[i])
            b_cs.append(cn)
            b_rs.append(rn)
        else:
            yp, kn, vn = layer_c(yp, None, None, norm_g[i], c_w_in[j], c_q_norm_g[j], c_k_norm_g[j],
                                 c_rel_bias[j], mk, mv, xq_norm_g[i], w_out[i])
            c_kp.append(kn)
            c_vp.append(vn)
            ys, kn, vn = layer_c(ys, cache_c_k[j], cache_c_v[j], norm_g[i], c_w_in[j], c_q_norm_g[j],
                                 c_k_norm_g[j], c_rel_bias[j], cache_mem_k[i], cache_mem_v[i],
                                 xq_norm_g[i], w_out[i])
            c_ks.append(kn)
            c_vs.append(vn)
    return (yp, ys,
            jnp.stack(a_kp), jnp.stack(a_vp), jnp.stack(a_ks), jnp.stack(a_vs),
            jnp.stack(b_cp), jnp.stack(b_rp), jnp.stack(b_cs), jnp.stack(b_rs),
            jnp.stack(c_kp), jnp.stack(c_vp), jnp.stack(c_ks), jnp.stack(c_vs),
            jnp.stack(m_k), jnp.stack(m_v))
```

```python
import numpy as np
import concourse.bass as bass
import concourse.mybir as mybir
from concourse.bass_types import AP
from concourse.bass_utils import run_bass_kernel_spmd

F32 = mybir.dt.float32
BF = mybir.dt.bfloat16
AF = mybir.ActivationFunctionType
ALU = mybir.AluOpType
AX = mybir.AxisListType

D = 2048
TT = 17
NCOL = TT * 128
EPS = 1e-6
NEG = -1e30


class Res:
    __slots__ = ("name", "w", "r", "dsem", "dcnt")

    def __init__(self, name):
        self.name = name
        self.w = []
        self.r = []
        self.dsem = None
        self.dcnt = 0


class Sched:
    ENG = ("pe", "act", "dve", "pool", "sp")

    def __init__(self, nc):
        self.nc = nc
        self.prog = {e: [] for e in self.ENG}
        self.esem = {e: nc.alloc_semaphore("es_" + e) for e in ("pe", "act", "dve", "pool")}
        self.cnt = {e: 0 for e in self.ENG}
        self.known = {e: {} for e in self.ENG}
        self.semobj = {"es_" + e: s for e, s in self.esem.items()}
        self.nd = 0
        self.nwaits = 0
        self.out_events = []

    def _deps(self, eng, reads, writes, adds):
        need = {}
        own = "es_" + eng
        kn = self.known[eng]

        def add(ev, raw):
            k, v, clk = ev
            if k == own and eng == "pe":
                return
            if kn.get(k, 0) >= v:
                return
            if need.get(k, (0, None))[0] < v:
                need[k] = (v, clk)

        for res in reads:
            for ev in res.w:
                add(ev, True)
        for res in writes:
            for ev in res.w:
                add(ev, False)
            for ev in res.r:
                add(ev, False)
        for res in adds:
            for ev in res.r:
                add(ev, False)
        waits = []
        for k, (v, clk) in sorted(need.items(), key=lambda kv: -len(kv[1][1])):
            if kn.get(k, 0) >= v:
                continue
            waits.append((self.semobj[k], v))
            for kk, vv in clk.items():
                if kn.get(kk, 0) < vv:
                    kn[kk] = vv
            if kn.get(k, 0) < v:
                kn[k] = v
        self.nwaits += len(waits)
        return waits

    def _mark(self, ev, reads, writes, adds):
        k = ev[0]
        for res in writes:
            res.w = [ev]
            res.r = []
        for res in adds:
            res.w = [e for e in res.w if e[0] != k]
            res.w.append(ev)
        for res in reads:
            res.r = [e for e in res.r if e[0] != k]
            res.r.append(ev)

    def op(self, eng, name, reads=(), writes=(), adds=(), **kw):
        waits = self._deps(eng, reads, writes, adds)
        self.cnt[eng] += 1
        n = self.cnt[eng]
        sem = self.esem[eng]

        def run(e, name=name, kw=kw, waits=waits, sem=sem):
            for s, v in waits:
                e.wait_ge(s, v)
            getattr(e, name)(**kw).then_inc(sem, 1)

        self.prog[eng].append(run)
        clk = dict(self.known[eng])
        clk["es_" + eng] = n
        ev = ("es_" + eng, n, clk)
        self._mark(ev, reads, writes, adds)
        return ev

    def dma(self, q, out, in_, sres, reads=(), writes=(), adds=(), is_output=False):
        waits = self._deps(q, reads, writes, adds)
        if sres.dsem is None:
            sres.dsem = {}
            sres.dcnt = {}
        if q not in sres.dsem:
            self.nd += 1
            key = "ds%d" % self.nd
            sres.dsem[q] = key
            sres.dcnt[q] = 0
            self.semobj[key] = self.nc.alloc_semaphore(key)
        sres.dcnt[q] += 16
        dkey = sres.dsem[q]
        dval = sres.dcnt[q]
        sem = self.semobj[dkey]

        def run(e, waits=waits, sem=sem, out=out, in_=in_):
            for s, v in waits:
                e.wait_ge(s, v)
            e.dma_start(out=out, in_=in_).then_inc(sem, 16)

        self.prog[q].append(run)
        ev = (dkey, dval, dict(self.known[q]))
        self._mark(ev, reads, writes, adds)
        self.out_events.append(ev)
        return ev

    def emit(self):
        nc = self.nc
        last = {}
        for k, v, clk in self.out_events:
            last[k] = max(last.get(k, 0), v)
        for e in ("pe", "act", "dve", "pool"):
            if self.cnt[e]:
                last["es_" + e] = self.cnt[e]
        fwaits = [(self.semobj[k], v) for k, v in last.items()]
        prog = self.prog
        with nc.Block() as block:
            @block.tensor
            def _(e):
                for f in prog["pe"]:
                    f(e)

            @block.scalar
            def _(e):
                for f in prog["act"]:
                    f(e)

            @block.vector
            def _(e):
                for f in prog["dve"]:
                    f(e)

            @block.gpsimd
            def _(e):
                for f in prog["pool"]:
                    f(e)

            @block.sync
            def _(e):
                for f in prog["sp"]:
                    f(e)
                for s, v in fwaits:
                    e.wait_ge(s, v)


def rows_of(tt):
    return 128 if tt < 16 else 16


IN_SPECS = [
    ("x_prompt", [2048, 2048]), ("x_sample", [16, 2048]), ("mem_prompt", [256, 2048]),
    ("cache_a_k", [2, 128, 512]), ("cache_a_v", [2, 128, 512]),
    ("cache_b_ckv", [4096, 256]), ("cache_b_krope", [4096, 64]),
    ("cache_c_k", [512, 1536]), ("cache_c_v", [512, 1536]),
    ("cache_mem_k", [4, 256, 512]), ("cache_mem_v", [4, 256, 512]),
    ("t5_bias", [32, 12]), ("norm_g", [4, 2048]), ("w_out", [4, 2048, 2048]),
    ("mem_norm_g", [4, 2048]), ("w_mem_kv", [4, 2048, 1024]),
    ("xq_norm_g", [4, 128]), ("xk_norm_g", [4, 128]),
    ("a_w_in", [2, 2048, 5120]), ("a_q_norm_g", [2, 128]), ("a_k_norm_g", [2, 128]),
    ("a_sink", [2, 12]),
    ("b_w_in", [1, 2048, 3392]), ("b_cq_norm_g", [1, 512]), ("b_w_q_b", [1, 512, 2304]),
    ("b_ckv_norm_g", [1, 256]), ("b_w_kv_b", [1, 256, 3072]),
    ("b_q_norm_g", [1, 192]), ("b_k_norm_g", [1, 192]),
    ("c_w_in", [1, 2048, 7168]), ("c_q_norm_g", [1, 128]), ("c_k_norm_g", [1, 128]),
    ("c_rel_bias", [1, 12, 257]),
    ("k_ident", [128, 128]), ("k_t5oh", [32, 384]), ("k_cos", [128, 17, 32]), ("k_sin", [128, 17, 32]),
]
OUT_SPECS = [
    ("y_prompt", [2048, 2048]), ("y_sample", [16, 2048]),
    ("a_k_prompt", [2, 128, 512]), ("a_v_prompt", [2, 128, 512]),
    ("a_k_sample", [2, 16, 512]), ("a_v_sample", [2, 16, 512]),
    ("b_ckv_prompt", [2048, 256]), ("b_krope_prompt", [2048, 64]),
    ("b_ckv_sample", [16, 256]), ("b_krope_sample", [16, 64]),
    ("c_k_prompt", [512, 1536]), ("c_v_prompt", [512, 1536]),
    ("c_k_sample", [16, 1536]), ("c_v_sample", [16, 1536]),
    ("mem_k_prompt", [4, 256, 512]), ("mem_v_prompt", [4, 256, 512]),
]


class Prog:
    def __init__(self, n_layers=4, stop=None):
        self.n_layers = n_layers
        self.stop = stop
        self.stopped = False
        nc = self.nc = bass.Bass("TRN2", target_bir_lowering=False)
        self.S = Sched(nc)
        self.din = {n: nc.dram_tensor(n, s, F32, kind="ExternalInput") for n, s in IN_SPECS}
        self.dout = {n: nc.dram_tensor(n, s, F32, kind="ExternalOutput") for n, s in OUT_SPECS}
        self.dres = {}
        self._rr = {}
        self.deferred = []
        self.lag = 1
        self.alloc()
        self.prologue()
        for L in range(n_layers):
            if not self.stopped:
                self.layer(L)
        self.S.emit()

    def chk(self, L, ph):
        if self.stop is not None and self.stop == (L, ph):
            self.stopped = True
        return self.stopped

    def rres(self, name):
        if name not in self.dres:
            self.dres[name] = Res(name)
        return self.dres[name]

    def rot(self, key, n):
        i = self._rr.get(key, 0)
        self._rr[key] = i + 1
        return i % n

    def sb(self, name, shape, dt):
        t = self.nc.alloc_sbuf_tensor(name, shape, dt)
        return t, self.rres("sb_" + name)

    def scr(self, name, shape, dt=BF):
        t = self.nc.dram_tensor(name, shape, dt)
        return t, self.rres("dr_" + name)

    def op(self, eng, name, reads=(), writes=(), adds=(), **kw):
        ex = [r for r in reads if r in self.ps_set]
        if ex:
            reads = [r for r in reads if r not in self.ps_set]
            writes = list(writes) + ex
        return self.S.op(eng, name, reads, writes, adds, **kw)

    def dma(self, out, in_, sres, reads=(), writes=(), adds=(), q="sp", is_output=False):
        return self.S.dma(q, out, in_, sres, reads, writes, adds, is_output)

    def newstat(self):
        i = self.rot("stat", 12)
        return self.stat[i], self.stat_r[i]

    def bank(self, b):
        return self.ps[:, b * 512:(b + 1) * 512]

    BANKS = {"acc": [0, 1, 4, 5, 6, 7], "tp": [2, 3], "s": [0, 1, 4, 5], "o": [6, 7]}

    def nb(self, kind):
        lst = self.BANKS[kind]
        b = lst[self.rot("bank_" + kind, len(lst))]
        return self.bank(b), self.ps_r[b]

    def defer(self, fn, delay=1):
        self._seq = getattr(self, "_seq", 0) + 1
        self.deferred.append([delay, self._seq, fn])

    def step(self):
        due = []
        rest = []
        for it in self.deferred:
            it[0] -= 1
            (due if it[0] <= 0 else rest).append(it)
        self.deferred = rest
        for it in sorted(due, key=lambda t: t[1]):
            it[2]()

    def flush(self):
        while self.deferred:
            self.step()

    def alloc(self):
        nc = self.nc
        self.ps = nc.alloc_psum_tensor("ps", [128, 4096], F32)
        self.ps_r = [Res("bank%d" % i) for i in range(8)]
        self.ps_set = set(self.ps_r)
        self.actT, _ = self.sb("actT", [128, 16, NCOL], BF)
        self.act_r = [Res("act%d" % t) for t in range(TT)]
        self.arena = []
        self.arena_r = []
        for i in range(2):
            t, r = self.sb("arena%d" % i, [128, 6528], F32)
            self.arena.append(t)
            self.arena_r.append(r)
        self.xta, self.xta_r = self.sb("xta", [128, 2 * 2176], F32)
        self.xt_r = [Res("xt0"), Res("xt1")]
        self.cq_r = self.xta_r
        self.Rt, self.R_r = self.sb("Rt", [128, 2176], F32)
        self.Rh_r = [Res("Rh0"), Res("Rh1")]
        self.ckvT, self.ckvT_r = self.sb("ckvT", [128, 2, NCOL], BF)
        self.kr_all, self.kr_all_r = self.sb("kr_all", [128, TT, 64], F32)
        self.krss, self.krss_r = self.sb("krss", [128, TT], F32)
        self.krss_c, self.krss_c_r = self.sb("krss_c", [128, 40], F32)
        self.gbm_r = Res("gbm")
        self.zf = []
        self.zf_r = []
        self.of = []
        self.of_r = []
        self.tb = []
        self.tb_r = []
        self.vb = []
        self.vb_r = []
        self.gb = []
        self.gb_r = []
        self.ld = []
        self.ld_r = []
        self.pexp = []
        self.pexp_r = []
        self.ogf = []
        self.ogf_r = []
        self.ckvt = []
        self.ckvt_r = []
        self.krt = []
        self.krt_r = []
        for i in range(2):
            for lst, rl, nm, shp, dt in (
                (self.zf, self.zf_r, "zf", [128, 512], F32), (self.of, self.of_r, "of", [128, 512], F32),
                (self.tb, self.tb_r, "tb", [128, 4, 128], BF), (self.vb, self.vb_r, "vb", [128, 4, 129], BF),
                (self.gb, self.gb_r, "gb", [128, 512], BF), (self.ld, self.ld_r, "ld", [128, 512], F32),
                (self.pexp, self.pexp_r, "pexp", [128, 512], BF), (self.pexp, self.pexp_r, "pexq", [128, 512], BF),
                (self.ogf, self.ogf_r, "ogf", [128, 128], F32), (self.ckvt, self.ckvt_r, "ckvt", [128, 2, 128], BF),
                (self.krt, self.krt_r, "krt", [128, 64], F32),
            ):
                t, r = self.sb("%s%d" % (nm, i), shp, dt)
                lst.append(t)
                rl.append(r)
        for lst, rl, nm, shp, dt in ((self.tb, self.tb_r, "tb", [128, 4, 128], BF), (self.vb, self.vb_r, "vb", [128, 4, 129], BF)):
            t, r = self.sb("%s2" % nm, shp, dt)
            lst.append(t)
            rl.append(r)
        t, r = self.sb("krt2", [128, 64], F32)
        self.krt.append(t)
        self.krt_r.append(r)
        self.sq, self.sq_r = self.sb("sq", [128, 512], F32)
        self.rp, self.rp_r = self.sb("rp", [128, 6, 64], F32)
        self.mkT, self.mkT_r = self.sb("mkT", [128, 4, 256], BF)
        self.mv, self.mv_r = self.sb("mv", [128, 2, 4, 129], BF)
        self.mkTs, self.mkTs_r = self.sb("mkTs", [128, 4, 256], BF)
        self.mvs, self.mvs_r = self.sb("mvs", [128, 2, 4, 129], BF)
        self.kcA, self.kcA_r = self.sb("kcA", [128, 4, 128], BF)
        self.vcA, self.vcA_r = self.sb("vcA", [128, 4, 129], BF)
        self.kcC, self.kcC_r = self.sb("kcC", [128, 512], BF)
        self.vcC, self.vcC_r = self.sb("vcC", [128, 4, 129], BF)
        self.gcolA, self.gcolA_r = self.sb("gcolA", [128, 128], F32)
        self.gcolB, self.gcolB_r = self.sb("gcolB", [128, 32], F32)
        self.gbc, self.gbc_r = self.sb("gbc", [128, 384], F32)
        self.esink, self.esink_r = self.sb("esink", [128, 24], F32)
        self.ident, self.ident_r = self.sb("ident", [128, 128], F32)
        self.cosT, self.cos_r = self.sb("cosT", [128, TT, 32], F32)
        self.sinT, self.sin_r = self.sb("sinT", [128, TT, 32], F32)
        self.cm05, self.cm05_r = self.sb("cm05", [128, 4], F32)
        self.stat = []
        self.stat_r = []
        for i in range(12):
            t, r = self.sb("stat%d" % i, [128, 16], F32)
            self.stat.append(t)
            self.stat_r.append(r)
        self.xres = [self.scr("xres%d" % i, [2064, 2048], F32) for i in range(2)]
        self.qT_s = self.scr("qT_s", [12, 128, NCOL])
        self.q192_s = self.scr("q192_s", [12, 192, NCOL])
        self.k192_s = self.scr("k192_s", [12, 192, NCOL])
        self.k192s_s = self.scr("k192s_s", [12, 192, 4224])
        self.xqT_s = self.scr("xqT_s", [4, 128, NCOL])
        self.kT_s = self.scr("kT_s", [12, 128, NCOL])
        self.v_s = self.scr("v_s", [12, TT, 128, 129])
        self.gate_s = self.scr("gate_s", [NCOL, 2048])
        self.vs_s = self.scr("vs_s", [12, 33, 128, 129])
        self.FrepA = self.scr("FrepA", [12, 128, 384], F32)
        self.FrepC = self.scr("FrepC", [12, 128, 768], F32)

    def act_ap(self, tt, k):
        return self.actT[:, k, tt * 128: tt * 128 + rows_of(tt)]

    def xt(self, i):
        return self.xta[:, i * 2176: i * 2176 + 2048]

    def cqT(self):
        return self.xta[:, :].bitcast(BF).rearrange("p (k c) -> p k c", k=4)

    def memT(self):
        return self.Rt[:, 0:2048].bitcast(BF).rearrange("p (k c) -> p k c", k=16)

    def Rbf(self):
        return self.Rt[:, :].bitcast(BF)

    def arena_bf(self, i):
        return self.arena[i][:, :].bitcast(BF)

    def wview(self, i, nk, w):
        return self.arena_bf(i)[:, 0:nk * w].rearrange("p (k w) -> p k w", k=nk)

    def prologue(self):
        din = self.din
        self.dma(self.ident[:], din["k_ident"].ap(), self.ident_r, writes=[self.ident_r])
        self.dma(self.cosT[:], din["k_cos"].ap(), self.cos_r, writes=[self.cos_r])
        self.dma(self.sinT[:], din["k_sin"].ap(), self.sin_r, writes=[self.sin_r])
        self.op("pool", "memset", writes=[self.cm05_r], ap=self.cm05[:], constant=-0.5)
        for i in range(len(self.vb)):
            self.op("pool", "memset", writes=[self.vb_r[i]], ap=self.vb[i][:], constant=1.0)
        for t, r in ((self.mv, self.mv_r), (self.mvs, self.mvs_r)):
            self.op("pool", "memset", writes=[r], ap=t[:], constant=1.0)
        for t, r in ((self.vcA, self.vcA_r), (self.vcC, self.vcC_r)):
            self.op("pool", "memset", writes=[r], ap=t[:], constant=1.0)
        ga = self.ld[0]
        gar = self.ld_r[0]
        self.dma(ga[0:64, 0:128], din["norm_g"].ap().rearrange("l (k c) -> (l k) c", c=128), gar, adds=[gar])
        self.dma(ga[64:128, 0:128], din["mem_norm_g"].ap().rearrange("l (k c) -> (l k) c", c=128), gar, adds=[gar])
        gb_ = self.ld[1]
        gbr = self.ld_r[1]
        self.op("dve", "memset", writes=[gbr, self.gbm_r], ap=gb_[0:32, 0:128], constant=0.0)
        rows = [("xq_norm_g", 0, 4, None), ("xk_norm_g", 4, 4, None), ("a_q_norm_g", 8, 2, None),
                ("a_k_norm_g", 10, 2, None), ("c_q_norm_g", 12, 1, None), ("c_k_norm_g", 13, 1, None)]
        for nm, r0, n, _ in rows:
            self.dma(gb_[r0:r0 + n, 0:128], din[nm].ap(), gbr, reads=[self.gbm_r], adds=[gbr])
        self.dma(gb_[14:18, 0:128], din["b_cq_norm_g"].ap().rearrange("o (k c) -> (o k) c", c=128), gbr, reads=[self.gbm_r], adds=[gbr])
        self.dma(gb_[18:20, 0:128], din["b_ckv_norm_g"].ap().rearrange("o (k c) -> (o k) c", c=128), gbr, reads=[self.gbm_r], adds=[gbr])
        self.dma(gb_[20:21, 0:128], din["b_q_norm_g"].ap()[:, 0:128], gbr, reads=[self.gbm_r], adds=[gbr])
        self.dma(gb_[21:22, 0:64], din["b_q_norm_g"].ap()[:, 128:192], gbr, reads=[self.gbm_r], adds=[gbr])
        self.dma(gb_[22:23, 0:128], din["b_k_norm_g"].ap()[:, 0:128], gbr, reads=[self.gbm_r], adds=[gbr])
        self.dma(gb_[23:24, 0:64], din["b_k_norm_g"].ap()[:, 128:192], gbr, reads=[self.gbm_r], adds=[gbr])
        tpb, tpr = self.nb("tp")
        self.op("pe", "transpose", reads=[gar, self.ident_r], writes=[tpr], out=tpb[:, 0:128], in_=ga[:, 0:128],
                identity=self.ident[:])
        self.op("act", "activation", reads=[tpr], writes=[self.gcolA_r], out=self.gcolA[:], in_=tpb[:, 0:128], func=AF.Copy)
        tpb, tpr = self.nb("tp")
        self.op("pe", "transpose", reads=[gbr, self.ident_r], writes=[tpr], out=tpb[:, 0:32], in_=gb_[0:32, 0:128],
                identity=self.ident[0:32, 0:32])
        self.op("act", "activation", reads=[tpr], writes=[self.gcolB_r], out=self.gcolB[:], in_=tpb[:, 0:32], func=AF.Copy)
        self.dma(self.esink[:], din["a_sink"].ap().rearrange("a h -> (a h)").partition_broadcast(128), self.esink_r,
                 writes=[self.esink_r])
        self.op("act", "activation", reads=[self.esink_r], writes=[self.esink_r], out=self.esink[:], in_=self.esink[:],
                func=AF.Exp)
        t5 = self.zf[0]
        t5r = self.zf_r[0]
        oh = self.of[0]
        ohr = self.of_r[0]
        self.dma(t5[0:32, 0:12], din["t5_bias"].ap(), t5r, writes=[t5r])
        self.dma(oh[0:32, 0:384], din["k_t5oh"].ap(), ohr, writes=[ohr])
        ab, ar = self.nb("acc")
        self.op("pe", "matmul", reads=[t5r, ohr], writes=[ar], out=ab[0:12, 0:384], lhsT=t5[0:32, 0:12],
                rhs=oh[0:32, 0:384], start=True, stop=True)
        fa = self.zf[1]
        far = self.zf_r[1]
        self.op("dve", "tensor_copy", reads=[ar], writes=[far], out=fa[0:12, 0:384], in_=ab[0:12, 0:384])
        FA, FAr = self.FrepA
        self.dma(FA.ap(), fa[0:12, 0:384].unsqueeze(1).broadcast_to([12, 128, 384]), far, reads=[far], writes=[FAr])
        fc = self.xta
        fcr = self.xt_r[0]
        self.dma(fc[0:12, 0:257], din["c_rel_bias"].ap()[0], fcr, writes=[fcr])
        self.op("dve", "tensor_copy", reads=[fcr], writes=[fcr], out=fc[0:12, 257:768],
                in_=fc[0:12, 256:257].broadcast_to([12, 511]))
        FC, FCr = self.FrepC
        self.dma(FC.ap(), fc[0:12, 0:768].unsqueeze(1).broadcast_to([12, 128, 768]), fcr, reads=[fcr], writes=[FCr])

    def norm_transpose(self, srcs, gcol0, dst_fn, dres_fn):
        info = {}

        def stage_a1(idx):
            src, rows, key = srcs[idx]
            i = self.rot("xt", 2)
            xt = self.xt(i)
            xr = self.xt_r[i]
            self.dma(xt[0:rows, :], src, xr, writes=[xr, self.xta_r])
            st, sr = self.newstat()
            for c in range(4):
                self.op("act", "activation", reads=[xr], writes=[self.ps_r[4 + c], sr], out=self.bank(4 + c)[0:rows, :],
                        in_=xt[0:rows, c * 512:(c + 1) * 512], func=AF.Square, accum_out=st[0:rows, 4 + c:5 + c])
            info[idx] = (xt, xr, st, sr)

        def stage_a2(idx):
            src, rows, key = srcs[idx]
            xt, xr, st, sr = info[idx]
            self.op("dve", "tensor_reduce", reads=[sr], writes=[sr], out=st[0:rows, 0:1], in_=st[0:rows, 4:8], axis=AX.X,
                    op=ALU.add)
            self.op("dve", "tensor_scalar", reads=[sr], writes=[sr], out=st[0:rows, 1:2], in0=st[0:rows, 0:1],
                    scalar1=1.0 / D, scalar2=EPS, op0=ALU.mult, op1=ALU.add)
            self.op("pool", "tensor_tensor", reads=[sr, self.cm05_r], writes=[sr], out=st[0:rows, 2:3],
                    in0=st[0:rows, 1:2], in1=self.cm05[0:rows, 0:1], op=ALU.pow)

        def stage_b(idx):
            src, rows, key = srcs[idx]
            xt, xr, st, sr = info.pop(idx)
            self.op("dve", "tensor_scalar", reads=[sr, xr], writes=[xr], out=xt[0:rows, :], in0=xt[0:rows, :],
                    scalar1=st[0:rows, 2:3], scalar2=None, op0=ALU.mult)
            for j in range(4):
                tpb, tpr = self.nb("tp")
                tpv = tpb.rearrange("p (a b) -> p a b", a=4)
                for c in range(4):
                    k = 4 * j + c
                    self.op("pe", "transpose", reads=[xr, self.ident_r, self.xta_r], writes=[tpr], out=tpv[:, c, 0:rows],
                            in_=xt[0:rows, k * 128:(k + 1) * 128], identity=self.ident[0:rows, 0:rows])
                if j == 3:
                    for c in range(4):
                        k = 4 * j + c
                        self.op("act", "activation", reads=[tpr, self.gcolA_r], writes=[dres_fn(key)],
                                out=dst_fn(key, j)[:, c, :], in_=tpv[:, c, 0:rows], func=AF.Copy,
                                scale=self.gcolA[:, gcol0 + k: gcol0 + k + 1])
                else:
                    g = self.gcolA[:, gcol0 + 4 * j: gcol0 + 4 * j + 4].unsqueeze(2).broadcast_to([128, 4, rows])
                    self.op("dve", "tensor_tensor", reads=[tpr, self.gcolA_r], writes=[dres_fn(key)], out=dst_fn(key, j),
                            in0=tpv[:, 0:4, 0:rows], in1=g, op=ALU.mult)

        n = len(srcs)
        stage_a1(0)
        stage_a2(0)
        for idx in range(n):
            if idx + 1 < n:
                stage_a1(idx + 1)
            stage_b(idx)
            if idx + 1 < n:
                stage_a2(idx + 1)

    def proj(self, W, wcols, nk, blocks, tiles, act_fn, epilogue, tile_outer=False):
        wh, woff = W
        if tile_outer:
            ai = self.rot("arena", 2)
            ar = self.arena_r[ai]
            tot = sum(w for _, w in blocks)
            wv = self.arena_bf(ai)[:, 0:nk * tot].rearrange("p (k w) -> p k w", k=nk)
            pos = 0
            wpos = []
            for (c0, w) in blocks:
                src = AP(wh, woff + c0, [[wcols, 128], [128 * wcols, nk], [1, w]])
                self.dma(wv[:, :, pos:pos + w], src, ar, adds=[ar], q="pool")
                wpos.append(pos)
                pos += w
            for (key, rows, rd) in tiles:
                for bi, (c0, w) in enumerate(blocks):
                    acc, accr = self.nb("acc")
                    for k in range(nk):
                        self.op("pe", "matmul", reads=[ar] + rd, writes=[accr], out=acc[0:rows, 0:w],
                                lhsT=act_fn(key, k), rhs=wv[:, k, wpos[bi]:wpos[bi] + w], start=(k == 0), stop=(k == nk - 1))
                    self.step()
                    epilogue(key, rows, bi, acc[0:rows, 0:w], accr)
            self.flush()
            return
        def issue_w(bi):
            c0, w = blocks[bi]
            ai = self.rot("arena", 2)
            ar = self.arena_r[ai]
            wv = self.wview(ai, nk, w)
            src = AP(wh, woff + c0, [[wcols, 128], [128 * wcols, nk], [1, w]])
            self.dma(wv, src, ar, writes=[ar], q="pool")
            return ar, wv

        cur = issue_w(0)
        for bi, (c0, w) in enumerate(blocks):
            nxt = issue_w(bi + 1) if bi + 1 < len(blocks) else None
            ar, wv = cur
            cur = nxt
            for (key, rows, rd) in tiles:
                acc, accr = self.nb("acc")
                for k in range(nk):
                    self.op("pe", "matmul", reads=[ar] + rd, writes=[accr], out=acc[0:rows, 0:w],
                            lhsT=act_fn(key, k), rhs=wv[:, k, 0:w], start=(k == 0), stop=(k == nk - 1))
                self.step()
                epilogue(key, rows, bi, acc[0:rows, 0:w], accr)
        self.flush()

    def headnorm(self, acc, accr, rows, nh, hd, extra_ss=None):
        w = nh * hd
        self.op("act", "activation", reads=[accr], writes=[self.sq_r], out=self.sq[0:rows, 0:w], in_=acc, func=AF.Square)
        st, sr = self.newstat()
        self.op("dve", "tensor_reduce", reads=[self.sq_r], writes=[sr], out=st[0:rows, 0:nh],
                in_=self.sq[0:rows, 0:w].rearrange("p (a b) -> p a b", a=nh), axis=AX.X, op=ALU.add)
        self.op("dve", "tensor_scalar", reads=[sr], writes=[sr], out=st[0:rows, 4:4 + nh], in0=st[0:rows, 0:nh],
                scalar1=1.0 / hd, scalar2=EPS, op0=ALU.mult, op1=ALU.add)
        self.op("pool", "tensor_tensor", reads=[sr, self.cm05_r], writes=[sr], out=st[0:rows, 8:8 + nh],
                in0=st[0:rows, 4:4 + nh], in1=self.cm05[0:rows, 0:nh], op=ALU.pow)
        i = self.rot("zf", 2)
        zf = self.zf[i]
        zr = self.zf_r[i]
        zv = zf[0:rows, 0:w].rearrange("p (a b) -> p a b", a=nh)
        self.defer(lambda: self.op("dve", "tensor_tensor", reads=[accr, sr], writes=[zr], out=zv,
                                   in0=acc.rearrange("p (a b) -> p a b", a=nh),
                                   in1=st[0:rows, 8:8 + nh].unsqueeze(2).broadcast_to([rows, nh, hd]), op=ALU.mult), 1)
        return zv, zr

    def transpose_out(self, zv, zr, rows, nh, gcol, dst_sb=None, dst_res=None, dst_dram=None, dram_res=None):
        self.defer(lambda: self._transpose_out(zv, zr, rows, nh, gcol, dst_sb, dst_res, dst_dram, dram_res), 3)

    def _transpose_out(self, zv, zr, rows, nh, gcol, dst_sb, dst_res, dst_dram, dram_res):
        tpb, tpr = self.nb("tp")
        tpv = tpb.rearrange("p (a b) -> p a b", a=4)
        for c in range(nh):
            self.op("pe", "transpose", reads=[zr, self.ident_r], writes=[tpr], out=tpv[:, c, 0:rows], in_=zv[:, c, :],
                    identity=self.ident[0:rows, 0:rows])
        if dst_sb is not None:
            self.op("act", "activation", reads=[tpr, self.gcolB_r], writes=[dst_res], out=dst_sb, in_=tpv[:, 0:nh, 0:rows],
                    func=AF.Copy, scale=gcol)
            return
        i = self.rot("tb", 3)
        tb = self.tb[i]
        tr = self.tb_r[i]
        self.op("act", "activation", reads=[tpr, self.gcolB_r], writes=[tr], out=tb[:, 0:nh, 0:rows],
                in_=tpv[:, 0:nh, 0:rows], func=AF.Copy, scale=gcol)
        self.dma(dst_dram, tb[:, 0:nh, 0:rows], tr, reads=[tr], adds=[dram_res])

    def rows_out(self, src, src_res, rows, w, dst):
        i = self.rot("of", 2)
        of = self.of[i]
        orr = self.of_r[i]
        self.op("dve", "tensor_copy", reads=[src_res], writes=[orr], out=of[0:rows, 0:w], in_=src)
        self.dma(dst, of[0:rows, 0:w], orr, reads=[orr], is_output=True)

    def gain_rows_out(self, zv, zr, rows, nh, g_ap, dst):
        self.defer(lambda: self._gain_rows_out(zv, zr, rows, nh, g_ap, dst), 1)

    def _gain_rows_out(self, zv, zr, rows, nh, g_ap, dst):
        i = self.rot("of", 2)
        of = self.of[i]
        orr = self.of_r[i]
        self.op("dve", "tensor_tensor", reads=[zr, self.gbc_r], writes=[orr],
                out=of[0:rows, 0:nh * 128].rearrange("p (a b) -> p a b", a=nh), in0=zv,
                in1=g_ap.unsqueeze(1).broadcast_to([rows, nh, 128]), op=ALU.mult)
        self.dma(dst, of[0:rows, 0:nh * 128], orr, reads=[orr], is_output=True)

    def v_out(self, accv, accr, rows, nh, dst_dram, dram_res, dst_sb=None, dst_res=None):
        if dst_sb is not None:
            self.op("act", "activation", reads=[accr], writes=[dst_res], out=dst_sb, in_=accv, func=AF.Copy)
            return
        i = self.rot("vb", 3)
        vb = self.vb[i]
        vr = self.vb_r[i]
        self.op("act", "activation", reads=[accr], writes=[vr], out=vb[0:rows, 0:nh, 0:128], in_=accv, func=AF.Copy)
        self.dma(dst_dram, vb[0:rows, 0:nh, :], vr, reads=[vr], adds=[dram_res])

    def gate_out(self, acc, accr, rows, w, tt, gc0):
        i = self.rot("gb", 2)
        gb = self.gb[i]
        gr = self.gb_r[i]
        self.op("act", "activation", reads=[accr], writes=[gr], out=gb[0:rows, 0:w], in_=acc, func=AF.Silu)
        G, Gr = self.gate_s
        self.dma(AP(G, tt * 128 * 2048 + gc0, [[2048, rows], [1, w]]), gb[0:rows, 0:w], gr, reads=[gr], adds=[Gr])

    def qk_store(self, scr, h0, nh, tt, rows, width=NCOL, dpart=128, hrows=None, r0=0):
        t, _ = scr
        hrows = hrows or dpart
        return AP(t, (h0 * hrows + r0) * width + tt * 128, [[width, dpart], [hrows * width, nh], [1, rows]])

    def v_store(self, scr, h0, nh, tile, rows, ntiles=TT):
        t, _ = scr
        return AP(t, (h0 * ntiles + tile) * 128 * 129, [[129, rows], [ntiles * 128 * 129, nh], [1, 129]])

    def attn(self, kts, nq, scale, sink_ap, gate_ap, gate_reads, dst_ap, dst_res):
        chunks = []
        cur = []
        for kt in kts:
            if cur:
                p = cur[-1]
                brk = (len(cur) * nq >= 512 or kt["nk"] != p["nk"] or (kt["bias"] is None) != (p["bias"] is None)
                       or (kt["bias"] is not None and kt["bias"][1] != p["bias"][1] + 1))
                if brk:
                    chunks.append(cur)
                    cur = []
            cur.append(kt)
        chunks.append(cur)
        assert len(chunks) <= 4
        ob, orr = self.nb("o")
        staged = []
        for ci, ch in enumerate(chunks):
            sbk, sr = self.nb("s")
            nk = ch[0]["nk"]
            n = len(ch)
            for i, kt in enumerate(ch):
                m = len(kt["qk"])
                for pi, (l, r) in enumerate(kt["qk"]):
                    self.op("pe", "matmul", reads=kt["reads"], writes=[sr], out=sbk[0:nk, i * nq:(i + 1) * nq], lhsT=l, rhs=r,
                            start=(pi == 0), stop=(pi == m - 1))
            if ci == 0:
                self.step()
            ip = self.rot("pexp", 4)
            pe_t = self.pexp[ip]
            per = self.pexp_r[ip]
            if ch[0]["bias"] is not None:
                j = self.rot("zf", 2)
                tm = self.zf[j]
                tmr = self.zf_r[j]
                self.op("act", "activation", reads=[sr], writes=[tmr], out=tm[0:nk, 0:n * nq], in_=sbk[0:nk, 0:n * nq],
                        func=AF.Exp, scale=scale)
                bv, s0 = ch[0]["bias"]
                self.op("dve", "tensor_tensor", reads=[tmr] + ch[0]["reads"], writes=[per],
                        out=pe_t[0:nk, 0:n * nq].rearrange("p (a b) -> p a b", a=n),
                        in0=tm[0:nk, 0:n * nq].rearrange("p (a b) -> p a b", a=n), in1=bv[0:nk, s0:s0 + n, 0:nq], op=ALU.mult)
            else:
                self.op("act", "activation", reads=[sr], writes=[per], out=pe_t[0:nk, 0:n * nq], in_=sbk[0:nk, 0:n * nq],
                        func=AF.Exp, scale=scale)
            for i, kt in enumerate(ch):
                if kt.get("diag"):
                    self.op("pool", "memset", writes=[per], ap=pe_t[64:128, i * nq:i * nq + 64], constant=0.0)
            staged.append((ch, pe_t, per, nk))
        tot = len(kts)

        def part2():
            idx = 0
            for ch, pe_t, per, nk in staged:
                for i, kt in enumerate(ch):
                    self.op("pe", "matmul", reads=[per] + kt["reads"], writes=[orr], out=ob[0:nq, 0:129],
                            lhsT=pe_t[0:nk, i * nq:(i + 1) * nq], rhs=kt["v"], start=(idx == 0), stop=(idx == tot - 1))
                    idx += 1
            self.attn_post(ob, orr, nq, sink_ap, gate_ap, gate_reads, dst_ap, dst_res, tail_delay=(1 if dfr else 2))

        dfr = len(chunks) <= 2
        if dfr:
            self.defer(part2, 1)
        else:
            part2()

    def attn_post(self, ob, orr, nq, sink_ap, gate_ap, gate_reads, dst_ap, dst_res, tail_delay=2):
        st, sr = self.newstat()
        if sink_ap is not None:
            self.op("dve", "tensor_tensor", reads=[orr, self.esink_r], writes=[sr], out=st[0:nq, 0:1],
                    in0=ob[0:nq, 128:129], in1=sink_ap, op=ALU.add)
            self.op("dve", "reciprocal", reads=[sr], writes=[sr], out=st[0:nq, 1:2], in_=st[0:nq, 0:1])
        else:
            self.op("dve", "reciprocal", reads=[orr], writes=[sr], out=st[0:nq, 1:2], in_=ob[0:nq, 128:129])
        i = self.rot("ogf", 2)
        og = self.ogf[i]
        ogr = self.ogf_r[i]
        self.op("dve", "scalar_tensor_tensor", reads=[orr, sr] + gate_reads, writes=[ogr], out=og[0:nq, :],
                in0=ob[0:nq, 0:128], scalar=st[0:nq, 1:2], in1=gate_ap, op0=ALU.mult, op1=ALU.mult)

        def tail():
            tpb, tpr = self.nb("tp")
            self.op("pe", "transpose", reads=[ogr, self.ident_r], writes=[tpr], out=tpb[:, 0:nq], in_=og[0:nq, :],
                    identity=self.ident[0:nq, 0:nq])
            self.op("act", "activation", reads=[tpr], writes=[dst_res], out=dst_ap, in_=tpb[:, 0:nq], func=AF.Copy)
        self.defer(tail, tail_delay)

    A_Q, A_G, A_K, A_V, A_B = 0, 2176, 4352, 6528, 8736

    def load_head(self, ai, q_src, gate_col, k_src, v_src, kw=NCOL, with_kv=True):
        ar = self.arena_r[ai]
        ab = self.arena_bf(ai)
        qh, qoff, qres = q_src
        self.dma(ab[:, self.A_Q:self.A_Q + 2064], AP(qh, qoff, [[NCOL, 128], [1, 2064]]), ar, reads=[qres], adds=[ar])
        G, Gr = self.gate_s
        gv = ab[:, self.A_G:self.A_G + 2176].rearrange("p (t c) -> p t c", t=TT)
        self.dma(gv[:, 0:16, :], AP(G, gate_col, [[2048, 128], [128 * 2048, 16], [1, 128]]), ar, reads=[Gr], adds=[ar])
        self.dma(gv[0:16, 16, :], AP(G, 2048 * 2048 + gate_col, [[2048, 16], [1, 128]]), ar, reads=[Gr], adds=[ar])
        if with_kv:
            kh, koff, kres = k_src
            self.dma(ab[:, self.A_K:self.A_K + 2064], AP(kh, koff, [[kw, 128], [1, 2064]]), ar, reads=[kres], adds=[ar])
            vh, voff, vres = v_src
            vv = ab[:, self.A_V:self.A_V + TT * 129].rearrange("p (t c) -> p t c", t=TT)
            self.dma(vv[:, 0:16, :], AP(vh, voff, [[129, 128], [128 * 129, 16], [1, 129]]), ar, reads=[vres], adds=[ar])
            self.dma(vv[0:16, 16, :], AP(vh, voff + 16 * 128 * 129, [[129, 16], [1, 129]]), ar, reads=[vres], adds=[ar])
        return ab, gv

    def bias_view(self, ai, n):
        f0 = self.A_B // 2
        return self.arena[ai][:, f0:f0 + n * 128].rearrange("p (a b) -> p a b", a=n)

    def load_bias(self, ai, Frep, L, h, offs_nk_nq):
        ar = self.arena_r[ai]
        bv = self.bias_view(ai, len(offs_nk_nq))
        Ft, Fr = Frep
        for i, (off, nk, nq) in enumerate(offs_nk_nq):
            self.dma(bv[0:nk, i, 0:nq], AP(Ft, h * 128 * L + off, [[L - 1, nk], [1, nq]]), ar, reads=[Fr], adds=[ar])
        return bv

    def layer(self, L):
        kind, j = L % 3, L // 3
        din = self.din
        if kind == 0:
            self.dma(self.gbc[:, 0:128], din["a_k_norm_g"].ap()[j].partition_broadcast(128), self.gbc_r, writes=[self.gbc_r])
        elif kind == 1:
            self.dma(self.gbc[:, 0:256], din["b_ckv_norm_g"].ap()[0].partition_broadcast(128), self.gbc_r, writes=[self.gbc_r])
        else:
            self.dma(self.gbc[:, 0:128], din["c_k_norm_g"].ap()[0].partition_broadcast(128), self.gbc_r, writes=[self.gbc_r])
        self.dma(self.gbc[:, 256:384], din["xk_norm_g"].ap()[L].partition_broadcast(128), self.gbc_r, adds=[self.gbc_r])
        if L == 0:
            xp = din["x_prompt"]
            xs = din["x_sample"]
            srcs = [(AP(xp, t * 128 * D, [[D, 128], [1, D]]), 128, t) for t in range(16)]
            srcs.append((AP(xs, 0, [[D, 16], [1, D]]), 16, 16))
            xrd = []
        else:
            xh, xr_ = self.xres[(L - 1) % 2]
            srcs = [(AP(xh, t * 128 * D, [[D, rows_of(t)], [1, D]]), rows_of(t), t) for t in range(TT)]
            xrd = [xr_]
        self._xsrc = srcs
        self.norm_transpose_x(srcs, xrd, L * 16)
        if self.chk(L, "N"):
            return
        self.mem_phase(L)
        if self.chk(L, "M"):
            return
        if kind == 0:
            self.layer_a(L, j)
        elif kind == 1:
            self.layer_b(L)
        else:
            self.layer_c(L)
        if self.stopped or self.chk(L, "T"):
            return
        self.mem_heads(L)
        if self.chk(L, "X"):
            return
        self.out_phase(L)

    def norm_transpose_x(self, srcs, xrd, gcol0):
        S = self
        orig = S.dma

        def dst(key, jj):
            rows = rows_of(key)
            return S.actT[:, 4 * jj:4 * jj + 4, key * 128: key * 128 + rows]

        if xrd:
            def dma2(out, in_, sres, reads=(), writes=(), adds=(), q="sp", is_output=False):
                return orig(out, in_, sres, reads=list(reads) + xrd, writes=writes, adds=adds, q=q, is_output=is_output)
            S.dma = dma2
        try:
            S.norm_transpose(srcs, gcol0, dst, lambda key: S.act_r[key])
        finally:
            S.dma = orig

    def mem_phase(self, L):
        din = self.din
        mp = din["mem_prompt"]
        srcs = [(AP(mp, t * 128 * D, [[D, 128], [1, D]]), 128, t) for t in range(2)]
        memT = self.memT()
        self.norm_transpose(srcs, 64 + L * 16, lambda key, jj: memT[:, 4 * jj:4 * jj + 4, key * 128:(key + 1) * 128],
                            lambda key: self.R_r)
        if self.chk(L, "M1"):
            return
        mko = self.dout["mem_k_prompt"]
        mvo = self.dout["mem_v_prompt"]

        dbg = "abcde"

        def ep(key, rows, bi, acc, accr):
            if bi == 0:
                if "a" not in dbg:
                    return
                zv, zr = self.headnorm(acc, accr, rows, 4, 128)
                if "b" in dbg:
                    self.transpose_out(zv, zr, rows, 4, self.gcolB[:, 4 + L:5 + L],
                                       dst_sb=self.mkT[:, :, key * 128:(key + 1) * 128], dst_res=self.mkT_r)
                if "c" in dbg:
                    self.gain_rows_out(zv, zr, rows, 4, self.gbc[0:rows, 256:384],
                                       AP(mko, L * 256 * 512 + key * 128 * 512, [[512, rows], [1, 512]]))
            else:
                accv = acc.rearrange("p (a b) -> p a b", a=4)
                if "d" in dbg:
                    self.v_out(accv, accr, rows, 4, None, None, dst_sb=self.mv[:, key, :, 0:128], dst_res=self.mv_r)
                if "e" in dbg:
                    self.rows_out(acc, accr, rows, 512, AP(mvo, L * 256 * 512 + key * 128 * 512, [[512, rows], [1, 512]]))

        self.proj((din["w_mem_kv"], L * 2048 * 1024), 1024, 16, [(0, 512), (512, 512)],
                  [(t, 128, [self.R_r]) for t in range(2)], lambda key, k: memT[:, k, key * 128:(key + 1) * 128], ep)
        if self.chk(L, "M2"):
            return
        ck = din["cache_mem_k"]
        cv = din["cache_mem_v"]
        for t in range(2):
            i = self.rot("ld", 2)
            ld = self.ld[i]
            lr = self.ld_r[i]
            self.dma(ld[:, :], AP(ck, L * 256 * 512 + t * 128 * 512, [[512, 128], [1, 512]]), lr, writes=[lr])
            tpb, tpr = self.nb("tp")
            tpv = tpb.rearrange("p (a b) -> p a b", a=4)
            for c in range(4):
                self.op("pe", "transpose", reads=[lr, self.ident_r], writes=[tpr], out=tpv[:, c, :],
                        in_=ld[:, c * 128:(c + 1) * 128], identity=self.ident[:])
            self.op("act", "activation", reads=[tpr], writes=[self.mkTs_r], out=self.mkTs[:, :, t * 128:(t + 1) * 128],
                    in_=tpv[:, 0:4, :], func=AF.Copy)
            i = self.rot("ld", 2)
            ld = self.ld[i]
            lr = self.ld_r[i]
            self.dma(ld[:, :], AP(cv, L * 256 * 512 + t * 128 * 512, [[512, 128], [1, 512]]), lr, writes=[lr])
            self.op("dve", "tensor_copy", reads=[lr], writes=[self.mvs_r], out=self.mvs[:, t, :, 0:128],
                    in_=ld[:, :].rearrange("p (a b) -> p a b", a=4))

    def x_tiles(self):
        return [(t, rows_of(t), [self.act_r[t]]) for t in range(TT)]

    def layer_a(self, L, j):
        din = self.din
        dout = self.dout
        gq = self.gcolB[:, 8 + j:9 + j]
        gk = self.gcolB[:, 10 + j:11 + j]
        gxq = self.gcolB[:, L:L + 1]
        blocks = [(c * 512, 512) for c in range(10)]

        def ep(tt, rows, bi, acc, accr):
            if bi < 3:
                zv, zr = self.headnorm(acc, accr, rows, 4, 128)
                self.transpose_out(zv, zr, rows, 4, gq, dst_dram=self.qk_store(self.qT_s, bi * 4, 4, tt, rows),
                                   dram_res=self.qT_s[1])
            elif bi == 3:
                zv, zr = self.headnorm(acc, accr, rows, 4, 128)
                self.transpose_out(zv, zr, rows, 4, gk, dst_dram=self.qk_store(self.kT_s, 0, 4, tt, rows),
                                   dram_res=self.kT_s[1])
                if tt == 15:
                    self.gain_rows_out(zv, zr, rows, 4, self.gbc[0:rows, 0:128],
                                       AP(dout["a_k_prompt"], j * 128 * 512, [[512, 128], [1, 512]]))
                elif tt == 16:
                    self.gain_rows_out(zv, zr, rows, 4, self.gbc[0:rows, 0:128],
                                       AP(dout["a_k_sample"], j * 16 * 512, [[512, 16], [1, 512]]))
            elif bi == 4:
                accv = acc.rearrange("p (a b) -> p a b", a=4)
                self.v_out(accv, accr, rows, 4, self.v_store(self.v_s, 0, 4, tt, rows), self.v_s[1])
                if tt == 15:
                    self.rows_out(acc, accr, rows, 512, AP(dout["a_v_prompt"], j * 128 * 512, [[512, 128], [1, 512]]))
                elif tt == 16:
                    self.rows_out(acc, accr, rows, 512, AP(dout["a_v_sample"], j * 16 * 512, [[512, 16], [1, 512]]))
            elif bi == 5:
                zv, zr = self.headnorm(acc, accr, rows, 4, 128)
                self.transpose_out(zv, zr, rows, 4, gxq, dst_dram=self.qk_store(self.xqT_s, 0, 4, tt, rows),
                                   dram_res=self.xqT_s[1])
            else:
                self.gate_out(acc, accr, rows, 512, tt, (bi - 6) * 512)

        self.proj((din["a_w_in"], j * 2048 * 5120), 5120, 16, blocks, self.x_tiles(), self.act_ap, ep)
        if self.chk(L, "P"):
            return
        i = self.rot("ld", 2)
        ld = self.ld[i]
        lr = self.ld_r[i]
        self.dma(ld[:, :], AP(din["cache_a_k"], j * 128 * 512, [[512, 128], [1, 512]]), lr, writes=[lr])
        tpb, tpr = self.nb("tp")
        tpv = tpb.rearrange("p (a b) -> p a b", a=4)
        for c in range(4):
            self.op("pe", "transpose", reads=[lr, self.ident_r], writes=[tpr], out=tpv[:, c, :],
                    in_=ld[:, c * 128:(c + 1) * 128], identity=self.ident[:])
        self.op("act", "activation", reads=[tpr], writes=[self.kcA_r], out=self.kcA[:], in_=tpv[:, 0:4, :], func=AF.Copy)
        i = self.rot("ld", 2)
        ld = self.ld[i]
        lr = self.ld_r[i]
        self.dma(ld[:, :], AP(din["cache_a_v"], j * 128 * 512, [[512, 128], [1, 512]]), lr, writes=[lr])
        self.op("dve", "tensor_copy", reads=[lr], writes=[self.vcA_r], out=self.vcA[:, :, 0:128],
                in_=ld[:, :].rearrange("p (a b) -> p a b", a=4))
        sc = 128 ** -0.5
        for h in range(12):
            kh = h // 3
            ai = self.rot("arena", 2)
            ar = self.arena_r[ai]
            ab, gv = self.load_head(ai, (self.qT_s[0], h * 128 * NCOL, self.qT_s[1]), h * 128,
                                    (self.kT_s[0], kh * 128 * NCOL, self.kT_s[1]),
                                    (self.v_s[0], kh * TT * 128 * 129, self.v_s[1]))
            bv = self.load_bias(ai, self.FrepA, 384, h,
                                [(256, 128, 128), (128, 128, 128), (256, 128, 16), (128, 16, 16)])
            self.op("pool", "memset", writes=[ar], ap=bv[64:128, 1, 0:64], constant=NEG)
            self.op("pool", "memset", writes=[ar], ap=bv[0:64, 0, 64:128], constant=NEG)
            self.op("act", "activation", writes=[ar], out=bv[:, 0:2, :], in_=bv[:, 0:2, :], func=AF.Exp)
            self.op("act", "activation", writes=[ar], out=bv[:, 2, 0:16], in_=bv[:, 2, 0:16], func=AF.Exp)
            self.op("act", "activation", writes=[ar], out=bv[0:16, 3, 0:16], in_=bv[0:16, 3, 0:16], func=AF.Exp)
            q = ab[:, self.A_Q:self.A_Q + NCOL]
            k = ab[:, self.A_K:self.A_K + NCOL]
            vv = ab[:, self.A_V:self.A_V + TT * 129].rearrange("p (t c) -> p t c", t=TT)
            sink = self.esink[:, j * 12 + h: j * 12 + h + 1]
            for t in range(16):
                kts = []
                if t >= 1:
                    kts.append(dict(qk=[(k[:, (t - 1) * 128:t * 128], q[:, t * 128:(t + 1) * 128])], nk=128,
                                    v=vv[:, t - 1, :], bias=(bv, 0), reads=[ar]))
                kts.append(dict(qk=[(k[:, t * 128:(t + 1) * 128], q[:, t * 128:(t + 1) * 128])], nk=128,
                                v=vv[:, t, :], bias=(bv, 1), reads=[ar]))
                self.attn(kts, 128, sc, sink, gv[:, t, :], [ar], self.actT[:, h, t * 128:(t + 1) * 128], self.act_r[t])
            qs = q[:, 2048:2064]
            kts = [dict(qk=[(self.kcA[:, kh, :], qs)], nk=128, v=self.vcA[:, kh, :], bias=(bv, 2),
                        reads=[ar, self.kcA_r, self.vcA_r]),
                   dict(qk=[(k[:, 2048:2064], qs)], nk=16, v=vv[0:16, 16, :], bias=(bv, 3), reads=[ar])]
            self.attn(kts, 16, sc, sink[0:16, :], gv[0:16, 16, :], [ar], self.actT[:, h, 2048:2064], self.act_r[16])
            self.flush()

    def layer_c(self, L):
        din = self.din
        dout = self.dout
        gq = self.gcolB[:, 12:13]
        gk = self.gcolB[:, 13:14]
        gxq = self.gcolB[:, L:L + 1]
        blocks = [(c * 512, 512) for c in range(14)]

        def ep(tt, rows, bi, acc, accr):
            if bi < 3:
                zv, zr = self.headnorm(acc, accr, rows, 4, 128)
                self.transpose_out(zv, zr, rows, 4, gq, dst_dram=self.qk_store(self.qT_s, bi * 4, 4, tt, rows),
                                   dram_res=self.qT_s[1])
            elif bi < 6:
                b = bi - 3
                zv, zr = self.headnorm(acc, accr, rows, 4, 128)
                self.transpose_out(zv, zr, rows, 4, gk, dst_dram=self.qk_store(self.kT_s, b * 4, 4, tt, rows),
                                   dram_res=self.kT_s[1])
                if 12 <= tt < 16:
                    self.gain_rows_out(zv, zr, rows, 4, self.gbc[0:rows, 0:128],
                                       AP(dout["c_k_prompt"], (tt - 12) * 128 * 1536 + b * 512, [[1536, 128], [1, 512]]))
                elif tt == 16:
                    self.gain_rows_out(zv, zr, rows, 4, self.gbc[0:rows, 0:128],
                                       AP(dout["c_k_sample"], b * 512, [[1536, 16], [1, 512]]))
            elif bi < 9:
                b = bi - 6
                accv = acc.rearrange("p (a b) -> p a b", a=4)
                self.v_out(accv, accr, rows, 4, self.v_store(self.v_s, b * 4, 4, tt, rows), self.v_s[1])
                if 12 <= tt < 16:
                    self.rows_out(acc, accr, rows, 512,
                                  AP(dout["c_v_prompt"], (tt - 12) * 128 * 1536 + b * 512, [[1536, 128], [1, 512]]))
                elif tt == 16:
                    self.rows_out(acc, accr, rows, 512, AP(dout["c_v_sample"], b * 512, [[1536, 16], [1, 512]]))
            elif bi == 9:
                zv, zr = self.headnorm(acc, accr, rows, 4, 128)
                self.transpose_out(zv, zr, rows, 4, gxq, dst_dram=self.qk_store(self.xqT_s, 0, 4, tt, rows),
                                   dram_res=self.xqT_s[1])
            else:
                self.gate_out(acc, accr, rows, 512, tt, (bi - 10) * 512)

        self.proj((din["c_w_in"], 0), 7168, 16, blocks, self.x_tiles(), self.act_ap, ep)
        if self.chk(L, "P"):
            return
        sc = 128 ** -0.5
        ck = din["cache_c_k"]
        cv = din["cache_c_v"]
        for h in range(12):
            ai = self.rot("arena", 2)
            ar = self.arena_r[ai]
            ab, gv = self.load_head(ai, (self.qT_s[0], h * 128 * NCOL, self.qT_s[1]), h * 128,
                                    (self.kT_s[0], h * 128 * NCOL, self.kT_s[1]),
                                    (self.v_s[0], h * TT * 128 * 129, self.v_s[1]))
            bv = self.load_bias(ai, self.FrepC, 768, h, [(128 + 128 * (4 - i), 128, 128) for i in range(5)])
            f0 = self.A_B // 2 + 640
            bs = self.arena[ai][:, f0:f0 + 80].rearrange("p (a b) -> p a b", a=5)
            Ft, Fr = self.FrepC
            for i_, (off_, nk_) in enumerate([(640 - 128 * jc, 128) for jc in range(4)] + [(128, 16)]):
                self.dma(bs[0:nk_, i_, 0:16], AP(Ft, h * 128 * 768 + off_, [[767, nk_], [1, 16]]), ar, reads=[Fr], adds=[ar])
            self.op("pool", "memset", writes=[ar], ap=bv[64:128, 4, 0:64], constant=NEG)
            self.op("pool", "memset", writes=[ar], ap=bv[0:64, 0, 64:128], constant=NEG)
            self.op("act", "activation", writes=[ar], out=bv[:, 0:5, :], in_=bv[:, 0:5, :], func=AF.Exp)
            self.op("act", "activation", writes=[ar], out=bs[:, 0:4, :], in_=bs[:, 0:4, :], func=AF.Exp)
            self.op("act", "activation", writes=[ar], out=bs[0:16, 4, :], in_=bs[0:16, 4, :], func=AF.Exp)
            q = ab[:, self.A_Q:self.A_Q + NCOL]
            k = ab[:, self.A_K:self.A_K + NCOL]
            vv = ab[:, self.A_V:self.A_V + TT * 129].rearrange("p (t c) -> p t c", t=TT)
            for t in range(16):
                kts = []
                for o in range(4, -1, -1):
                    jt = t - o
                    if jt < 0:
                        continue
                    kts.append(dict(qk=[(k[:, jt * 128:(jt + 1) * 128], q[:, t * 128:(t + 1) * 128])], nk=128,
                                    v=vv[:, jt, :], bias=(bv, 4 - o), reads=[ar]))
                self.attn(kts, 128, sc, None, gv[:, t, :], [ar], self.actT[:, h, t * 128:(t + 1) * 128], self.act_r[t])
            self.flush()
            i = self.rot("ld", 2)
            ld = self.ld[i]
            lr = self.ld_r[i]
            self.dma(ld[:, :].rearrange("p (a b) -> p a b", a=4), AP(ck, h * 128, [[1536, 128], [128 * 1536, 4], [1, 128]]),
                     lr, writes=[lr])
            tpb, tpr = self.nb("tp")
            tpv = tpb.rearrange("p (a b) -> p a b", a=4)
            for c in range(4):
                self.op("pe", "transpose", reads=[lr, self.ident_r], writes=[tpr], out=tpv[:, c, :],
                        in_=ld[:, c * 128:(c + 1) * 128], identity=self.ident[:])
            self.op("act", "activation", reads=[tpr], writes=[self.kcC_r], out=self.kcC[:], in_=tpb[:, 0:512], func=AF.Copy)
            i = self.rot("ld", 2)
            ld = self.ld[i]
            lr = self.ld_r[i]
            self.dma(ld[:, :].rearrange("p (a b) -> p a b", a=4), AP(cv, h * 128, [[1536, 128], [128 * 1536, 4], [1, 128]]),
                     lr, writes=[lr])
            self.op("dve", "tensor_copy", reads=[lr], writes=[self.vcC_r], out=self.vcC[:, :, 0:128],
                    in_=ld[:, :].rearrange("p (a b) -> p a b", a=4))
            qs = q[:, 2048:2064]
            kts = []
            for jc in range(4):
                kts.append(dict(qk=[(self.kcC[:, jc * 128:(jc + 1) * 128], qs)], nk=128, v=self.vcC[:, jc, :],
                                bias=(bs, jc), reads=[ar, self.kcC_r, self.vcC_r]))
            kts.append(dict(qk=[(k[:, 2048:2064], qs)], nk=16, v=vv[0:16, 16, :], bias=(bs, 4), reads=[ar]))
            self.attn(kts, 16, sc, None, gv[0:16, 16, :], [ar], self.actT[:, h, 2048:2064], self.act_r[16])
            self.flush()

    def rope(self, x1, x2, rd, rows, nh, tt, scale_ap, out1, out2, wr):
        rp = self.rp
        rr = self.rp_r
        cs = self.cosT[0:rows, tt, :].unsqueeze(1).broadcast_to([rows, nh, 32])
        sn = self.sinT[0:rows, tt, :].unsqueeze(1).broadcast_to([rows, nh, 32])
        def t(i):
            return rp[0:rows, i, 0:nh * 32].rearrange("p (a b) -> p a b", a=nh)
        crd = [self.cos_r, self.sin_r]
        self.op("dve", "tensor_tensor", reads=rd + crd, writes=[rr], out=t(0), in0=x1, in1=cs, op=ALU.mult)
        self.op("dve", "tensor_tensor", reads=rd + crd, writes=[rr], out=t(1), in0=x2, in1=sn, op=ALU.mult)
        self.op("dve", "tensor_tensor", reads=rd + crd, writes=[rr], out=t(2), in0=x1, in1=sn, op=ALU.mult)
        self.op("dve", "tensor_tensor", reads=rd + crd, writes=[rr], out=t(3), in0=x2, in1=cs, op=ALU.mult)
        if scale_ap is None:
            self.op("dve", "tensor_tensor", reads=[rr], writes=wr, out=out1, in0=t(0), in1=t(1), op=ALU.subtract)
            self.op("dve", "tensor_tensor", reads=[rr], writes=wr, out=out2, in0=t(2), in1=t(3), op=ALU.add)
        else:
            self.op("dve", "tensor_tensor", reads=[rr], writes=[rr], out=t(4), in0=t(0), in1=t(1), op=ALU.subtract)
            self.op("dve", "tensor_tensor", reads=[rr], writes=[rr], out=t(5), in0=t(2), in1=t(3), op=ALU.add)
            self.op("dve", "tensor_tensor", reads=[rr] + rd, writes=wr, out=out1, in0=t(4), in1=scale_ap, op=ALU.mult)
            self.op("dve", "tensor_tensor", reads=[rr] + rd, writes=wr, out=out2, in0=t(5), in1=scale_ap, op=ALU.mult)

    def layer_b(self, L):
        din = self.din
        dout = self.dout
        gxq = self.gcolB[:, L:L + 1]
        cqT = self.cqT()
        blocks = [(0, 512), (512, 320), (832, 512)] + [(1344 + c * 512, 512) for c in range(4)]

        def ep(tt, rows, bi, acc, accr):
            if bi == 0:
                zv, zr = self.headnorm(acc, accr, rows, 1, 512)
                z4 = zv.rearrange("p a (c b) -> p (a c) b", c=4)

                def tail():
                    tpb, tpr = self.nb("tp")
                    tpv = tpb.rearrange("p (a b) -> p a b", a=4)
                    for c in range(4):
                        self.op("pe", "transpose", reads=[zr, self.ident_r], writes=[tpr], out=tpv[:, c, 0:rows],
                                in_=z4[:, c, :], identity=self.ident[0:rows, 0:rows])
                    g = self.gcolB[:, 14:18].unsqueeze(2).broadcast_to([128, 4, rows])
                    self.op("dve", "tensor_tensor", reads=[tpr, self.gcolB_r], adds=[self.cq_r],
                            out=cqT[:, :, tt * 128:tt * 128 + rows], in0=tpv[:, 0:4, 0:rows], in1=g, op=ALU.mult)
                self.defer(tail, 3)
            elif bi == 1:
                zv, zr = self.headnorm(acc[:, 0:256], accr, rows, 1, 256)
                i = self.rot("of", 2)
                of = self.of[i]
                orr = self.of_r[i]
                if tt < 16:
                    dst_ckv = AP(dout["b_ckv_prompt"], tt * 128 * 256, [[256, 128], [1, 256]])
                else:
                    dst_ckv = AP(dout["b_ckv_sample"], 0, [[256, 16], [1, 256]])

                def s1():
                    self.op("dve", "tensor_tensor", reads=[zr, self.gbc_r], writes=[orr], out=of[0:rows, 0:256],
                            in0=zv.rearrange("p a b -> p (a b)"), in1=self.gbc[0:rows, 0:256], op=ALU.mult)
                    self.dma(dst_ckv, of[0:rows, 0:256], orr, reads=[orr], is_output=True)
                self.defer(s1, 1)

                def tail():
                    tpb, tpr = self.nb("tp")
                    tpv = tpb.rearrange("p (a b) -> p a b", a=4)
                    for c in range(2):
                        self.op("pe", "transpose", reads=[orr, self.ident_r], writes=[tpr], out=tpv[:, c, 0:rows],
                                in_=of[0:rows, c * 128:(c + 1) * 128], identity=self.ident[0:rows, 0:rows])
                    self.op("act", "activation", reads=[tpr], adds=[self.ckvT_r], out=self.ckvT[:, :, tt * 128:tt * 128 + rows],
                            in_=tpv[:, 0:2, 0:rows], func=AF.Copy)
                self.defer(tail, 3)
                x1 = acc[:, 256:288].unsqueeze(1)
                x2 = acc[:, 288:320].unsqueeze(1)
                o1 = self.kr_all[0:rows, tt, 0:32].unsqueeze(1)
                o2 = self.kr_all[0:rows, tt, 32:64].unsqueeze(1)
                self.rope(x1, x2, [accr], rows, 1, tt, None, o1, o2, [self.kr_all_r])
                if tt < 16:
                    dst = AP(dout["b_krope_prompt"], tt * 128 * 64, [[64, 128], [1, 64]])
                else:
                    dst = AP(dout["b_krope_sample"], 0, [[64, 16], [1, 64]])
                self.dma(dst, self.kr_all[0:rows, tt, :], self.kr_all_r, reads=[self.kr_all_r], is_output=True)
                self.op("act", "activation", reads=[self.kr_all_r], writes=[self.sq_r, self.krss_r], out=self.sq[0:rows, 0:64],
                        in_=self.kr_all[0:rows, tt, :], func=AF.Square, accum_out=self.krss[0:rows, tt:tt + 1])
                self.op("dve", "tensor_scalar", reads=[self.krss_r], writes=[self.krss_r], out=self.krss[0:rows, tt:tt + 1],
                        in0=self.krss[0:rows, tt:tt + 1], scalar1=1.0 / 192, scalar2=EPS, op0=ALU.mult, op1=ALU.add)
            elif bi == 2:
                zv, zr = self.headnorm(acc, accr, rows, 4, 128)
                self.transpose_out(zv, zr, rows, 4, gxq, dst_dram=self.qk_store(self.xqT_s, 0, 4, tt, rows),
                                   dram_res=self.xqT_s[1])
            else:
                self.gate_out(acc, accr, rows, 512, tt, (bi - 3) * 512)

        self.proj((din["b_w_in"], 0), 3392, 16, blocks, self.x_tiles(), self.act_ap, ep)
        if self.chk(L, "P"):
            return

        def epq(tt, rows, bi, acc, accr):
            h0 = bi * 2
            self.op("act", "activation", reads=[accr], writes=[self.sq_r], out=self.sq[0:rows, 0:384], in_=acc, func=AF.Square)
            st, sr = self.newstat()
            self.op("dve", "tensor_reduce", reads=[self.sq_r], writes=[sr], out=st[0:rows, 0:2],
                    in_=self.sq[0:rows, 0:384].rearrange("p (a b) -> p a b", a=2), axis=AX.X, op=ALU.add)
            self.op("dve", "tensor_scalar", reads=[sr], writes=[sr], out=st[0:rows, 4:6], in0=st[0:rows, 0:2],
                    scalar1=1.0 / 192, scalar2=EPS, op0=ALU.mult, op1=ALU.add)
            self.op("pool", "tensor_tensor", reads=[sr, self.cm05_r], writes=[sr], out=st[0:rows, 8:10],
                    in0=st[0:rows, 4:6], in1=self.cm05[0:rows, 0:2], op=ALU.pow)
            i = self.rot("zf", 2)
            zf = self.zf[i]
            zr = self.zf_r[i]
            a3 = acc.rearrange("p (a b) -> p a b", a=2)
            z3 = zf[0:rows, 0:384].rearrange("p (a b) -> p a b", a=2)
            def s1():
                self.op("dve", "tensor_tensor", reads=[accr, sr], writes=[zr], out=z3[:, :, 0:128], in0=a3[:, :, 0:128],
                        in1=st[0:rows, 8:10].unsqueeze(2).broadcast_to([rows, 2, 128]), op=ALU.mult)
                rs32 = st[0:rows, 8:10].unsqueeze(2).broadcast_to([rows, 2, 32])
                self.rope(a3[:, :, 128:160], a3[:, :, 160:192], [accr, sr], rows, 2, tt, rs32, z3[:, :, 128:160],
                          z3[:, :, 160:192], [zr])
            self.defer(s1, 1)

            def tail():
                tpb, tpr = self.nb("tp")
                tpv = tpb.rearrange("p (a b) -> p a b", a=4)
                for c in range(2):
                    self.op("pe", "transpose", reads=[zr, self.ident_r], writes=[tpr], out=tpv[:, c, 0:rows],
                            in_=z3[:, c, 0:128], identity=self.ident[0:rows, 0:rows])
                    self.op("pe", "transpose", reads=[zr, self.ident_r], writes=[tpr], out=tpv[0:64, 2 + c, 0:rows],
                            in_=z3[:, c, 128:192], identity=self.ident[0:rows, 0:rows])
                i = self.rot("tb", 3)
                tb = self.tb[i]
                tr = self.tb_r[i]
                self.op("act", "activation", reads=[tpr, self.gcolB_r], writes=[tr], out=tb[:, 0:2, 0:rows],
                        in_=tpv[:, 0:2, 0:rows], func=AF.Copy, scale=self.gcolB[:, 20:21])
                self.op("act", "activation", reads=[tpr, self.gcolB_r], writes=[tr], out=tb[0:64, 2:4, 0:rows],
                        in_=tpv[0:64, 2:4, 0:rows], func=AF.Copy, scale=self.gcolB[0:64, 21:22])
                self.dma(self.qk_store(self.q192_s, h0, 2, tt, rows, hrows=192), tb[:, 0:2, 0:rows], tr, reads=[tr],
                         adds=[self.q192_s[1]])
                self.dma(self.qk_store(self.q192_s, h0, 2, tt, rows, dpart=64, hrows=192, r0=128), tb[0:64, 2:4, 0:rows], tr,
                         reads=[tr], adds=[self.q192_s[1]])
            self.defer(tail, 3)

        self.proj((din["b_w_q_b"], 0), 2304, 4, [(c * 384, 384) for c in range(6)],
                  [(t, rows_of(t), [self.cq_r]) for t in range(TT)],
                  lambda key, k: cqT[:, k, key * 128:key * 128 + rows_of(key)], epq)

        def epkv(key, rows, bi, acc, accr):
            kind_, idx = key
            h0 = bi * 2
            if kind_ == "c":
                kr = self.krt[idx % 3][0:rows, :]
                krr = [self.krt_r[idx % 3]]
                ssap = self.krss_c[0:rows, idx:idx + 1]
                ssr = [self.krss_c_r]
            else:
                kr = self.kr_all[0:rows, idx, :]
                krr = [self.kr_all_r]
                ssap = self.krss[0:rows, idx:idx + 1]
                ssr = [self.krss_r]
            a4 = acc.rearrange("p (a b) -> p a b", a=2)
            sq2 = self.sq[0:rows, 0:256].rearrange("p (a b) -> p a b", a=2)
            self.op("act", "activation", reads=[accr], writes=[self.sq_r], out=sq2, in_=a4[:, :, 0:128], func=AF.Square)
            st, sr = self.newstat()
            self.op("dve", "tensor_reduce", reads=[self.sq_r], writes=[sr], out=st[0:rows, 0:2], in_=sq2, axis=AX.X, op=ALU.add)
            self.op("dve", "tensor_scalar", reads=[sr] + ssr, writes=[sr], out=st[0:rows, 4:6], in0=st[0:rows, 0:2],
                    scalar1=1.0 / 192, scalar2=ssap, op0=ALU.mult, op1=ALU.add)
            self.op("pool", "tensor_tensor", reads=[sr, self.cm05_r], writes=[sr], out=st[0:rows, 8:10],
                    in0=st[0:rows, 4:6], in1=self.cm05[0:rows, 0:2], op=ALU.pow)
            rs2 = st[0:rows, 8:10].unsqueeze(2)
            i = self.rot("zf", 2)
            zf = self.zf[i]
            zr = self.zf_r[i]
            z3 = zf[0:rows, 0:384].rearrange("p (a b) -> p a b", a=2)
            def s1():
                self.op("dve", "tensor_tensor", reads=[accr, sr], writes=[zr], out=z3[:, :, 0:128], in0=a4[:, :, 0:128],
                        in1=rs2.broadcast_to([rows, 2, 128]), op=ALU.mult)
                for c_ in range(2):
                    self.op("act", "activation", reads=krr + [sr], writes=[zr], out=z3[:, c_, 128:192], in_=kr, func=AF.Copy,
                            scale=st[0:rows, 8 + c_:9 + c_])
            self.defer(s1, 1)
            if kind_ == "p":
                kd, vd, tile, width, nt = self.k192_s, self.v_s, idx, NCOL, TT
            else:
                tile = idx if kind_ == "c" else 32
                kd, vd, width, nt = self.k192s_s, self.vs_s, 4224, 33

            def tail():
                tpb, tpr = self.nb("tp")
                tpv = tpb.rearrange("p (a b) -> p a b", a=4)
                for c in range(2):
                    self.op("pe", "transpose", reads=[zr, self.ident_r], writes=[tpr], out=tpv[:, c, 0:rows],
                            in_=z3[:, c, 0:128], identity=self.ident[0:rows, 0:rows])
                    self.op("pe", "transpose", reads=[zr, self.ident_r], writes=[tpr], out=tpv[0:64, 2 + c, 0:rows],
                            in_=z3[:, c, 128:192], identity=self.ident[0:rows, 0:rows])
                i = self.rot("tb", 3)
                tb = self.tb[i]
                tr = self.tb_r[i]
                self.op("act", "activation", reads=[tpr, self.gcolB_r], writes=[tr], out=tb[:, 0:2, 0:rows],
                        in_=tpv[:, 0:2, 0:rows], func=AF.Copy, scale=self.gcolB[:, 22:23])
                self.op("act", "activation", reads=[tpr, self.gcolB_r], writes=[tr], out=tb[0:64, 2:4, 0:rows],
                        in_=tpv[0:64, 2:4, 0:rows], func=AF.Copy, scale=self.gcolB[0:64, 23:24])
                self.dma(self.qk_store(kd, h0, 2, tile, rows, width=width, hrows=192), tb[:, 0:2, 0:rows], tr, reads=[tr],
                         adds=[kd[1]])
                self.dma(self.qk_store(kd, h0, 2, tile, rows, width=width, dpart=64, hrows=192, r0=128), tb[0:64, 2:4, 0:rows],
                         tr, reads=[tr], adds=[kd[1]])
            self.defer(tail, 3)
            self.v_out(a4[:, :, 128:256], accr, rows, 2, self.v_store(vd, h0, 2, tile, rows, ntiles=nt), vd[1])

        tiles = [(("p", t), 128, [self.ckvT_r]) for t in range(16)] + [(("s", 16), 16, [self.ckvT_r])]

        def actkv(key, k):
            kind_, idx = key
            if kind_ == "c":
                return self.ckvt[idx % 2][:, k, :]
            rows = rows_of(idx)
            return self.ckvT[:, k, idx * 128: idx * 128 + rows]

        cache_tiles = []
        for jc in range(32):
            cache_tiles.append((("c", jc), 128, [self.ckvt_r[jc % 2]]))
        self._kv_prep_pending = True
        self.proj_kv((din["b_w_kv_b"], 0), tiles, cache_tiles, actkv, epkv)

        sc = 192 ** -0.5
        QB, KB = 8736, 10912
        for h in range(12):
            ai = self.rot("arena", 2)
            ar = self.arena_r[ai]
            ab = self.arena_bf(ai)
            Q, Qr = self.q192_s
            Kt, Kr = self.k192_s
            self.dma(ab[0:96, self.A_Q:self.A_Q + 2048], AP(Q, h * 192 * NCOL, [[NCOL, 96], [1, 2048]]), ar, reads=[Qr], adds=[ar])
            self.dma(ab[0:96, QB:QB + 2048], AP(Q, (h * 192 + 96) * NCOL, [[NCOL, 96], [1, 2048]]), ar, reads=[Qr], adds=[ar])
            self.dma(ab[0:96, self.A_K:self.A_K + 2048], AP(Kt, h * 192 * NCOL, [[NCOL, 96], [1, 2048]]), ar, reads=[Kr], adds=[ar])
            self.dma(ab[0:96, KB:KB + 2048], AP(Kt, (h * 192 + 96) * NCOL, [[NCOL, 96], [1, 2048]]), ar, reads=[Kr], adds=[ar])
            G, Gr = self.gate_s
            gv = ab[:, self.A_G:self.A_G + 2176].rearrange("p (t c) -> p t c", t=TT)
            self.dma(gv[:, 0:16, :], AP(G, h * 128, [[2048, 128], [128 * 2048, 16], [1, 128]]), ar, reads=[Gr], adds=[ar])
            vv = ab[:, self.A_V:self.A_V + TT * 129].rearrange("p (t c) -> p t c", t=TT)
            self.dma(vv[:, 0:16, :], AP(self.v_s[0], h * TT * 128 * 129, [[129, 128], [128 * 129, 16], [1, 129]]), ar,
                     reads=[self.v_s[1]], adds=[ar])
            qa = ab[0:96, self.A_Q:self.A_Q + NCOL]
            qb = ab[0:96, QB:QB + NCOL]
            ka = ab[0:96, self.A_K:self.A_K + 2048]
            kb = ab[0:96, KB:KB + 2048]
            for t in range(16):
                kts = []
                for jt in range(t + 1):
                    kts.append(dict(qk=[(ka[:, jt * 128:(jt + 1) * 128], qa[:, t * 128:(t + 1) * 128]),
                                        (kb[:, jt * 128:(jt + 1) * 128], qb[:, t * 128:(t + 1) * 128])], nk=128,
                                    v=vv[:, jt, :], bias=None, diag=(jt == t), reads=[ar]))
                self.attn(kts, 128, sc, None, gv[:, t, :], [ar], self.actT[:, h, t * 128:(t + 1) * 128], self.act_r[t])
            self.flush()
        KA0, KB0, V0, QA0, QB0, G0 = 0, 4224, 8448, 12708, 12724, 12740
        for h in range(12):
            ai = self.rot("arena", 2)
            ar = self.arena_r[ai]
            ab = self.arena_bf(ai)
            Q, Qr = self.q192_s
            Kt, Kr = self.k192s_s
            ka = ab[0:96, KA0:KA0 + 4224]
            kb = ab[0:96, KB0:KB0 + 4224]
            vv = ab[:, V0:V0 + 33 * 129].rearrange("p (t c) -> p t c", t=33)
            qa = ab[0:96, QA0:QA0 + 16]
            qb = ab[0:96, QB0:QB0 + 16]
            gt = ab[0:16, G0:G0 + 128]
            self.dma(ka[:, 0:4112], AP(Kt, h * 192 * 4224, [[4224, 96], [1, 4112]]), ar, reads=[Kr], adds=[ar])
            self.dma(kb[:, 0:4112], AP(Kt, (h * 192 + 96) * 4224, [[4224, 96], [1, 4112]]), ar, reads=[Kr], adds=[ar])
            self.dma(vv[:, 0:32, :], AP(self.vs_s[0], h * 33 * 128 * 129, [[129, 128], [128 * 129, 32], [1, 129]]), ar,
                     reads=[self.vs_s[1]], adds=[ar])
            self.dma(vv[0:16, 32, :], AP(self.vs_s[0], (h * 33 + 32) * 128 * 129, [[129, 16], [1, 129]]), ar,
                     reads=[self.vs_s[1]], adds=[ar])
            self.dma(qa, AP(Q, h * 192 * NCOL + 2048, [[NCOL, 96], [1, 16]]), ar, reads=[Qr], adds=[ar])
            self.dma(qb, AP(Q, (h * 192 + 96) * NCOL + 2048, [[NCOL, 96], [1, 16]]), ar, reads=[Qr], adds=[ar])
            self.dma(gt, AP(self.gate_s[0], 2048 * 2048 + h * 128, [[2048, 16], [1, 128]]), ar, reads=[self.gate_s[1]], adds=[ar])
            s0, s0r = self.nb("s")
            s1, s1r = self.nb("s")
            for jc in range(32):
                self.op("pe", "matmul", reads=[ar], writes=[s0r], out=s0[:, jc * 16:(jc + 1) * 16],
                        lhsT=ka[:, jc * 128:(jc + 1) * 128], rhs=qa, start=True, stop=False)
                self.op("pe", "matmul", reads=[ar], writes=[s0r], out=s0[:, jc * 16:(jc + 1) * 16],
                        lhsT=kb[:, jc * 128:(jc + 1) * 128], rhs=qb, start=False, stop=True)
            self.op("pe", "matmul", reads=[ar], writes=[s1r], out=s1[0:16, 0:16], lhsT=ka[:, 4096:4112], rhs=qa,
                    start=True, stop=False)
            self.op("pe", "matmul", reads=[ar], writes=[s1r], out=s1[0:16, 0:16], lhsT=kb[:, 4096:4112], rhs=qb,
                    start=False, stop=True)
            p0, p0r = self.pexp[0], self.pexp_r[0]
            p1, p1r = self.pexp[1], self.pexp_r[1]
            self.op("act", "activation", reads=[s0r], writes=[p0r], out=p0[:, 0:512], in_=s0[:, 0:512], func=AF.Exp, scale=sc)
            self.op("act", "activation", reads=[s1r], writes=[p1r], out=p1[0:16, 0:16], in_=s1[0:16, 0:16], func=AF.Exp, scale=sc)
            ob, orr = self.nb("o")
            for jc in range(32):
                self.op("pe", "matmul", reads=[p0r, ar], writes=[orr], out=ob[0:16, 0:129], lhsT=p0[:, jc * 16:(jc + 1) * 16],
                        rhs=vv[:, jc, :], start=(jc == 0), stop=False)
            self.op("pe", "matmul", reads=[p1r, ar], writes=[orr], out=ob[0:16, 0:129], lhsT=p1[0:16, 0:16], rhs=vv[0:16, 32, :],
                    start=False, stop=True)
            self.attn_post(ob, orr, 16, None, gt, [ar], self.actT[:, h, 2048:2064], self.act_r[16])
            self.flush()

    def proj_kv(self, W, tiles, cache_tiles, actkv, epkv):
        din = self.din
        wh, woff = W
        ai = self.rot("arena", 2)
        ar = self.arena_r[ai]
        wv = self.arena_bf(ai)[:, 0:2 * 3072].rearrange("p (k w) -> p k w", k=2)
        for c in range(6):
            src = AP(wh, woff + c * 512, [[3072, 128], [128 * 3072, 2], [1, 512]])
            self.dma(wv[:, :, c * 512:(c + 1) * 512], src, ar, adds=[ar], q="pool")
        ck = din["cache_b_ckv"]
        ckr = din["cache_b_krope"]

        def prep(idx):
            b = idx % 2
            i = self.rot("ld", 2)
            ld = self.ld[i]
            lr = self.ld_r[i]
            self.dma(ld[:, 0:256], AP(ck, idx * 128 * 256, [[256, 128], [1, 256]]), lr, writes=[lr])
            b3 = idx % 3
            self.dma(self.krt[b3][:], AP(ckr, idx * 128 * 64, [[64, 128], [1, 64]]), self.krt_r[b3], writes=[self.krt_r[b3]])
            tpb, tpr = self.nb("tp")
            tpv = tpb.rearrange("p (a b) -> p a b", a=4)
            for c in range(2):
                self.op("pe", "transpose", reads=[lr, self.ident_r], writes=[tpr], out=tpv[:, c, :],
                        in_=ld[:, c * 128:(c + 1) * 128], identity=self.ident[:])
            self.op("act", "activation", reads=[tpr], writes=[self.ckvt_r[b]], out=self.ckvt[b][:], in_=tpv[:, 0:2, :],
                    func=AF.Copy)
            self.op("act", "activation", reads=[self.krt_r[b3]], writes=[self.rp_r, self.krss_c_r], out=self.rp[:, 0, :],
                    in_=self.krt[b3][:], func=AF.Square, accum_out=self.krss_c[:, idx:idx + 1])
            self.op("dve", "tensor_scalar", reads=[self.krss_c_r], writes=[self.krss_c_r], out=self.krss_c[:, idx:idx + 1],
                    in0=self.krss_c[:, idx:idx + 1], scalar1=1.0 / 192, scalar2=EPS, op0=ALU.mult, op1=ALU.add)

        allt = tiles + cache_tiles
        for ti, (key, rows, rd) in enumerate(allt):
            kind_, idx = key
            if kind_ == "c" and idx == 0:
                prep(0)
            if ti + 1 < len(allt) and allt[ti + 1][0][0] == "c" and allt[ti + 1][0][1] > 0:
                prep(allt[ti + 1][0][1])
            for bi in range(6):
                acc, accr = self.nb("acc")
                for k in range(2):
                    self.op("pe", "matmul", reads=[ar] + rd, writes=[accr], out=acc[0:rows, 0:512], lhsT=actkv(key, k),
                            rhs=wv[:, k, bi * 512:(bi + 1) * 512], start=(k == 0), stop=(k == 1))
                self.step()
                epkv(key, rows, bi, acc[0:rows, 0:512], accr)
        self.flush()

    def mem_heads(self, L):
        sc = 128 ** -0.5
        for hm in range(4):
            ai = self.rot("arena", 2)
            ar = self.arena_r[ai]
            ab, gv = self.load_head(ai, (self.xqT_s[0], hm * 128 * NCOL, self.xqT_s[1]), (12 + hm) * 128, None, None,
                                    with_kv=False)
            q = ab[:, self.A_Q:self.A_Q + NCOL]
            for t in range(16):
                kts = [dict(qk=[(self.mkT[:, hm, jt * 128:(jt + 1) * 128], q[:, t * 128:(t + 1) * 128])], nk=128,
                            v=self.mv[:, jt, hm, :], bias=None, reads=[ar, self.mkT_r, self.mv_r]) for jt in range(2)]
                self.attn(kts, 128, sc, None, gv[:, t, :], [ar], self.actT[:, 12 + hm, t * 128:(t + 1) * 128], self.act_r[t])
            qs = q[:, 2048:2064]
            kts = [dict(qk=[(self.mkTs[:, hm, jt * 128:(jt + 1) * 128], qs)], nk=128, v=self.mvs[:, jt, hm, :], bias=None,
                        reads=[ar, self.mkTs_r, self.mvs_r]) for jt in range(2)]
            self.attn(kts, 16, sc, None, gv[0:16, 16, :], [ar], self.actT[:, 12 + hm, 2048:2064], self.act_r[16])
            self.flush()

    def out_phase(self, L):
        din = self.din
        last = (L == self.n_layers - 1)
        xnew, xnew_r = self.xres[L % 2]
        srcs = self._xsrc
        if L > 0:
            xold_r = [self.xres[(L - 1) % 2][1]]
        else:
            xold_r = []

        pend = {}
        order = [(tt, bi) for bi in range(4) for tt in range(TT)]

        def issue(n):
            tt, bi = order[n]
            rows = rows_of(tt)
            i = self.rot("ld", 2)
            ld = self.ld[i]
            lr = self.ld_r[i]
            s = srcs[tt][0]
            src = AP(s.tensor, s.offset + bi * 512, [[D, rows], [1, 512]])
            self.dma(ld[0:rows, :], src, lr, reads=xold_r, writes=[lr])
            pend[(tt, bi)] = (ld, lr)

        issue(0)

        def ep(tt, rows, bi, acc, accr):
            n = order.index((tt, bi))
            if n + 1 < len(order):
                issue(n + 1)
            ld, lr = pend.pop((tt, bi))
            j = self.rot("of", 2)
            of = self.of[j]
            orr = self.of_r[j]
            self.op("dve", "tensor_tensor", reads=[accr, lr], writes=[orr], out=of[0:rows, :], in0=acc, in1=ld[0:rows, :],
                    op=ALU.add)
            if last:
                if tt < 16:
                    dst = AP(self.dout["y_prompt"], tt * 128 * D + bi * 512, [[D, 128], [1, 512]])
                else:
                    dst = AP(self.dout["y_sample"], bi * 512, [[D, 16], [1, 512]])
                self.dma(dst, of[0:rows, :], orr, reads=[orr], is_output=True)
            else:
                dst = AP(xnew, tt * 128 * D + bi * 512, [[D, rows], [1, 512]])
                self.dma(dst, of[0:rows, :], orr, reads=[orr], adds=[xnew_r])

        self.proj((din["w_out"], L * 2048 * 2048), 2048, 16, [(c * 512, 512) for c in range(4)], self.x_tiles(),
                  self.act_ap, ep)


def _consts():
    ident = np.eye(128, dtype=np.float32)
    rel = np.arange(384) - 128
    half, max_exact = 16, 8
    ret = np.where(rel < 0, half, 0)
    n = np.abs(rel)
    nf = np.maximum(n, 1).astype(np.float32)
    large = max_exact + (np.log(nf / max_exact) / np.float32(np.log(128 / max_exact)) * (half - max_exact)).astype(np.int32)
    large = np.minimum(large, half - 1)
    bucket = ret + np.where(n < max_exact, n, large)
    oh = np.zeros((32, 384), np.float32)
    oh[bucket, np.arange(384)] = 1.0
    pos = np.zeros((128, 17), np.float32)
    for tt in range(16):
        pos[:, tt] = tt * 128 + np.arange(128)
    pos[:, 16] = 4096 + np.arange(128)
    inv = (np.float32(10000.0) ** (-np.arange(32, dtype=np.float32) / np.float32(32))).astype(np.float32)
    ang = (pos[:, :, None] * inv[None, None, :]).astype(np.float32)
    return ident, oh, np.cos(ang).astype(np.float32), np.sin(ang).astype(np.float32)


_PROG = {}


def kernel(**inputs):
    n = 8
    if "p" not in _PROG:
        _PROG["p"] = Prog()
    prog = _PROG["p"]
    ident, oh, cs, sn = _consts()
    per_batch = {"x_prompt": 0, "x_sample": 0, "mem_prompt": 0, "cache_a_k": 1, "cache_a_v": 1, "cache_b_ckv": 1,
                 "cache_b_krope": 1, "cache_c_k": 1, "cache_c_v": 1, "cache_mem_k": 1, "cache_mem_v": 1}
    shapes = dict(IN_SPECS)
    in_maps = []
    for b in range(n):
        m = {}
        for name, shp in IN_SPECS:
            if name.startswith("k_"):
                continue
            a = np.asarray(inputs[name], dtype=np.float32)
            if name in per_batch:
                a = np.take(a, b, axis=per_batch[name])
                if name in ("cache_b_ckv", "cache_b_krope", "cache_c_k", "cache_c_v"):
                    a = a[0]
            m[name] = np.ascontiguousarray(a).reshape(shp)
        m["k_ident"], m["k_t5oh"], m["k_cos"], m["k_sin"] = ident, oh, cs, sn
        in_maps.append(m)
    res = run_bass_kernel_spmd(prog.nc, in_maps, core_ids=list(range(n)))
    r = res.results

    def st(name, shape_fn):
        return np.stack([shape_fn(r[b][name]) for b in range(n)])

    y_p = st("y_prompt", lambda a: a)
    y_s = st("y_sample", lambda a: a)
    def lay(name, nl, rows, kvh):
        return np.stack([r[b][name].reshape(nl, rows, kvh, 128) for b in range(n)], axis=1)
    outs = (
        y_p, y_s,
        lay("a_k_prompt", 2, 128, 4), lay("a_v_prompt", 2, 128, 4), lay("a_k_sample", 2, 16, 4), lay("a_v_sample", 2, 16, 4),
        st("b_ckv_prompt", lambda a: a)[None], st("b_krope_prompt", lambda a: a)[None],
        st("b_ckv_sample", lambda a: a)[None], st("b_krope_sample", lambda a: a)[None],
        lay("c_k_prompt", 1, 512, 12), lay("c_v_prompt", 1, 512, 12), lay("c_k_sample", 1, 16, 12), lay("c_v_sample", 1, 16, 12),
        lay("mem_k_prompt", 4, 256, 4), lay("mem_v_prompt", 4, 256, 4),
    )
    return tuple(np.ascontiguousarray(o, dtype=np.float32) for o in outs)
```

```python
import numpy as np
import concourse.bass as bass
import concourse.mybir as mybir
from concourse.bass_types import AP
from concourse.bass_utils import run_bass_kernel_spmd

F32 = mybir.dt.float32
BF = mybir.dt.bfloat16
AF = mybir.ActivationFunctionType
ALU = mybir.AluOpType
AX = mybir.AxisListType

D = 2048
TT = 17
NCOL = TT * 128
EPS = 1e-6
NEG = -1e30


class Res:
    __slots__ = ("name", "w", "r", "dsem", "dcnt")

    def __init__(self, name):
        self.name = name
        self.w = []
        self.r = []
        self.dsem = None
        self.dcnt = 0


class Sched:
    ENG = ("pe", "act", "dve", "pool", "sp")

    def __init__(self, nc):
        self.nc = nc
        self.prog = {e: [] for e in self.ENG}
        self.esem = {e: nc.alloc_semaphore("es_" + e) for e in ("pe", "act", "dve", "pool")}
        self.cnt = {e: 0 for e in self.ENG}
        self.known = {e: {} for e in self.ENG}
        self.semobj = {"es_" + e: s for e, s in self.esem.items()}
        self.nd = 0
        self.nwaits = 0
        self.out_events = []

    def _deps(self, eng, reads, writes, adds):
        need = {}
        own = "es_" + eng
        kn = self.known[eng]

        def add(ev, raw):
            k, v, clk = ev
            if k == own and (eng == "pe" or not raw):
                return
            if kn.get(k, 0) >= v:
                return
            if need.get(k, (0, None))[0] < v:
                need[k] = (v, clk)

        for res in reads:
            for ev in res.w:
                add(ev, True)
        for res in writes:
            for ev in res.w:
                add(ev, False)
            for ev in res.r:
                add(ev, False)
        for res in adds:
            for ev in res.r:
                add(ev, False)
        waits = []
        for k, (v, clk) in sorted(need.items(), key=lambda kv: -len(kv[1][1])):
            if kn.get(k, 0) >= v:
                continue
            waits.append((self.semobj[k], v))
            for kk, vv in clk.items():
                if kn.get(kk, 0) < vv:
                    kn[kk] = vv
            if kn.get(k, 0) < v:
                kn[k] = v
        self.nwaits += len(waits)
        return waits

    def _mark(self, ev, reads, writes, adds):
        k = ev[0]
        for res in writes:
            res.w = [ev]
            res.r = []
        for res in adds:
            res.w = [e for e in res.w if e[0] != k]
            res.w.append(ev)
        for res in reads:
            res.r = [e for e in res.r if e[0] != k]
            res.r.append(ev)

    def op(self, eng, name, reads=(), writes=(), adds=(), **kw):
        waits = self._deps(eng, reads, writes, adds)
        self.cnt[eng] += 1
        n = self.cnt[eng]
        sem = self.esem[eng]

        def run(e, name=name, kw=kw, waits=waits, sem=sem):
            for s, v in waits:
                e.wait_ge(s, v)
            getattr(e, name)(**kw).then_inc(sem, 1)

        self.prog[eng].append(run)
        clk = dict(self.known[eng])
        clk["es_" + eng] = n
        ev = ("es_" + eng, n, clk)
        self._mark(ev, reads, writes, adds)
        return ev

    def dma(self, q, out, in_, sres, reads=(), writes=(), adds=(), is_output=False):
        waits = self._deps(q, reads, writes, adds)
        if sres.dsem is None:
            sres.dsem = {}
            sres.dcnt = {}
        if q not in sres.dsem:
            self.nd += 1
            key = "ds%d" % self.nd
            sres.dsem[q] = key
            sres.dcnt[q] = 0
            self.semobj[key] = self.nc.alloc_semaphore(key)
        sres.dcnt[q] += 16
        dkey = sres.dsem[q]
        dval = sres.dcnt[q]
        sem = self.semobj[dkey]

        def run(e, waits=waits, sem=sem, out=out, in_=in_):
            for s, v in waits:
                e.wait_ge(s, v)
            e.dma_start(out=out, in_=in_).then_inc(sem, 16)

        self.prog[q].append(run)
        ev = (dkey, dval, dict(self.known[q]))
        self._mark(ev, reads, writes, adds)
        self.out_events.append(ev)
        return ev

    def emit(self):
        nc = self.nc
        last = {}
        for k, v, clk in self.out_events:
            last[k] = max(last.get(k, 0), v)
        for e in ("pe", "act", "dve", "pool"):
            if self.cnt[e]:
                last["es_" + e] = self.cnt[e]
        fwaits = [(self.semobj[k], v) for k, v in last.items()]
        prog = self.prog
        with nc.Block() as block:
            @block.tensor
            def _(e):
                for f in prog["pe"]:
                    f(e)

            @block.scalar
            def _(e):
                for f in prog["act"]:
                    f(e)

            @block.vector
            def _(e):
                for f in prog["dve"]:
                    f(e)

            @block.gpsimd
            def _(e):
                for f in prog["pool"]:
                    f(e)

            @block.sync
            def _(e):
                for f in prog["sp"]:
                    f(e)
                for s, v in fwaits:
                    e.wait_ge(s, v)


def rows_of(tt):
    return 128 if tt < 16 else 16


IN_SPECS = [
    ("x_prompt", [2048, 2048]), ("x_sample", [16, 2048]), ("mem_prompt", [256, 2048]),
    ("cache_a_k", [2, 128, 512]), ("cache_a_v", [2, 128, 512]),
    ("cache_b_ckv", [4096, 256]), ("cache_b_krope", [4096, 64]),
    ("cache_c_k", [512, 1536]), ("cache_c_v", [512, 1536]),
    ("cache_mem_k", [4, 256, 512]), ("cache_mem_v", [4, 256, 512]),
    ("t5_bias", [32, 12]), ("norm_g", [4, 2048]), ("w_out", [4, 2048, 2048]),
    ("mem_norm_g", [4, 2048]), ("w_mem_kv", [4, 2048, 1024]),
    ("xq_norm_g", [4, 128]), ("xk_norm_g", [4, 128]),
    ("a_w_in", [2, 2048, 5120]), ("a_q_norm_g", [2, 128]), ("a_k_norm_g", [2, 128]),
    ("a_sink", [2, 12]),
    ("b_w_in", [1, 2048, 3392]), ("b_cq_norm_g", [1, 512]), ("b_w_q_b", [1, 512, 2304]),
    ("b_ckv_norm_g", [1, 256]), ("b_w_kv_b", [1, 256, 3072]),
    ("b_q_norm_g", [1, 192]), ("b_k_norm_g", [1, 192]),
    ("c_w_in", [1, 2048, 7168]), ("c_q_norm_g", [1, 128]), ("c_k_norm_g", [1, 128]),
    ("c_rel_bias", [1, 12, 257]),
    ("k_ident", [128, 128]), ("k_t5oh", [32, 384]), ("k_cos", [128, 17, 32]), ("k_sin", [128, 17, 32]),
]
OUT_SPECS = [
    ("y_prompt", [2048, 2048]), ("y_sample", [16, 2048]),
    ("a_k_prompt", [2, 128, 512]), ("a_v_prompt", [2, 128, 512]),
    ("a_k_sample", [2, 16, 512]), ("a_v_sample", [2, 16, 512]),
    ("b_ckv_prompt", [2048, 256]), ("b_krope_prompt", [2048, 64]),
    ("b_ckv_sample", [16, 256]), ("b_krope_sample", [16, 64]),
    ("c_k_prompt", [512, 1536]), ("c_v_prompt", [512, 1536]),
    ("c_k_sample", [16, 1536]), ("c_v_sample", [16, 1536]),
    ("mem_k_prompt", [4, 256, 512]), ("mem_v_prompt", [4, 256, 512]),
]


class Prog:
    def __init__(self, n_layers=4, stop=None):
        self.n_layers = n_layers
        self.stop = stop
        self.stopped = False
        nc = self.nc = bass.Bass("TRN2", target_bir_lowering=False)
        self.S = Sched(nc)
        self.din = {n: nc.dram_tensor(n, s, F32, kind="ExternalInput") for n, s in IN_SPECS}
        self.dout = {n: nc.dram_tensor(n, s, F32, kind="ExternalOutput") for n, s in OUT_SPECS}
        self.dres = {}
        self._rr = {}
        self.deferred = []
        self.lag = 1
        self.alloc()
        self.prologue()
        for L in range(n_layers):
            if not self.stopped:
                self.layer(L)
        self.S.emit()

    def chk(self, L, ph):
        if self.stop is not None and self.stop == (L, ph):
            self.stopped = True
        return self.stopped

    def rres(self, name):
        if name not in self.dres:
            self.dres[name] = Res(name)
        return self.dres[name]

    def rot(self, key, n):
        i = self._rr.get(key, 0)
        self._rr[key] = i + 1
        return i % n

    def sb(self, name, shape, dt):
        t = self.nc.alloc_sbuf_tensor(name, shape, dt)
        return t, self.rres("sb_" + name)

    def scr(self, name, shape, dt=BF):
        t = self.nc.dram_tensor(name, shape, dt)
        return t, self.rres("dr_" + name)

    def op(self, eng, name, reads=(), writes=(), adds=(), **kw):
        ex = [r for r in reads if r in self.ps_set]
        if ex:
            reads = [r for r in reads if r not in self.ps_set]
            writes = list(writes) + ex
        return self.S.op(eng, name, reads, writes, adds, **kw)

    def dma(self, out, in_, sres, reads=(), writes=(), adds=(), q="sp", is_output=False):
        return self.S.dma(q, out, in_, sres, reads, writes, adds, is_output)

    def newstat(self):
        i = self.rot("stat", 12)
        return self.stat[i], self.stat_r[i]

    def bank(self, b):
        return self.ps[:, b * 512:(b + 1) * 512]

    BANKS = {"acc": [0, 1, 4, 5, 6, 7], "tp": [2, 3], "s": [0, 1, 4, 5], "o": [6, 7]}

    def nb(self, kind):
        lst = self.BANKS[kind]
        b = lst[self.rot("bank_" + kind, len(lst))]
        return self.bank(b), self.ps_r[b]

    def defer(self, fn, delay=1):
        self._seq = getattr(self, "_seq", 0) + 1
        self.deferred.append([delay, self._seq, fn])

    def step(self):
        due = []
        rest = []
        for it in self.deferred:
            it[0] -= 1
            (due if it[0] <= 0 else rest).append(it)
        self.deferred = rest
        for it in sorted(due, key=lambda t: t[1]):
            it[2]()

    def flush(self):
        while self.deferred:
            self.step()

    def alloc(self):
        nc = self.nc
        self.ps = nc.alloc_psum_tensor("ps", [128, 4096], F32)
        self.ps_r = [Res("bank%d" % i) for i in range(8)]
        self.ps_set = set(self.ps_r)
        self.actT, _ = self.sb("actT", [128, 16, NCOL], BF)
        self.act_r = [Res("act%d" % t) for t in range(TT)]
        self.arena = []
        self.arena_r = []
        for i in range(2):
            t, r = self.sb("arena%d" % i, [128, 6528], F32)
            self.arena.append(t)
            self.arena_r.append(r)
        self.xta, self.xta_r = self.sb("xta", [128, 2 * 2176], F32)
        self.xt_r = [Res("xt0"), Res("xt1")]
        self.cq_r = self.xta_r
        self.Rt, self.R_r = self.sb("Rt", [128, 2176], F32)
        self.Rh_r = [Res("Rh0"), Res("Rh1")]
        self.ckvT, self.ckvT_r = self.sb("ckvT", [128, 2, NCOL], BF)
        self.kr_all, self.kr_all_r = self.sb("kr_all", [128, TT, 64], F32)
        self.krss, self.krss_r = self.sb("krss", [128, TT], F32)
        self.krss_c, self.krss_c_r = self.sb("krss_c", [128, 40], F32)
        self.gbm_r = Res("gbm")
        self.zf = []
        self.zf_r = []
        self.of = []
        self.of_r = []
        self.tb = []
        self.tb_r = []
        self.vb = []
        self.vb_r = []
        self.gb = []
        self.gb_r = []
        self.ld = []
        self.ld_r = []
        self.pexp = []
        self.pexp_r = []
        self.ogf = []
        self.ogf_r = []
        self.ckvt = []
        self.ckvt_r = []
        self.krt = []
        self.krt_r = []
        for i in range(2):
            for lst, rl, nm, shp, dt in (
                (self.zf, self.zf_r, "zf", [128, 512], F32), (self.of, self.of_r, "of", [128, 512], F32),
                (self.tb, self.tb_r, "tb", [128, 4, 128], BF), (self.vb, self.vb_r, "vb", [128, 4, 129], BF),
                (self.gb, self.gb_r, "gb", [128, 512], BF), (self.ld, self.ld_r, "ld", [128, 512], F32),
                (self.pexp, self.pexp_r, "pexp", [128, 512], BF), (self.pexp, self.pexp_r, "pexq", [128, 512], BF),
                (self.ogf, self.ogf_r, "ogf", [128, 128], F32), (self.ckvt, self.ckvt_r, "ckvt", [128, 2, 128], BF),
                (self.krt, self.krt_r, "krt", [128, 64], F32),
            ):
                t, r = self.sb("%s%d" % (nm, i), shp, dt)
                lst.append(t)
                rl.append(r)
        for lst, rl, nm, shp, dt in ((self.tb, self.tb_r, "tb", [128, 4, 128], BF), (self.vb, self.vb_r, "vb", [128, 4, 129], BF)):
            t, r = self.sb("%s2" % nm, shp, dt)
            lst.append(t)
            rl.append(r)
        t, r = self.sb("krt2", [128, 64], F32)
        self.krt.append(t)
        self.krt_r.append(r)
        self.sq, self.sq_r = self.sb("sq", [128, 512], F32)
        self.rp, self.rp_r = self.sb("rp", [128, 6, 64], F32)
        self.mkT, self.mkT_r = self.sb("mkT", [128, 4, 256], BF)
        self.mv, self.mv_r = self.sb("mv", [128, 2, 4, 129], BF)
        self.mkTs, self.mkTs_r = self.sb("mkTs", [128, 4, 256], BF)
        self.mvs, self.mvs_r = self.sb("mvs", [128, 2, 4, 129], BF)
        self.kcA, self.kcA_r = self.sb("kcA", [128, 4, 128], BF)
        self.vcA, self.vcA_r = self.sb("vcA", [128, 4, 129], BF)
        self.kcC, self.kcC_r = self.sb("kcC", [128, 512], BF)
        self.vcC, self.vcC_r = self.sb("vcC", [128, 4, 129], BF)
        self.gcolA, self.gcolA_r = self.sb("gcolA", [128, 128], F32)
        self.gcolB, self.gcolB_r = self.sb("gcolB", [128, 32], F32)
        self.gbc, self.gbc_r = self.sb("gbc", [128, 384], F32)
        self.esink, self.esink_r = self.sb("esink", [128, 24], F32)
        self.ident, self.ident_r = self.sb("ident", [128, 128], F32)
        self.cosT, self.cos_r = self.sb("cosT", [128, TT, 32], F32)
        self.sinT, self.sin_r = self.sb("sinT", [128, TT, 32], F32)
        self.cm05, self.cm05_r = self.sb("cm05", [128, 4], F32)
        self.stat = []
        self.stat_r = []
        for i in range(12):
            t, r = self.sb("stat%d" % i, [128, 16], F32)
            self.stat.append(t)
            self.stat_r.append(r)
        self.xres = [self.scr("xres%d" % i, [2064, 2048], F32) for i in range(2)]
        self.qT_s = self.scr("qT_s", [12, 128, NCOL])
        self.q192_s = self.scr("q192_s", [12, 192, NCOL])
        self.k192_s = self.scr("k192_s", [12, 192, NCOL])
        self.k192s_s = self.scr("k192s_s", [12, 192, 4224])
        self.xqT_s = self.scr("xqT_s", [4, 128, NCOL])
        self.kT_s = self.scr("kT_s", [12, 128, NCOL])
        self.v_s = self.scr("v_s", [12, TT, 128, 129])
        self.gate_s = self.scr("gate_s", [NCOL, 2048])
        self.vs_s = self.scr("vs_s", [12, 33, 128, 129])
        self.FrepA = self.scr("FrepA", [12, 128, 384], F32)
        self.FrepC = self.scr("FrepC", [12, 128, 768], F32)

    def act_ap(self, tt, k):
        return self.actT[:, k, tt * 128: tt * 128 + rows_of(tt)]

    def xt(self, i):
        return self.xta[:, i * 2176: i * 2176 + 2048]

    def cqT(self):
        return self.xta[:, :].bitcast(BF).rearrange("p (k c) -> p k c", k=4)

    def memT(self):
        return self.Rt[:, 0:2048].bitcast(BF).rearrange("p (k c) -> p k c", k=16)

    def Rbf(self):
        return self.Rt[:, :].bitcast(BF)

    def arena_bf(self, i):
        return self.arena[i][:, :].bitcast(BF)

    def wview(self, i, nk, w):
        return self.arena_bf(i)[:, 0:nk * w].rearrange("p (k w) -> p k w", k=nk)

    def prologue(self):
        din = self.din
        self.dma(self.ident[:], din["k_ident"].ap(), self.ident_r, writes=[self.ident_r])
        self.dma(self.cosT[:], din["k_cos"].ap(), self.cos_r, writes=[self.cos_r])
        self.dma(self.sinT[:], din["k_sin"].ap(), self.sin_r, writes=[self.sin_r])
        self.op("pool", "memset", writes=[self.cm05_r], ap=self.cm05[:], constant=-0.5)
        for i in range(len(self.vb)):
            self.op("pool", "memset", writes=[self.vb_r[i]], ap=self.vb[i][:], constant=1.0)
        for t, r in ((self.mv, self.mv_r), (self.mvs, self.mvs_r)):
            self.op("pool", "memset", writes=[r], ap=t[:], constant=1.0)
        for t, r in ((self.vcA, self.vcA_r), (self.vcC, self.vcC_r)):
            self.op("pool", "memset", writes=[r], ap=t[:], constant=1.0)
        ga = self.ld[0]
        gar = self.ld_r[0]
        self.dma(ga[0:64, 0:128], din["norm_g"].ap().rearrange("l (k c) -> (l k) c", c=128), gar, adds=[gar])
        self.dma(ga[64:128, 0:128], din["mem_norm_g"].ap().rearrange("l (k c) -> (l k) c", c=128), gar, adds=[gar])
        gb_ = self.ld[1]
        gbr = self.ld_r[1]
        self.op("dve", "memset", writes=[gbr, self.gbm_r], ap=gb_[0:32, 0:128], constant=0.0)
        rows = [("xq_norm_g", 0, 4, None), ("xk_norm_g", 4, 4, None), ("a_q_norm_g", 8, 2, None),
                ("a_k_norm_g", 10, 2, None), ("c_q_norm_g", 12, 1, None), ("c_k_norm_g", 13, 1, None)]
        for nm, r0, n, _ in rows:
            self.dma(gb_[r0:r0 + n, 0:128], din[nm].ap(), gbr, reads=[self.gbm_r], adds=[gbr])
        self.dma(gb_[14:18, 0:128], din["b_cq_norm_g"].ap().rearrange("o (k c) -> (o k) c", c=128), gbr, reads=[self.gbm_r], adds=[gbr])
        self.dma(gb_[18:20, 0:128], din["b_ckv_norm_g"].ap().rearrange("o (k c) -> (o k) c", c=128), gbr, reads=[self.gbm_r], adds=[gbr])
        self.dma(gb_[20:21, 0:128], din["b_q_norm_g"].ap()[:, 0:128], gbr, reads=[self.gbm_r], adds=[gbr])
        self.dma(gb_[21:22, 0:64], din["b_q_norm_g"].ap()[:, 128:192], gbr, reads=[self.gbm_r], adds=[gbr])
        self.dma(gb_[22:23, 0:128], din["b_k_norm_g"].ap()[:, 0:128], gbr, reads=[self.gbm_r], adds=[gbr])
        self.dma(gb_[23:24, 0:64], din["b_k_norm_g"].ap()[:, 128:192], gbr, reads=[self.gbm_r], adds=[gbr])
        tpb, tpr = self.nb("tp")
        self.op("pe", "transpose", reads=[gar, self.ident_r], writes=[tpr], out=tpb[:, 0:128], in_=ga[:, 0:128],
                identity=self.ident[:])
        self.op("act", "activation", reads=[tpr], writes=[self.gcolA_r], out=self.gcolA[:], in_=tpb[:, 0:128], func=AF.Copy)
        tpb, tpr = self.nb("tp")
        self.op("pe", "transpose", reads=[gbr, self.ident_r], writes=[tpr], out=tpb[:, 0:32], in_=gb_[0:32, 0:128],
                identity=self.ident[0:32, 0:32])
        self.op("act", "activation", reads=[tpr], writes=[self.gcolB_r], out=self.gcolB[:], in_=tpb[:, 0:32], func=AF.Copy)
        self.dma(self.esink[:], din["a_sink"].ap().rearrange("a h -> (a h)").partition_broadcast(128), self.esink_r,
                 writes=[self.esink_r])
        self.op("act", "activation", reads=[self.esink_r], writes=[self.esink_r], out=self.esink[:], in_=self.esink[:],
                func=AF.Exp)
        t5 = self.zf[0]
        t5r = self.zf_r[0]
        oh = self.of[0]
        ohr = self.of_r[0]
        self.dma(t5[0:32, 0:12], din["t5_bias"].ap(), t5r, writes=[t5r])
        self.dma(oh[0:32, 0:384], din["k_t5oh"].ap(), ohr, writes=[ohr])
        ab, ar = self.nb("acc")
        self.op("pe", "matmul", reads=[t5r, ohr], writes=[ar], out=ab[0:12, 0:384], lhsT=t5[0:32, 0:12],
                rhs=oh[0:32, 0:384], start=True, stop=True)
        fa = self.zf[1]
        far = self.zf_r[1]
        self.op("dve", "tensor_copy", reads=[ar], writes=[far], out=fa[0:12, 0:384], in_=ab[0:12, 0:384])
        FA, FAr = self.FrepA
        self.dma(FA.ap(), fa[0:12, 0:384].unsqueeze(1).broadcast_to([12, 128, 384]), far, reads=[far], writes=[FAr])
        fc = self.xta
        fcr = self.xt_r[0]
        self.dma(fc[0:12, 0:257], din["c_rel_bias"].ap()[0], fcr, writes=[fcr])
        self.op("dve", "tensor_copy", reads=[fcr], writes=[fcr], out=fc[0:12, 257:768],
                in_=fc[0:12, 256:257].broadcast_to([12, 511]))
        FC, FCr = self.FrepC
        self.dma(FC.ap(), fc[0:12, 0:768].unsqueeze(1).broadcast_to([12, 128, 768]), fcr, reads=[fcr], writes=[FCr])

    def norm_transpose(self, srcs, gcol0, dst_fn, dres_fn):
        info = {}

        def stage_a1(idx):
            src, rows, key = srcs[idx]
            i = self.rot("xt", 2)
            xt = self.xt(i)
            xr = self.xt_r[i]
            self.dma(xt[0:rows, :], src, xr, writes=[xr, self.xta_r])
            st, sr = self.newstat()
            for c in range(4):
                self.op("act", "activation", reads=[xr], writes=[self.ps_r[4 + c], sr], out=self.bank(4 + c)[0:rows, :],
                        in_=xt[0:rows, c * 512:(c + 1) * 512], func=AF.Square, accum_out=st[0:rows, 4 + c:5 + c])
            info[idx] = (xt, xr, st, sr)

        def stage_a2(idx):
            src, rows, key = srcs[idx]
            xt, xr, st, sr = info[idx]
            self.op("dve", "tensor_reduce", reads=[sr], writes=[sr], out=st[0:rows, 0:1], in_=st[0:rows, 4:8], axis=AX.X,
                    op=ALU.add)
            self.op("dve", "tensor_scalar", reads=[sr], writes=[sr], out=st[0:rows, 1:2], in0=st[0:rows, 0:1],
                    scalar1=1.0 / D, scalar2=EPS, op0=ALU.mult, op1=ALU.add)
            self.op("pool", "tensor_tensor", reads=[sr, self.cm05_r], writes=[sr], out=st[0:rows, 2:3],
                    in0=st[0:rows, 1:2], in1=self.cm05[0:rows, 0:1], op=ALU.pow)

        def stage_b(idx):
            src, rows, key = srcs[idx]
            xt, xr, st, sr = info.pop(idx)
            self.op("dve", "tensor_scalar", reads=[sr, xr], writes=[xr], out=xt[0:rows, :], in0=xt[0:rows, :],
                    scalar1=st[0:rows, 2:3], scalar2=None, op0=ALU.mult)
            for j in range(4):
                tpb, tpr = self.nb("tp")
                tpv = tpb.rearrange("p (a b) -> p a b", a=4)
                for c in range(4):
                    k = 4 * j + c
                    self.op("pe", "transpose", reads=[xr, self.ident_r, self.xta_r], writes=[tpr], out=tpv[:, c, 0:rows],
                            in_=xt[0:rows, k * 128:(k + 1) * 128], identity=self.ident[0:rows, 0:rows])
                if j == 3:
                    for c in range(4):
                        k = 4 * j + c
                        self.op("act", "activation", reads=[tpr, self.gcolA_r], writes=[dres_fn(key)],
                                out=dst_fn(key, j)[:, c, :], in_=tpv[:, c, 0:rows], func=AF.Copy,
                                scale=self.gcolA[:, gcol0 + k: gcol0 + k + 1])
                else:
                    g = self.gcolA[:, gcol0 + 4 * j: gcol0 + 4 * j + 4].unsqueeze(2).broadcast_to([128, 4, rows])
                    self.op("dve", "tensor_tensor", reads=[tpr, self.gcolA_r], writes=[dres_fn(key)], out=dst_fn(key, j),
                            in0=tpv[:, 0:4, 0:rows], in1=g, op=ALU.mult)

        n = len(srcs)
        stage_a1(0)
        stage_a2(0)
        for idx in range(n):
            if idx + 1 < n:
                stage_a1(idx + 1)
            stage_b(idx)
            if idx + 1 < n:
                stage_a2(idx + 1)

    def proj(self, W, wcols, nk, blocks, tiles, act_fn, epilogue, tile_outer=False):
        wh, woff = W
        if tile_outer:
            ai = self.rot("arena", 2)
            ar = self.arena_r[ai]
            tot = sum(w for _, w in blocks)
            wv = self.arena_bf(ai)[:, 0:nk * tot].rearrange("p (k w) -> p k w", k=nk)
            pos = 0
            wpos = []
            for (c0, w) in blocks:
                src = AP(wh, woff + c0, [[wcols, 128], [128 * wcols, nk], [1, w]])
                self.dma(wv[:, :, pos:pos + w], src, ar, adds=[ar], q="pool")
                wpos.append(pos)
                pos += w
            for (key, rows, rd) in tiles:
                for bi, (c0, w) in enumerate(blocks):
                    acc, accr = self.nb("acc")
                    for k in range(nk):
                        self.op("pe", "matmul", reads=[ar] + rd, writes=[accr], out=acc[0:rows, 0:w],
                                lhsT=act_fn(key, k), rhs=wv[:, k, wpos[bi]:wpos[bi] + w], start=(k == 0), stop=(k == nk - 1))
                    self.step()
                    epilogue(key, rows, bi, acc[0:rows, 0:w], accr)
            self.flush()
            return
        def issue_w(bi):
            c0, w = blocks[bi]
            ai = self.rot("arena", 2)
            ar = self.arena_r[ai]
            wv = self.wview(ai, nk, w)
            src = AP(wh, woff + c0, [[wcols, 128], [128 * wcols, nk], [1, w]])
            self.dma(wv, src, ar, writes=[ar], q="pool")
            return ar, wv

        cur = issue_w(0)
        for bi, (c0, w) in enumerate(blocks):
            nxt = issue_w(bi + 1) if bi + 1 < len(blocks) else None
            ar, wv = cur
            cur = nxt
            for (key, rows, rd) in tiles:
                acc, accr = self.nb("acc")
                for k in range(nk):
                    self.op("pe", "matmul", reads=[ar] + rd, writes=[accr], out=acc[0:rows, 0:w],
                            lhsT=act_fn(key, k), rhs=wv[:, k, 0:w], start=(k == 0), stop=(k == nk - 1))
                self.step()
                epilogue(key, rows, bi, acc[0:rows, 0:w], accr)
        self.flush()

    def headnorm(self, acc, accr, rows, nh, hd, extra_ss=None):
        w = nh * hd
        self.op("act", "activation", reads=[accr], writes=[self.sq_r], out=self.sq[0:rows, 0:w], in_=acc, func=AF.Square)
        st, sr = self.newstat()
        self.op("dve", "tensor_reduce", reads=[self.sq_r], writes=[sr], out=st[0:rows, 0:nh],
                in_=self.sq[0:rows, 0:w].rearrange("p (a b) -> p a b", a=nh), axis=AX.X, op=ALU.add)
        self.op("dve", "tensor_scalar", reads=[sr], writes=[sr], out=st[0:rows, 4:4 + nh], in0=st[0:rows, 0:nh],
                scalar1=1.0 / hd, scalar2=EPS, op0=ALU.mult, op1=ALU.add)
        self.op("pool", "tensor_tensor", reads=[sr, self.cm05_r], writes=[sr], out=st[0:rows, 8:8 + nh],
                in0=st[0:rows, 4:4 + nh], in1=self.cm05[0:rows, 0:nh], op=ALU.pow)
        i = self.rot("zf", 2)
        zf = self.zf[i]
        zr = self.zf_r[i]
        zv = zf[0:rows, 0:w].rearrange("p (a b) -> p a b", a=nh)
        self.defer(lambda: self.op("dve", "tensor_tensor", reads=[accr, sr], writes=[zr], out=zv,
                                   in0=acc.rearrange("p (a b) -> p a b", a=nh),
                                   in1=st[0:rows, 8:8 + nh].unsqueeze(2).broadcast_to([rows, nh, hd]), op=ALU.mult), 1)
        return zv, zr

    def transpose_out(self, zv, zr, rows, nh, gcol, dst_sb=None, dst_res=None, dst_dram=None, dram_res=None):
        self.defer(lambda: self._transpose_out(zv, zr, rows, nh, gcol, dst_sb, dst_res, dst_dram, dram_res), 3)

    def _transpose_out(self, zv, zr, rows, nh, gcol, dst_sb, dst_res, dst_dram, dram_res):
        tpb, tpr = self.nb("tp")
        tpv = tpb.rearrange("p (a b) -> p a b", a=4)
        for c in range(nh):
            self.op("pe", "transpose", reads=[zr, self.ident_r], writes=[tpr], out=tpv[:, c, 0:rows], in_=zv[:, c, :],
                    identity=self.ident[0:rows, 0:rows])
        if dst_sb is not None:
            self.op("act", "activation", reads=[tpr, self.gcolB_r], writes=[dst_res], out=dst_sb, in_=tpv[:, 0:nh, 0:rows],
                    func=AF.Copy, scale=gcol)
            return
        i = self.rot("tb", 3)
        tb = self.tb[i]
        tr = self.tb_r[i]
        self.op("act", "activation", reads=[tpr, self.gcolB_r], writes=[tr], out=tb[:, 0:nh, 0:rows],
                in_=tpv[:, 0:nh, 0:rows], func=AF.Copy, scale=gcol)
        self.dma(dst_dram, tb[:, 0:nh, 0:rows], tr, reads=[tr], adds=[dram_res])

    def rows_out(self, src, src_res, rows, w, dst):
        i = self.rot("of", 2)
        of = self.of[i]
        orr = self.of_r[i]
        self.op("dve", "tensor_copy", reads=[src_res], writes=[orr], out=of[0:rows, 0:w], in_=src)
        self.dma(dst, of[0:rows, 0:w], orr, reads=[orr], is_output=True)

    def gain_rows_out(self, zv, zr, rows, nh, g_ap, dst):
        self.defer(lambda: self._gain_rows_out(zv, zr, rows, nh, g_ap, dst), 1)

    def _gain_rows_out(self, zv, zr, rows, nh, g_ap, dst):
        i = self.rot("of", 2)
        of = self.of[i]
        orr = self.of_r[i]
        self.op("dve", "tensor_tensor", reads=[zr, self.gbc_r], writes=[orr],
                out=of[0:rows, 0:nh * 128].rearrange("p (a b) -> p a b", a=nh), in0=zv,
                in1=g_ap.unsqueeze(1).broadcast_to([rows, nh, 128]), op=ALU.mult)
        self.dma(dst, of[0:rows, 0:nh * 128], orr, reads=[orr], is_output=True)

    def v_out(self, accv, accr, rows, nh, dst_dram, dram_res, dst_sb=None, dst_res=None):
        if dst_sb is not None:
            self.op("act", "activation", reads=[accr], writes=[dst_res], out=dst_sb, in_=accv, func=AF.Copy)
            return
        i = self.rot("vb", 3)
        vb = self.vb[i]
        vr = self.vb_r[i]
        self.op("act", "activation", reads=[accr], writes=[vr], out=vb[0:rows, 0:nh, 0:128], in_=accv, func=AF.Copy)
        self.dma(dst_dram, vb[0:rows, 0:nh, :], vr, reads=[vr], adds=[dram_res])

    def gate_out(self, acc, accr, rows, w, tt, gc0):
        i = self.rot("gb", 2)
        gb = self.gb[i]
        gr = self.gb_r[i]
        self.op("act", "activation", reads=[accr], writes=[gr], out=gb[0:rows, 0:w], in_=acc, func=AF.Silu)
        G, Gr = self.gate_s
        self.dma(AP(G, tt * 128 * 2048 + gc0, [[2048, rows], [1, w]]), gb[0:rows, 0:w], gr, reads=[gr], adds=[Gr])

    def qk_store(self, scr, h0, nh, tt, rows, width=NCOL, dpart=128, hrows=None, r0=0):
        t, _ = scr
        hrows = hrows or dpart
        return AP(t, (h0 * hrows + r0) * width + tt * 128, [[width, dpart], [hrows * width, nh], [1, rows]])

    def v_store(self, scr, h0, nh, tile, rows, ntiles=TT):
        t, _ = scr
        return AP(t, (h0 * ntiles + tile) * 128 * 129, [[129, rows], [ntiles * 128 * 129, nh], [1, 129]])

    def attn(self, kts, nq, scale, sink_ap, gate_ap, gate_reads, dst_ap, dst_res):
        chunks = []
        cur = []
        for kt in kts:
            if cur:
                p = cur[-1]
                brk = (len(cur) * nq >= 512 or kt["nk"] != p["nk"] or (kt["bias"] is None) != (p["bias"] is None)
                       or (kt["bias"] is not None and kt["bias"][1] != p["bias"][1] + 1))
                if brk:
                    chunks.append(cur)
                    cur = []
            cur.append(kt)
        chunks.append(cur)
        assert len(chunks) <= 4
        ob, orr = self.nb("o")
        staged = []
        for ci, ch in enumerate(chunks):
            sbk, sr = self.nb("s")
            nk = ch[0]["nk"]
            n = len(ch)
            for i, kt in enumerate(ch):
                m = len(kt["qk"])
                for pi, (l, r) in enumerate(kt["qk"]):
                    self.op("pe", "matmul", reads=kt["reads"], writes=[sr], out=sbk[0:nk, i * nq:(i + 1) * nq], lhsT=l, rhs=r,
                            start=(pi == 0), stop=(pi == m - 1))
            if ci == 0:
                self.step()
            ip = self.rot("pexp", 4)
            pe_t = self.pexp[ip]
            per = self.pexp_r[ip]
            if ch[0]["bias"] is not None:
                j = self.rot("zf", 2)
                tm = self.zf[j]
                tmr = self.zf_r[j]
                self.op("act", "activation", reads=[sr], writes=[tmr], out=tm[0:nk, 0:n * nq], in_=sbk[0:nk, 0:n * nq],
                        func=AF.Exp, scale=scale)
                bv, s0 = ch[0]["bias"]
                self.op("dve", "tensor_tensor", reads=[tmr] + ch[0]["reads"], writes=[per],
                        out=pe_t[0:nk, 0:n * nq].rearrange("p (a b) -> p a b", a=n),
                        in0=tm[0:nk, 0:n * nq].rearrange("p (a b) -> p a b", a=n), in1=bv[0:nk, s0:s0 + n, 0:nq], op=ALU.mult)
            else:
                self.op("act", "activation", reads=[sr], writes=[per], out=pe_t[0:nk, 0:n * nq], in_=sbk[0:nk, 0:n * nq],
                        func=AF.Exp, scale=scale)
            for i, kt in enumerate(ch):
                if kt.get("diag"):
                    self.op("pool", "memset", writes=[per], ap=pe_t[64:128, i * nq:i * nq + 64], constant=0.0)
            staged.append((ch, pe_t, per, nk))
        tot = len(kts)

        def part2():
            idx = 0
            for ch, pe_t, per, nk in staged:
                for i, kt in enumerate(ch):
                    self.op("pe", "matmul", reads=[per] + kt["reads"], writes=[orr], out=ob[0:nq, 0:129],
                            lhsT=pe_t[0:nk, i * nq:(i + 1) * nq], rhs=kt["v"], start=(idx == 0), stop=(idx == tot - 1))
                    idx += 1
            self.attn_post(ob, orr, nq, sink_ap, gate_ap, gate_reads, dst_ap, dst_res, tail_delay=(1 if dfr else 2))

        dfr = len(chunks) <= 2
        if dfr:
            self.defer(part2, 1)
        else:
            part2()

    def attn_post(self, ob, orr, nq, sink_ap, gate_ap, gate_reads, dst_ap, dst_res, tail_delay=2):
        st, sr = self.newstat()
        if sink_ap is not None:
            self.op("dve", "tensor_tensor", reads=[orr, self.esink_r], writes=[sr], out=st[0:nq, 0:1],
                    in0=ob[0:nq, 128:129], in1=sink_ap, op=ALU.add)
            self.op("dve", "reciprocal", reads=[sr], writes=[sr], out=st[0:nq, 1:2], in_=st[0:nq, 0:1])
        else:
            self.op("dve", "reciprocal", reads=[orr], writes=[sr], out=st[0:nq, 1:2], in_=ob[0:nq, 128:129])
        i = self.rot("ogf", 2)
        og = self.ogf[i]
        ogr = self.ogf_r[i]
        self.op("dve", "scalar_tensor_tensor", reads=[orr, sr] + gate_reads, writes=[ogr], out=og[0:nq, :],
                in0=ob[0:nq, 0:128], scalar=st[0:nq, 1:2], in1=gate_ap, op0=ALU.mult, op1=ALU.mult)

        def tail():
            tpb, tpr = self.nb("tp")
            self.op("pe", "transpose", reads=[ogr, self.ident_r], writes=[tpr], out=tpb[:, 0:nq], in_=og[0:nq, :],
                    identity=self.ident[0:nq, 0:nq])
            self.op("act", "activation", reads=[tpr], writes=[dst_res], out=dst_ap, in_=tpb[:, 0:nq], func=AF.Copy)
        self.defer(tail, tail_delay)

    A_Q, A_G, A_K, A_V, A_B = 0, 2176, 4352, 6528, 8736

    def load_head(self, ai, q_src, gate_col, k_src, v_src, kw=NCOL, with_kv=True):
        ar = self.arena_r[ai]
        ab = self.arena_bf(ai)
        qh, qoff, qres = q_src
        self.dma(ab[:, self.A_Q:self.A_Q + 2064], AP(qh, qoff, [[NCOL, 128], [1, 2064]]), ar, reads=[qres], adds=[ar])
        G, Gr = self.gate_s
        gv = ab[:, self.A_G:self.A_G + 2176].rearrange("p (t c) -> p t c", t=TT)
        self.dma(gv[:, 0:16, :], AP(G, gate_col, [[2048, 128], [128 * 2048, 16], [1, 128]]), ar, reads=[Gr], adds=[ar])
        self.dma(gv[0:16, 16, :], AP(G, 2048 * 2048 + gate_col, [[2048, 16], [1, 128]]), ar, reads=[Gr], adds=[ar])
        if with_kv:
            kh, koff, kres = k_src
            self.dma(ab[:, self.A_K:self.A_K + 2064], AP(kh, koff, [[kw, 128], [1, 2064]]), ar, reads=[kres], adds=[ar])
            vh, voff, vres = v_src
            vv = ab[:, self.A_V:self.A_V + TT * 129].rearrange("p (t c) -> p t c", t=TT)
            self.dma(vv[:, 0:16, :], AP(vh, voff, [[129, 128], [128 * 129, 16], [1, 129]]), ar, reads=[vres], adds=[ar])
            self.dma(vv[0:16, 16, :], AP(vh, voff + 16 * 128 * 129, [[129, 16], [1, 129]]), ar, reads=[vres], adds=[ar])
        return ab, gv

    def bias_view(self, ai, n):
        f0 = self.A_B // 2
        return self.arena[ai][:, f0:f0 + n * 128].rearrange("p (a b) -> p a b", a=n)

    def load_bias(self, ai, Frep, L, h, offs_nk_nq):
        ar = self.arena_r[ai]
        bv = self.bias_view(ai, len(offs_nk_nq))
        Ft, Fr = Frep
        for i, (off, nk, nq) in enumerate(offs_nk_nq):
            self.dma(bv[0:nk, i, 0:nq], AP(Ft, h * 128 * L + off, [[L - 1, nk], [1, nq]]), ar, reads=[Fr], adds=[ar])
        return bv

    def layer(self, L):
        kind, j = L % 3, L // 3
        din = self.din
        if kind == 0:
            self.dma(self.gbc[:, 0:128], din["a_k_norm_g"].ap()[j].partition_broadcast(128), self.gbc_r, writes=[self.gbc_r])
        elif kind == 1:
            self.dma(self.gbc[:, 0:256], din["b_ckv_norm_g"].ap()[0].partition_broadcast(128), self.gbc_r, writes=[self.gbc_r])
        else:
            self.dma(self.gbc[:, 0:128], din["c_k_norm_g"].ap()[0].partition_broadcast(128), self.gbc_r, writes=[self.gbc_r])
        self.dma(self.gbc[:, 256:384], din["xk_norm_g"].ap()[L].partition_broadcast(128), self.gbc_r, adds=[self.gbc_r])
        if L == 0:
            xp = din["x_prompt"]
            xs = din["x_sample"]
            srcs = [(AP(xp, t * 128 * D, [[D, 128], [1, D]]), 128, t) for t in range(16)]
            srcs.append((AP(xs, 0, [[D, 16], [1, D]]), 16, 16))
            xrd = []
        else:
            xh, xr_ = self.xres[(L - 1) % 2]
            srcs = [(AP(xh, t * 128 * D, [[D, rows_of(t)], [1, D]]), rows_of(t), t) for t in range(TT)]
            xrd = [xr_]
        self._xsrc = srcs
        self.norm_transpose_x(srcs, xrd, L * 16)
        if self.chk(L, "N"):
            return
        self.mem_phase(L)
        if self.chk(L, "M"):
            return
        if kind == 0:
            self.layer_a(L, j)
        elif kind == 1:
            self.layer_b(L)
        else:
            self.layer_c(L)
        if self.stopped or self.chk(L, "T"):
            return
        self.mem_heads(L)
        if self.chk(L, "X"):
            return
        self.out_phase(L)

    def norm_transpose_x(self, srcs, xrd, gcol0):
        S = self
        orig = S.dma

        def dst(key, jj):
            rows = rows_of(key)
            return S.actT[:, 4 * jj:4 * jj + 4, key * 128: key * 128 + rows]

        if xrd:
            def dma2(out, in_, sres, reads=(), writes=(), adds=(), q="sp", is_output=False):
                return orig(out, in_, sres, reads=list(reads) + xrd, writes=writes, adds=adds, q=q, is_output=is_output)
            S.dma = dma2
        try:
            S.norm_transpose(srcs, gcol0, dst, lambda key: S.act_r[key])
        finally:
            S.dma = orig

    def mem_phase(self, L):
        din = self.din
        mp = din["mem_prompt"]
        srcs = [(AP(mp, t * 128 * D, [[D, 128], [1, D]]), 128, t) for t in range(2)]
        memT = self.memT()
        self.norm_transpose(srcs, 64 + L * 16, lambda key, jj: memT[:, 4 * jj:4 * jj + 4, key * 128:(key + 1) * 128],
                            lambda key: self.R_r)
        if self.chk(L, "M1"):
            return
        mko = self.dout["mem_k_prompt"]
        mvo = self.dout["mem_v_prompt"]

        dbg = "abcde"

        def ep(key, rows, bi, acc, accr):
            if bi == 0:
                if "a" not in dbg:
                    return
                zv, zr = self.headnorm(acc, accr, rows, 4, 128)
                if "b" in dbg:
                    self.transpose_out(zv, zr, rows, 4, self.gcolB[:, 4 + L:5 + L],
                                       dst_sb=self.mkT[:, :, key * 128:(key + 1) * 128], dst_res=self.mkT_r)
                if "c" in dbg:
                    self.gain_rows_out(zv, zr, rows, 4, self.gbc[0:rows, 256:384],
                                       AP(mko, L * 256 * 512 + key * 128 * 512, [[512, rows], [1, 512]]))
            else:
                accv = acc.rearrange("p (a b) -> p a b", a=4)
                if "d" in dbg:
                    self.v_out(accv, accr, rows, 4, None, None, dst_sb=self.mv[:, key, :, 0:128], dst_res=self.mv_r)
                if "e" in dbg:
                    self.rows_out(acc, accr, rows, 512, AP(mvo, L * 256 * 512 + key * 128 * 512, [[512, rows], [1, 512]]))

        self.proj((din["w_mem_kv"], L * 2048 * 1024), 1024, 16, [(0, 512), (512, 512)],
                  [(t, 128, [self.R_r]) for t in range(2)], lambda key, k: memT[:, k, key * 128:(key + 1) * 128], ep)
        if self.chk(L, "M2"):
            return
        ck = din["cache_mem_k"]
        cv = din["cache_mem_v"]
        for t in range(2):
            i = self.rot("ld", 2)
            ld = self.ld[i]
            lr = self.ld_r[i]
            self.dma(ld[:, :], AP(ck, L * 256 * 512 + t * 128 * 512, [[512, 128], [1, 512]]), lr, writes=[lr])
            tpb, tpr = self.nb("tp")
            tpv = tpb.rearrange("p (a b) -> p a b", a=4)
            for c in range(4):
                self.op("pe", "transpose", reads=[lr, self.ident_r], writes=[tpr], out=tpv[:, c, :],
                        in_=ld[:, c * 128:(c + 1) * 128], identity=self.ident[:])
            self.op("act", "activation", reads=[tpr], writes=[self.mkTs_r], out=self.mkTs[:, :, t * 128:(t + 1) * 128],
                    in_=tpv[:, 0:4, :], func=AF.Copy)
            i = self.rot("ld", 2)
            ld = self.ld[i]
            lr = self.ld_r[i]
            self.dma(ld[:, :], AP(cv, L * 256 * 512 + t * 128 * 512, [[512, 128], [1, 512]]), lr, writes=[lr])
            self.op("dve", "tensor_copy", reads=[lr], writes=[self.mvs_r], out=self.mvs[:, t, :, 0:128],
                    in_=ld[:, :].rearrange("p (a b) -> p a b", a=4))

    def x_tiles(self):
        return [(t, rows_of(t), [self.act_r[t]]) for t in range(TT)]

    def layer_a(self, L, j):
        din = self.din
        dout = self.dout
        gq = self.gcolB[:, 8 + j:9 + j]
        gk = self.gcolB[:, 10 + j:11 + j]
        gxq = self.gcolB[:, L:L + 1]
        blocks = [(c * 512, 512) for c in range(10)]

        def ep(tt, rows, bi, acc, accr):
            if bi < 3:
                zv, zr = self.headnorm(acc, accr, rows, 4, 128)
                self.transpose_out(zv, zr, rows, 4, gq, dst_dram=self.qk_store(self.qT_s, bi * 4, 4, tt, rows),
                                   dram_res=self.qT_s[1])
            elif bi == 3:
                zv, zr = self.headnorm(acc, accr, rows, 4, 128)
                self.transpose_out(zv, zr, rows, 4, gk, dst_dram=self.qk_store(self.kT_s, 0, 4, tt, rows),
                                   dram_res=self.kT_s[1])
                if tt == 15:
                    self.gain_rows_out(zv, zr, rows, 4, self.gbc[0:rows, 0:128],
                                       AP(dout["a_k_prompt"], j * 128 * 512, [[512, 128], [1, 512]]))
                elif tt == 16:
                    self.gain_rows_out(zv, zr, rows, 4, self.gbc[0:rows, 0:128],
                                       AP(dout["a_k_sample"], j * 16 * 512, [[512, 16], [1, 512]]))
            elif bi == 4:
                accv = acc.rearrange("p (a b) -> p a b", a=4)
                self.v_out(accv, accr, rows, 4, self.v_store(self.v_s, 0, 4, tt, rows), self.v_s[1])
                if tt == 15:
                    self.rows_out(acc, accr, rows, 512, AP(dout["a_v_prompt"], j * 128 * 512, [[512, 128], [1, 512]]))
                elif tt == 16:
                    self.rows_out(acc, accr, rows, 512, AP(dout["a_v_sample"], j * 16 * 512, [[512, 16], [1, 512]]))
            elif bi == 5:
                zv, zr = self.headnorm(acc, accr, rows, 4, 128)
                self.transpose_out(zv, zr, rows, 4, gxq, dst_dram=self.qk_store(self.xqT_s, 0, 4, tt, rows),
                                   dram_res=self.xqT_s[1])
            else:
                self.gate_out(acc, accr, rows, 512, tt, (bi - 6) * 512)

        self.proj((din["a_w_in"], j * 2048 * 5120), 5120, 16, blocks, self.x_tiles(), self.act_ap, ep)
        if self.chk(L, "P"):
            return
        i = self.rot("ld", 2)
        ld = self.ld[i]
        lr = self.ld_r[i]
        self.dma(ld[:, :], AP(din["cache_a_k"], j * 128 * 512, [[512, 128], [1, 512]]), lr, writes=[lr])
        tpb, tpr = self.nb("tp")
        tpv = tpb.rearrange("p (a b) -> p a b", a=4)
        for c in range(4):
            self.op("pe", "transpose", reads=[lr, self.ident_r], writes=[tpr], out=tpv[:, c, :],
                    in_=ld[:, c * 128:(c + 1) * 128], identity=self.ident[:])
        self.op("act", "activation", reads=[tpr], writes=[self.kcA_r], out=self.kcA[:], in_=tpv[:, 0:4, :], func=AF.Copy)
        i = self.rot("ld", 2)
        ld = self.ld[i]
        lr = self.ld_r[i]
        self.dma(ld[:, :], AP(din["cache_a_v"], j * 128 * 512, [[512, 128], [1, 512]]), lr, writes=[lr])
        self.op("dve", "tensor_copy", reads=[lr], writes=[self.vcA_r], out=self.vcA[:, :, 0:128],
                in_=ld[:, :].rearrange("p (a b) -> p a b", a=4))
        sc = 128 ** -0.5
        for h in range(12):
            kh = h // 3
            ai = self.rot("arena", 2)
            ar = self.arena_r[ai]
            ab, gv = self.load_head(ai, (self.qT_s[0], h * 128 * NCOL, self.qT_s[1]), h * 128,
                                    (self.kT_s[0], kh * 128 * NCOL, self.kT_s[1]),
                                    (self.v_s[0], kh * TT * 128 * 129, self.v_s[1]))
            bv = self.load_bias(ai, self.FrepA, 384, h,
                                [(256, 128, 128), (128, 128, 128), (256, 128, 16), (128, 16, 16)])
            self.op("pool", "memset", writes=[ar], ap=bv[64:128, 1, 0:64], constant=NEG)
            self.op("pool", "memset", writes=[ar], ap=bv[0:64, 0, 64:128], constant=NEG)
            self.op("act", "activation", writes=[ar], out=bv[:, 0:2, :], in_=bv[:, 0:2, :], func=AF.Exp)
            self.op("act", "activation", writes=[ar], out=bv[:, 2, 0:16], in_=bv[:, 2, 0:16], func=AF.Exp)
            self.op("act", "activation", writes=[ar], out=bv[0:16, 3, 0:16], in_=bv[0:16, 3, 0:16], func=AF.Exp)
            q = ab[:, self.A_Q:self.A_Q + NCOL]
            k = ab[:, self.A_K:self.A_K + NCOL]
            vv = ab[:, self.A_V:self.A_V + TT * 129].rearrange("p (t c) -> p t c", t=TT)
            sink = self.esink[:, j * 12 + h: j * 12 + h + 1]
            for t in range(16):
                kts = []
                if t >= 1:
                    kts.append(dict(qk=[(k[:, (t - 1) * 128:t * 128], q[:, t * 128:(t + 1) * 128])], nk=128,
                                    v=vv[:, t - 1, :], bias=(bv, 0), reads=[ar]))
                kts.append(dict(qk=[(k[:, t * 128:(t + 1) * 128], q[:, t * 128:(t + 1) * 128])], nk=128,
                                v=vv[:, t, :], bias=(bv, 1), reads=[ar]))
                self.attn(kts, 128, sc, sink, gv[:, t, :], [ar], self.actT[:, h, t * 128:(t + 1) * 128], self.act_r[t])
            qs = q[:, 2048:2064]
            kts = [dict(qk=[(self.kcA[:, kh, :], qs)], nk=128, v=self.vcA[:, kh, :], bias=(bv, 2),
                        reads=[ar, self.kcA_r, self.vcA_r]),
                   dict(qk=[(k[:, 2048:2064], qs)], nk=16, v=vv[0:16, 16, :], bias=(bv, 3), reads=[ar])]
            self.attn(kts, 16, sc, sink[0:16, :], gv[0:16, 16, :], [ar], self.actT[:, h, 2048:2064], self.act_r[16])
            self.flush()

    def layer_c(self, L):
        din = self.din
        dout = self.dout
        gq = self.gcolB[:, 12:13]
        gk = self.gcolB[:, 13:14]
        gxq = self.gcolB[:, L:L + 1]
        blocks = [(c * 512, 512) for c in range(14)]

        def ep(tt, rows, bi, acc, accr):
            if bi < 3:
                zv, zr = self.headnorm(acc, accr, rows, 4, 128)
                self.transpose_out(zv, zr, rows, 4, gq, dst_dram=self.qk_store(self.qT_s, bi * 4, 4, tt, rows),
                                   dram_res=self.qT_s[1])
            elif bi < 6:
                b = bi - 3
                zv, zr = self.headnorm(acc, accr, rows, 4, 128)
                self.transpose_out(zv, zr, rows, 4, gk, dst_dram=self.qk_store(self.kT_s, b * 4, 4, tt, rows),
                                   dram_res=self.kT_s[1])
                if 12 <= tt < 16:
                    self.gain_rows_out(zv, zr, rows, 4, self.gbc[0:rows, 0:128],
                                       AP(dout["c_k_prompt"], (tt - 12) * 128 * 1536 + b * 512, [[1536, 128], [1, 512]]))
                elif tt == 16:
                    self.gain_rows_out(zv, zr, rows, 4, self.gbc[0:rows, 0:128],
                                       AP(dout["c_k_sample"], b * 512, [[1536, 16], [1, 512]]))
            elif bi < 9:
                b = bi - 6
                accv = acc.rearrange("p (a b) -> p a b", a=4)
                self.v_out(accv, accr, rows, 4, self.v_store(self.v_s, b * 4, 4, tt, rows), self.v_s[1])
                if 12 <= tt < 16:
                    self.rows_out(acc, accr, rows, 512,
                                  AP(dout["c_v_prompt"], (tt - 12) * 128 * 1536 + b * 512, [[1536, 128], [1, 512]]))
                elif tt == 16:
                    self.rows_out(acc, accr, rows, 512, AP(dout["c_v_sample"], b * 512, [[1536, 16], [1, 512]]))
            elif bi == 9:
                zv, zr = self.headnorm(acc, accr, rows, 4, 128)
                self.transpose_out(zv, zr, rows, 4, gxq, dst_dram=self.qk_store(self.xqT_s, 0, 4, tt, rows),
                                   dram_res=self.xqT_s[1])
            else:
                self.gate_out(acc, accr, rows, 512, tt, (bi - 10) * 512)

        self.proj((din["c_w_in"], 0), 7168, 16, blocks, self.x_tiles(), self.act_ap, ep)
        if self.chk(L, "P"):
            return
        sc = 128 ** -0.5
        ck = din["cache_c_k"]
        cv = din["cache_c_v"]
        for h in range(12):
            ai = self.rot("arena", 2)
            ar = self.arena_r[ai]
            ab, gv = self.load_head(ai, (self.qT_s[0], h * 128 * NCOL, self.qT_s[1]), h * 128,
                                    (self.kT_s[0], h * 128 * NCOL, self.kT_s[1]),
                                    (self.v_s[0], h * TT * 128 * 129, self.v_s[1]))
            bv = self.load_bias(ai, self.FrepC, 768, h, [(128 + 128 * (4 - i), 128, 128) for i in range(5)])
            f0 = self.A_B // 2 + 640
            bs = self.arena[ai][:, f0:f0 + 80].rearrange("p (a b) -> p a b", a=5)
            Ft, Fr = self.FrepC
            for i_, (off_, nk_) in enumerate([(640 - 128 * jc, 128) for jc in range(4)] + [(128, 16)]):
                self.dma(bs[0:nk_, i_, 0:16], AP(Ft, h * 128 * 768 + off_, [[767, nk_], [1, 16]]), ar, reads=[Fr], adds=[ar])
            self.op("pool", "memset", writes=[ar], ap=bv[64:128, 4, 0:64], constant=NEG)
            self.op("pool", "memset", writes=[ar], ap=bv[0:64, 0, 64:128], constant=NEG)
            self.op("act", "activation", writes=[ar], out=bv[:, 0:5, :], in_=bv[:, 0:5, :], func=AF.Exp)
            self.op("act", "activation", writes=[ar], out=bs[:, 0:4, :], in_=bs[:, 0:4, :], func=AF.Exp)
            self.op("act", "activation", writes=[ar], out=bs[0:16, 4, :], in_=bs[0:16, 4, :], func=AF.Exp)
            q = ab[:, self.A_Q:self.A_Q + NCOL]
            k = ab[:, self.A_K:self.A_K + NCOL]
            vv = ab[:, self.A_V:self.A_V + TT * 129].rearrange("p (t c) -> p t c", t=TT)
            for t in range(16):
                kts = []
                for o in range(4, -1, -1):
                    jt = t - o
                    if jt < 0:
                        continue
                    kts.append(dict(qk=[(k[:, jt * 128:(jt + 1) * 128], q[:, t * 128:(t + 1) * 128])], nk=128,
                                    v=vv[:, jt, :], bias=(bv, 4 - o), reads=[ar]))
                self.attn(kts, 128, sc, None, gv[:, t, :], [ar], self.actT[:, h, t * 128:(t + 1) * 128], self.act_r[t])
            self.flush()
            i = self.rot("ld", 2)
            ld = self.ld[i]
            lr = self.ld_r[i]
            self.dma(ld[:, :].rearrange("p (a b) -> p a b", a=4), AP(ck, h * 128, [[1536, 128], [128 * 1536, 4], [1, 128]]),
                     lr, writes=[lr])
            tpb, tpr = self.nb("tp")
            tpv = tpb.rearrange("p (a b) -> p a b", a=4)
            for c in range(4):
                self.op("pe", "transpose", reads=[lr, self.ident_r], writes=[tpr], out=tpv[:, c, :],
                        in_=ld[:, c * 128:(c + 1) * 128], identity=self.ident[:])
            self.op("act", "activation", reads=[tpr], writes=[self.kcC_r], out=self.kcC[:], in_=tpb[:, 0:512], func=AF.Copy)
            i = self.rot("ld", 2)
            ld = self.ld[i]
            lr = self.ld_r[i]
            self.dma(ld[:, :].rearrange("p (a b) -> p a b", a=4), AP(cv, h * 128, [[1536, 128], [128 * 1536, 4], [1, 128]]),
                     lr, writes=[lr])
            self.op("dve", "tensor_copy", reads=[lr], writes=[self.vcC_r], out=self.vcC[:, :, 0:128],
                    in_=ld[:, :].rearrange("p (a b) -> p a b", a=4))
            qs = q[:, 2048:2064]
            kts = []
            for jc in range(4):
                kts.append(dict(qk=[(self.kcC[:, jc * 128:(jc + 1) * 128], qs)], nk=128, v=self.vcC[:, jc, :],
                                bias=(bs, jc), reads=[ar, self.kcC_r, self.vcC_r]))
            kts.append(dict(qk=[(k[:, 2048:2064], qs)], nk=16, v=vv[0:16, 16, :], bias=(bs, 4), reads=[ar]))
            self.attn(kts, 16, sc, None, gv[0:16, 16, :], [ar], self.actT[:, h, 2048:2064], self.act_r[16])
            self.flush()

    def rope(self, x1, x2, rd, rows, nh, tt, scale_ap, out1, out2, wr):
        rp = self.rp
        rr = self.rp_r
        cs = self.cosT[0:rows, tt, :].unsqueeze(1).broadcast_to([rows, nh, 32])
        sn = self.sinT[0:rows, tt, :].unsqueeze(1).broadcast_to([rows, nh, 32])
        def t(i):
            return rp[0:rows, i, 0:nh * 32].rearrange("p (a b) -> p a b", a=nh)
        crd = [self.cos_r, self.sin_r]
        self.op("dve", "tensor_tensor", reads=rd + crd, writes=[rr], out=t(0), in0=x1, in1=cs, op=ALU.mult)
        self.op("dve", "tensor_tensor", reads=rd + crd, writes=[rr], out=t(1), in0=x2, in1=sn, op=ALU.mult)
        self.op("dve", "tensor_tensor", reads=rd + crd, writes=[rr], out=t(2), in0=x1, in1=sn, op=ALU.mult)
        self.op("dve", "tensor_tensor", reads=rd + crd, writes=[rr], out=t(3), in0=x2, in1=cs, op=ALU.mult)
        if scale_ap is None:
            self.op("dve", "tensor_tensor", reads=[rr], writes=wr, out=out1, in0=t(0), in1=t(1), op=ALU.subtract)
            self.op("dve", "tensor_tensor", reads=[rr], writes=wr, out=out2, in0=t(2), in1=t(3), op=ALU.add)
        else:
            self.op("dve", "tensor_tensor", reads=[rr], writes=[rr], out=t(4), in0=t(0), in1=t(1), op=ALU.subtract)
            self.op("dve", "tensor_tensor", reads=[rr], writes=[rr], out=t(5), in0=t(2), in1=t(3), op=ALU.add)
            self.op("dve", "tensor_tensor", reads=[rr] + rd, writes=wr, out=out1, in0=t(4), in1=scale_ap, op=ALU.mult)
            self.op("dve", "tensor_tensor", reads=[rr] + rd, writes=wr, out=out2, in0=t(5), in1=scale_ap, op=ALU.mult)

    def layer_b(self, L):
        din = self.din
        dout = self.dout
        gxq = self.gcolB[:, L:L + 1]
        cqT = self.cqT()
        blocks = [(0, 512), (512, 320), (832, 512)] + [(1344 + c * 512, 512) for c in range(4)]

        def ep(tt, rows, bi, acc, accr):
            if bi == 0:
                zv, zr = self.headnorm(acc, accr, rows, 1, 512)
                z4 = zv.rearrange("p a (c b) -> p (a c) b", c=4)

                def tail():
                    tpb, tpr = self.nb("tp")
                    tpv = tpb.rearrange("p (a b) -> p a b", a=4)
                    for c in range(4):
                        self.op("pe", "transpose", reads=[zr, self.ident_r], writes=[tpr], out=tpv[:, c, 0:rows],
                                in_=z4[:, c, :], identity=self.ident[0:rows, 0:rows])
                    g = self.gcolB[:, 14:18].unsqueeze(2).broadcast_to([128, 4, rows])
                    self.op("dve", "tensor_tensor", reads=[tpr, self.gcolB_r], adds=[self.cq_r],
                            out=cqT[:, :, tt * 128:tt * 128 + rows], in0=tpv[:, 0:4, 0:rows], in1=g, op=ALU.mult)
                self.defer(tail, 3)
            elif bi == 1:
                zv, zr = self.headnorm(acc[:, 0:256], accr, rows, 1, 256)
                i = self.rot("of", 2)
                of = self.of[i]
                orr = self.of_r[i]
                if tt < 16:
                    dst_ckv = AP(dout["b_ckv_prompt"], tt * 128 * 256, [[256, 128], [1, 256]])
                else:
                    dst_ckv = AP(dout["b_ckv_sample"], 0, [[256, 16], [1, 256]])

                def s1():
                    self.op("dve", "tensor_tensor", reads=[zr, self.gbc_r], writes=[orr], out=of[0:rows, 0:256],
                            in0=zv.rearrange("p a b -> p (a b)"), in1=self.gbc[0:rows, 0:256], op=ALU.mult)
                    self.dma(dst_ckv, of[0:rows, 0:256], orr, reads=[orr], is_output=True)
                self.defer(s1, 1)

                def tail():
                    tpb, tpr = self.nb("tp")
                    tpv = tpb.rearrange("p (a b) -> p a b", a=4)
                    for c in range(2):
                        self.op("pe", "transpose", reads=[orr, self.ident_r], writes=[tpr], out=tpv[:, c, 0:rows],
                                in_=of[0:rows, c * 128:(c + 1) * 128], identity=self.ident[0:rows, 0:rows])
                    self.op("act", "activation", reads=[tpr], adds=[self.ckvT_r], out=self.ckvT[:, :, tt * 128:tt * 128 + rows],
                            in_=tpv[:, 0:2, 0:rows], func=AF.Copy)
                self.defer(tail, 3)
                x1 = acc[:, 256:288].unsqueeze(1)
                x2 = acc[:, 288:320].unsqueeze(1)
                o1 = self.kr_all[0:rows, tt, 0:32].unsqueeze(1)
                o2 = self.kr_all[0:rows, tt, 32:64].unsqueeze(1)
                self.rope(x1, x2, [accr], rows, 1, tt, None, o1, o2, [self.kr_all_r])
                if tt < 16:
                    dst = AP(dout["b_krope_prompt"], tt * 128 * 64, [[64, 128], [1, 64]])
                else:
                    dst = AP(dout["b_krope_sample"], 0, [[64, 16], [1, 64]])
                self.dma(dst, self.kr_all[0:rows, tt, :], self.kr_all_r, reads=[self.kr_all_r], is_output=True)
                self.op("act", "activation", reads=[self.kr_all_r], writes=[self.sq_r, self.krss_r], out=self.sq[0:rows, 0:64],
                        in_=self.kr_all[0:rows, tt, :], func=AF.Square, accum_out=self.krss[0:rows, tt:tt + 1])
            elif bi == 2:
                zv, zr = self.headnorm(acc, accr, rows, 4, 128)
                self.transpose_out(zv, zr, rows, 4, gxq, dst_dram=self.qk_store(self.xqT_s, 0, 4, tt, rows),
                                   dram_res=self.xqT_s[1])
            else:
                self.gate_out(acc, accr, rows, 512, tt, (bi - 3) * 512)

        self.proj((din["b_w_in"], 0), 3392, 16, blocks, self.x_tiles(), self.act_ap, ep)
        if self.chk(L, "P"):
            return

        def epq(tt, rows, bi, acc, accr):
            h0 = bi * 2
            self.op("act", "activation", reads=[accr], writes=[self.sq_r], out=self.sq[0:rows, 0:384], in_=acc, func=AF.Square)
            st, sr = self.newstat()
            self.op("dve", "tensor_reduce", reads=[self.sq_r], writes=[sr], out=st[0:rows, 0:2],
                    in_=self.sq[0:rows, 0:384].rearrange("p (a b) -> p a b", a=2), axis=AX.X, op=ALU.add)
            self.op("dve", "tensor_scalar", reads=[sr], writes=[sr], out=st[0:rows, 4:6], in0=st[0:rows, 0:2],
                    scalar1=1.0 / 192, scalar2=EPS, op0=ALU.mult, op1=ALU.add)
            self.op("pool", "tensor_tensor", reads=[sr, self.cm05_r], writes=[sr], out=st[0:rows, 8:10],
                    in0=st[0:rows, 4:6], in1=self.cm05[0:rows, 0:2], op=ALU.pow)
            i = self.rot("zf", 2)
            zf = self.zf[i]
            zr = self.zf_r[i]
            a3 = acc.rearrange("p (a b) -> p a b", a=2)
            z3 = zf[0:rows, 0:384].rearrange("p (a b) -> p a b", a=2)
            def s1():
                self.op("dve", "tensor_tensor", reads=[accr, sr], writes=[zr], out=z3[:, :, 0:128], in0=a3[:, :, 0:128],
                        in1=st[0:rows, 8:10].unsqueeze(2).broadcast_to([rows, 2, 128]), op=ALU.mult)
                rs32 = st[0:rows, 8:10].unsqueeze(2).broadcast_to([rows, 2, 32])
                self.rope(a3[:, :, 128:160], a3[:, :, 160:192], [accr, sr], rows, 2, tt, rs32, z3[:, :, 128:160],
                          z3[:, :, 160:192], [zr])
            self.defer(s1, 1)

            def tail():
                tpb, tpr = self.nb("tp")
                tpv = tpb.rearrange("p (a b) -> p a b", a=4)
                for c in range(2):
                    self.op("pe", "transpose", reads=[zr, self.ident_r], writes=[tpr], out=tpv[:, c, 0:rows],
                            in_=z3[:, c, 0:128], identity=self.ident[0:rows, 0:rows])
                    self.op("pe", "transpose", reads=[zr, self.ident_r], writes=[tpr], out=tpv[0:64, 2 + c, 0:rows],
                            in_=z3[:, c, 128:192], identity=self.ident[0:rows, 0:rows])
                i = self.rot("tb", 3)
                tb = self.tb[i]
                tr = self.tb_r[i]
                self.op("act", "activation", reads=[tpr, self.gcolB_r], writes=[tr], out=tb[:, 0:2, 0:rows],
                        in_=tpv[:, 0:2, 0:rows], func=AF.Copy, scale=self.gcolB[:, 20:21])
                self.op("act", "activation", reads=[tpr, self.gcolB_r], writes=[tr], out=tb[0:64, 2:4, 0:rows],
                        in_=tpv[0:64, 2:4, 0:rows], func=AF.Copy, scale=self.gcolB[0:64, 21:22])
                self.dma(self.qk_store(self.q192_s, h0, 2, tt, rows, hrows=192), tb[:, 0:2, 0:rows], tr, reads=[tr],
                         adds=[self.q192_s[1]])
                self.dma(self.qk_store(self.q192_s, h0, 2, tt, rows, dpart=64, hrows=192, r0=128), tb[0:64, 2:4, 0:rows], tr,
                         reads=[tr], adds=[self.q192_s[1]])
            self.defer(tail, 3)

        self.proj((din["b_w_q_b"], 0), 2304, 4, [(c * 384, 384) for c in range(6)],
                  [(t, rows_of(t), [self.cq_r]) for t in range(TT)],
                  lambda key, k: cqT[:, k, key * 128:key * 128 + rows_of(key)], epq)

        def epkv(key, rows, bi, acc, accr):
            kind_, idx = key
            h0 = bi * 2
            if kind_ == "c":
                kr = self.krt[idx % 3][0:rows, :]
                krr = [self.krt_r[idx % 3]]
                ssap = self.krss_c[0:rows, idx:idx + 1]
                ssr = [self.krss_c_r]
            else:
                kr = self.kr_all[0:rows, idx, :]
                krr = [self.kr_all_r]
                ssap = self.krss[0:rows, idx:idx + 1]
                ssr = [self.krss_r]
            a4 = acc.rearrange("p (a b) -> p a b", a=2)
            sq2 = self.sq[0:rows, 0:256].rearrange("p (a b) -> p a b", a=2)
            self.op("act", "activation", reads=[accr], writes=[self.sq_r], out=sq2, in_=a4[:, :, 0:128], func=AF.Square)
            st, sr = self.newstat()
            self.op("dve", "tensor_reduce", reads=[self.sq_r], writes=[sr], out=st[0:rows, 0:2], in_=sq2, axis=AX.X, op=ALU.add)
            self.op("dve", "tensor_scalar", reads=[sr] + ssr, writes=[sr], out=st[0:rows, 12:14], in0=st[0:rows, 0:2],
                    scalar1=ssap, scalar2=None, op0=ALU.add)
            self.op("dve", "tensor_scalar", reads=[sr], writes=[sr], out=st[0:rows, 4:6], in0=st[0:rows, 12:14],
                    scalar1=1.0 / 192, scalar2=EPS, op0=ALU.mult, op1=ALU.add)
            self.op("pool", "tensor_tensor", reads=[sr, self.cm05_r], writes=[sr], out=st[0:rows, 8:10],
                    in0=st[0:rows, 4:6], in1=self.cm05[0:rows, 0:2], op=ALU.pow)
            rs2 = st[0:rows, 8:10].unsqueeze(2)
            i = self.rot("zf", 2)
            zf = self.zf[i]
            zr = self.zf_r[i]
            z3 = zf[0:rows, 0:384].rearrange("p (a b) -> p a b", a=2)
            def s1():
                self.op("dve", "tensor_tensor", reads=[accr, sr], writes=[zr], out=z3[:, :, 0:128], in0=a4[:, :, 0:128],
                        in1=rs2.broadcast_to([rows, 2, 128]), op=ALU.mult)
                self.op("dve", "tensor_tensor", reads=krr + [sr], writes=[zr], out=z3[:, :, 128:192],
                        in0=kr.unsqueeze(1).broadcast_to([rows, 2, 64]), in1=rs2.broadcast_to([rows, 2, 64]), op=ALU.mult)
            self.defer(s1, 1)
            if kind_ == "p":
                kd, vd, tile, width, nt = self.k192_s, self.v_s, idx, NCOL, TT
            else:
                tile = idx if kind_ == "c" else 32
                kd, vd, width, nt = self.k192s_s, self.vs_s, 4224, 33

            def tail():
                tpb, tpr = self.nb("tp")
                tpv = tpb.rearrange("p (a b) -> p a b", a=4)
                for c in range(2):
                    self.op("pe", "transpose", reads=[zr, self.ident_r], writes=[tpr], out=tpv[:, c, 0:rows],
                            in_=z3[:, c, 0:128], identity=self.ident[0:rows, 0:rows])
                    self.op("pe", "transpose", reads=[zr, self.ident_r], writes=[tpr], out=tpv[0:64, 2 + c, 0:rows],
                            in_=z3[:, c, 128:192], identity=self.ident[0:rows, 0:rows])
                i = self.rot("tb", 3)
                tb = self.tb[i]
                tr = self.tb_r[i]
                self.op("act", "activation", reads=[tpr, self.gcolB_r], writes=[tr], out=tb[:, 0:2, 0:rows],
                        in_=tpv[:, 0:2, 0:rows], func=AF.Copy, scale=self.gcolB[:, 22:23])
                self.op("act", "activation", reads=[tpr, self.gcolB_r], writes=[tr], out=tb[0:64, 2:4, 0:rows],
                        in_=tpv[0:64, 2:4, 0:rows], func=AF.Copy, scale=self.gcolB[0:64, 23:24])
                self.dma(self.qk_store(kd, h0, 2, tile, rows, width=width, hrows=192), tb[:, 0:2, 0:rows], tr, reads=[tr],
                         adds=[kd[1]])
                self.dma(self.qk_store(kd, h0, 2, tile, rows, width=width, dpart=64, hrows=192, r0=128), tb[0:64, 2:4, 0:rows],
                         tr, reads=[tr], adds=[kd[1]])
            self.defer(tail, 3)
            self.v_out(a4[:, :, 128:256], accr, rows, 2, self.v_store(vd, h0, 2, tile, rows, ntiles=nt), vd[1])

        tiles = [(("p", t), 128, [self.ckvT_r]) for t in range(16)] + [(("s", 16), 16, [self.ckvT_r])]

        def actkv(key, k):
            kind_, idx = key
            if kind_ == "c":
                return self.ckvt[idx % 2][:, k, :]
            rows = rows_of(idx)
            return self.ckvT[:, k, idx * 128: idx * 128 + rows]

        cache_tiles = []
        for jc in range(32):
            cache_tiles.append((("c", jc), 128, [self.ckvt_r[jc % 2]]))
        self._kv_prep_pending = True
        self.proj_kv((din["b_w_kv_b"], 0), tiles, cache_tiles, actkv, epkv)

        sc = 192 ** -0.5
        QB, KB = 8736, 10912
        for h in range(12):
            ai = self.rot("arena", 2)
            ar = self.arena_r[ai]
            ab = self.arena_bf(ai)
            Q, Qr = self.q192_s
            Kt, Kr = self.k192_s
            self.dma(ab[0:96, self.A_Q:self.A_Q + 2048], AP(Q, h * 192 * NCOL, [[NCOL, 96], [1, 2048]]), ar, reads=[Qr], adds=[ar])
            self.dma(ab[0:96, QB:QB + 2048], AP(Q, (h * 192 + 96) * NCOL, [[NCOL, 96], [1, 2048]]), ar, reads=[Qr], adds=[ar])
            self.dma(ab[0:96, self.A_K:self.A_K + 2048], AP(Kt, h * 192 * NCOL, [[NCOL, 96], [1, 2048]]), ar, reads=[Kr], adds=[ar])
            self.dma(ab[0:96, KB:KB + 2048], AP(Kt, (h * 192 + 96) * NCOL, [[NCOL, 96], [1, 2048]]), ar, reads=[Kr], adds=[ar])
            G, Gr = self.gate_s
            gv = ab[:, self.A_G:self.A_G + 2176].rearrange("p (t c) -> p t c", t=TT)
            self.dma(gv[:, 0:16, :], AP(G, h * 128, [[2048, 128], [128 * 2048, 16], [1, 128]]), ar, reads=[Gr], adds=[ar])
            vv = ab[:, self.A_V:self.A_V + TT * 129].rearrange("p (t c) -> p t c", t=TT)
            self.dma(vv[:, 0:16, :], AP(self.v_s[0], h * TT * 128 * 129, [[129, 128], [128 * 129, 16], [1, 129]]), ar,
                     reads=[self.v_s[1]], adds=[ar])
            qa = ab[0:96, self.A_Q:self.A_Q + NCOL]
            qb = ab[0:96, QB:QB + NCOL]
            ka = ab[0:96, self.A_K:self.A_K + 2048]
            kb = ab[0:96, KB:KB + 2048]
            for t in range(16):
                kts = []
                for jt in range(t + 1):
                    kts.append(dict(qk=[(ka[:, jt * 128:(jt + 1) * 128], qa[:, t * 128:(t + 1) * 128]),
                                        (kb[:, jt * 128:(jt + 1) * 128], qb[:, t * 128:(t + 1) * 128])], nk=128,
                                    v=vv[:, jt, :], bias=None, diag=(jt == t), reads=[ar]))
                self.attn(kts, 128, sc, None, gv[:, t, :], [ar], self.actT[:, h, t * 128:(t + 1) * 128], self.act_r[t])
            self.flush()
        KA0, KB0, V0, QA0, QB0, G0 = 0, 4224, 8448, 12708, 12724, 12740
        for h in range(12):
            ai = self.rot("arena", 2)
            ar = self.arena_r[ai]
            ab = self.arena_bf(ai)
            Q, Qr = self.q192_s
            Kt, Kr = self.k192s_s
            ka = ab[0:96, KA0:KA0 + 4224]
            kb = ab[0:96, KB0:KB0 + 4224]
            vv = ab[:, V0:V0 + 33 * 129].rearrange("p (t c) -> p t c", t=33)
            qa = ab[0:96, QA0:QA0 + 16]
            qb = ab[0:96, QB0:QB0 + 16]
            gt = ab[0:16, G0:G0 + 128]
            self.dma(ka[:, 0:4112], AP(Kt, h * 192 * 4224, [[4224, 96], [1, 4112]]), ar, reads=[Kr], adds=[ar])
            self.dma(kb[:, 0:4112], AP(Kt, (h * 192 + 96) * 4224, [[4224, 96], [1, 4112]]), ar, reads=[Kr], adds=[ar])
            self.dma(vv[:, 0:32, :], AP(self.vs_s[0], h * 33 * 128 * 129, [[129, 128], [128 * 129, 32], [1, 129]]), ar,
                     reads=[self.vs_s[1]], adds=[ar])
            self.dma(vv[0:16, 32, :], AP(self.vs_s[0], (h * 33 + 32) * 128 * 129, [[129, 16], [1, 129]]), ar,
                     reads=[self.vs_s[1]], adds=[ar])
            self.dma(qa, AP(Q, h * 192 * NCOL + 2048, [[NCOL, 96], [1, 16]]), ar, reads=[Qr], adds=[ar])
            self.dma(qb, AP(Q, (h * 192 + 96) * NCOL + 2048, [[NCOL, 96], [1, 16]]), ar, reads=[Qr], adds=[ar])
            self.dma(gt, AP(self.gate_s[0], 2048 * 2048 + h * 128, [[2048, 16], [1, 128]]), ar, reads=[self.gate_s[1]], adds=[ar])
            s0, s0r = self.nb("s")
            s1, s1r = self.nb("s")
            for jc in range(32):
                self.op("pe", "matmul", reads=[ar], writes=[s0r], out=s0[:, jc * 16:(jc + 1) * 16],
                        lhsT=ka[:, jc * 128:(jc + 1) * 128], rhs=qa, start=True, stop=False)
                self.op("pe", "matmul", reads=[ar], writes=[s0r], out=s0[:, jc * 16:(jc + 1) * 16],
                        lhsT=kb[:, jc * 128:(jc + 1) * 128], rhs=qb, start=False, stop=True)
            self.op("pe", "matmul", reads=[ar], writes=[s1r], out=s1[0:16, 0:16], lhsT=ka[:, 4096:4112], rhs=qa,
                    start=True, stop=False)
            self.op("pe", "matmul", reads=[ar], writes=[s1r], out=s1[0:16, 0:16], lhsT=kb[:, 4096:4112], rhs=qb,
                    start=False, stop=True)
            p0, p0r = self.pexp[0], self.pexp_r[0]
            p1, p1r = self.pexp[1], self.pexp_r[1]
            self.op("act", "activation", reads=[s0r], writes=[p0r], out=p0[:, 0:512], in_=s0[:, 0:512], func=AF.Exp, scale=sc)
            self.op("act", "activation", reads=[s1r], writes=[p1r], out=p1[0:16, 0:16], in_=s1[0:16, 0:16], func=AF.Exp, scale=sc)
            ob, orr = self.nb("o")
            for jc in range(32):
                self.op("pe", "matmul", reads=[p0r, ar], writes=[orr], out=ob[0:16, 0:129], lhsT=p0[:, jc * 16:(jc + 1) * 16],
                        rhs=vv[:, jc, :], start=(jc == 0), stop=False)
            self.op("pe", "matmul", reads=[p1r, ar], writes=[orr], out=ob[0:16, 0:129], lhsT=p1[0:16, 0:16], rhs=vv[0:16, 32, :],
                    start=False, stop=True)
            self.attn_post(ob, orr, 16, None, gt, [ar], self.actT[:, h, 2048:2064], self.act_r[16])
            self.flush()

    def proj_kv(self, W, tiles, cache_tiles, actkv, epkv):
        din = self.din
        wh, woff = W
        ai = self.rot("arena", 2)
        ar = self.arena_r[ai]
        wv = self.arena_bf(ai)[:, 0:2 * 3072].rearrange("p (k w) -> p k w", k=2)
        for c in range(6):
            src = AP(wh, woff + c * 512, [[3072, 128], [128 * 3072, 2], [1, 512]])
            self.dma(wv[:, :, c * 512:(c + 1) * 512], src, ar, adds=[ar], q="pool")
        ck = din["cache_b_ckv"]
        ckr = din["cache_b_krope"]

        def prep(idx):
            b = idx % 2
            i = self.rot("ld", 2)
            ld = self.ld[i]
            lr = self.ld_r[i]
            self.dma(ld[:, 0:256], AP(ck, idx * 128 * 256, [[256, 128], [1, 256]]), lr, writes=[lr])
            b3 = idx % 3
            self.dma(self.krt[b3][:], AP(ckr, idx * 128 * 64, [[64, 128], [1, 64]]), self.krt_r[b3], writes=[self.krt_r[b3]])
            tpb, tpr = self.nb("tp")
            tpv = tpb.rearrange("p (a b) -> p a b", a=4)
            for c in range(2):
                self.op("pe", "transpose", reads=[lr, self.ident_r], writes=[tpr], out=tpv[:, c, :],
                        in_=ld[:, c * 128:(c + 1) * 128], identity=self.ident[:])
            self.op("act", "activation", reads=[tpr], writes=[self.ckvt_r[b]], out=self.ckvt[b][:], in_=tpv[:, 0:2, :],
                    func=AF.Copy)
            self.op("act", "activation", reads=[self.krt_r[b3]], writes=[self.rp_r, self.krss_c_r], out=self.rp[:, 0, :],
                    in_=self.krt[b3][:], func=AF.Square, accum_out=self.krss_c[:, idx:idx + 1])

        allt = tiles + cache_tiles
        for ti, (key, rows, rd) in enumerate(allt):
            kind_, idx = key
            if kind_ == "c" and idx == 0:
                prep(0)
            if ti + 1 < len(allt) and allt[ti + 1][0][0] == "c" and allt[ti + 1][0][1] > 0:
                prep(allt[ti + 1][0][1])
            for bi in range(6):
                acc, accr = self.nb("acc")
                for k in range(2):
                    self.op("pe", "matmul", reads=[ar] + rd, writes=[accr], out=acc[0:rows, 0:512], lhsT=actkv(key, k),
                            rhs=wv[:, k, bi * 512:(bi + 1) * 512], start=(k == 0), stop=(k == 1))
                self.step()
                epkv(key, rows, bi, acc[0:rows, 0:512], accr)
        self.flush()

    def mem_heads(self, L):
        sc = 128 ** -0.5
        for hm in range(4):
            ai = self.rot("arena", 2)
            ar = self.arena_r[ai]
            ab, gv = self.load_head(ai, (self.xqT_s[0], hm * 128 * NCOL, self.xqT_s[1]), (12 + hm) * 128, None, None,
                                    with_kv=False)
            q = ab[:, self.A_Q:self.A_Q + NCOL]
            for t in range(16):
                kts = [dict(qk=[(self.mkT[:, hm, jt * 128:(jt + 1) * 128], q[:, t * 128:(t + 1) * 128])], nk=128,
                            v=self.mv[:, jt, hm, :], bias=None, reads=[ar, self.mkT_r, self.mv_r]) for jt in range(2)]
                self.attn(kts, 128, sc, None, gv[:, t, :], [ar], self.actT[:, 12 + hm, t * 128:(t + 1) * 128], self.act_r[t])
            qs = q[:, 2048:2064]
            kts = [dict(qk=[(self.mkTs[:, hm, jt * 128:(jt + 1) * 128], qs)], nk=128, v=self.mvs[:, jt, hm, :], bias=None,
                        reads=[ar, self.mkTs_r, self.mvs_r]) for jt in range(2)]
            self.attn(kts, 16, sc, None, gv[0:16, 16, :], [ar], self.actT[:, 12 + hm, 2048:2064], self.act_r[16])
            self.flush()

    def out_phase(self, L):
        din = self.din
        last = (L == self.n_layers - 1)
        xnew, xnew_r = self.xres[L % 2]
        srcs = self._xsrc
        if L > 0:
            xold_r = [self.xres[(L - 1) % 2][1]]
        else:
            xold_r = []

        pend = {}
        order = [(tt, bi) for bi in range(4) for tt in range(TT)]

        def issue(n):
            tt, bi = order[n]
            rows = rows_of(tt)
            i = self.rot("ld", 2)
            ld = self.ld[i]
            lr = self.ld_r[i]
            s = srcs[tt][0]
            src = AP(s.tensor, s.offset + bi * 512, [[D, rows], [1, 512]])
            self.dma(ld[0:rows, :], src, lr, reads=xold_r, writes=[lr])
            pend[(tt, bi)] = (ld, lr)

        issue(0)

        def ep(tt, rows, bi, acc, accr):
            n = order.index((tt, bi))
            if n + 1 < len(order):
                issue(n + 1)
            ld, lr = pend.pop((tt, bi))
            j = self.rot("of", 2)
            of = self.of[j]
            orr = self.of_r[j]
            self.op("dve", "tensor_tensor", reads=[accr, lr], writes=[orr], out=of[0:rows, :], in0=acc, in1=ld[0:rows, :],
                    op=ALU.add)
            if last:
                if tt < 16:
                    dst = AP(self.dout["y_prompt"], tt * 128 * D + bi * 512, [[D, 128], [1, 512]])
                else:
                    dst = AP(self.dout["y_sample"], bi * 512, [[D, 16], [1, 512]])
                self.dma(dst, of[0:rows, :], orr, reads=[orr], is_output=True)
            else:
                dst = AP(xnew, tt * 128 * D + bi * 512, [[D, rows], [1, 512]])
                self.dma(dst, of[0:rows, :], orr, reads=[orr], adds=[xnew_r])

        self.proj((din["w_out"], L * 2048 * 2048), 2048, 16, [(c * 512, 512) for c in range(4)], self.x_tiles(),
                  self.act_ap, ep)


def _consts():
    ident = np.eye(128, dtype=np.float32)
    rel = np.arange(384) - 128
    half, max_exact = 16, 8
    ret = np.where(rel < 0, half, 0)
    n = np.abs(rel)
    nf = np.maximum(n, 1).astype(np.float32)
    large = max_exact + (np.log(nf / max_exact) / np.float32(np.log(128 / max_exact)) * (half - max_exact)).astype(np.int32)
    large = np.minimum(large, half - 1)
    bucket = ret + np.where(n < max_exact, n, large)
    oh = np.zeros((32, 384), np.float32)
    oh[bucket, np.arange(384)] = 1.0
    pos = np.zeros((128, 17), np.float32)
    for tt in range(16):
        pos[:, tt] = tt * 128 + np.arange(128)
    pos[:, 16] = 4096 + np.arange(128)
    inv = (np.float32(10000.0) ** (-np.arange(32, dtype=np.float32) / np.float32(32))).astype(np.float32)
    ang = (pos[:, :, None] * inv[None, None, :]).astype(np.float32)
    return ident, oh, np.cos(ang).astype(np.float32), np.sin(ang).astype(np.float32)


_PROG = {}


def kernel(**inputs):
    n = 8
    if "p" not in _PROG:
        _PROG["p"] = Prog()
    prog = _PROG["p"]
    ident, oh, cs, sn = _consts()
    per_batch = {"x_prompt": 0, "x_sample": 0, "mem_prompt": 0, "cache_a_k": 1, "cache_a_v": 1, "cache_b_ckv": 1,
                 "cache_b_krope": 1, "cache_c_k": 1, "cache_c_v": 1, "cache_mem_k": 1, "cache_mem_v": 1}
    shapes = dict(IN_SPECS)
    in_maps = []
    for b in range(n):
        m = {}
        for name, shp in IN_SPECS:
            if name.startswith("k_"):
                continue
            a = np.asarray(inputs[name], dtype=np.float32)
            if name in per_batch:
                a = np.take(a, b, axis=per_batch[name])
                if name in ("cache_b_ckv", "cache_b_krope", "cache_c_k", "cache_c_v"):
                    a = a[0]
            m[name] = np.ascontiguousarray(a).reshape(shp)
        m["k_ident"], m["k_t5oh"], m["k_cos"], m["k_sin"] = ident, oh, cs, sn
        in_maps.append(m)
    res = run_bass_kernel_spmd(prog.nc, in_maps, core_ids=list(range(n)))
    r = res.results

    def st(name, shape_fn):
        return np.stack([shape_fn(r[b][name]) for b in range(n)])

    y_p = st("y_prompt", lambda a: a)
    y_s = st("y_sample", lambda a: a)
    def lay(name, nl, rows, kvh):
        return np.stack([r[b][name].reshape(nl, rows, kvh, 128) for b in range(n)], axis=1)
    outs = (
        y_p, y_s,
        lay("a_k_prompt", 2, 128, 4), lay("a_v_prompt", 2, 128, 4), lay("a_k_sample", 2, 16, 4), lay("a_v_sample", 2, 16, 4),
        st("b_ckv_prompt", lambda a: a)[None], st("b_krope_prompt", lambda a: a)[None],
        st("b_ckv_sample", lambda a: a)[None], st("b_krope_sample", lambda a: a)[None],
        lay("c_k_prompt", 1, 512, 12), lay("c_v_prompt", 1, 512, 12), lay("c_k_sample", 1, 16, 12), lay("c_v_sample", 1, 16, 12),
        lay("mem_k_prompt", 4, 256, 4), lay("mem_v_prompt", 4, 256, 4),
    )
    return tuple(np.ascontiguousarray(o, dtype=np.float32) for o in outs)
```

```python
import numpy as np
import concourse.bass as bass
import concourse.mybir as mybir
from concourse.bass_types import AP
from concourse.bass_utils import run_bass_kernel_spmd

F32 = mybir.dt.float32
BF = mybir.dt.bfloat16
AF = mybir.ActivationFunctionType
ALU = mybir.AluOpType
AX = mybir.AxisListType

D = 2048
TT = 17
NCOL = TT * 128
EPS = 1e-6
NEG = -1e30


class Res:
    __slots__ = ("name", "w", "r", "dsem", "dcnt")

    def __init__(self, name):
        self.name = name
        self.w = []
        self.r = []
        self.dsem = None
        self.dcnt = 0


class Sched:
    ENG = ("pe", "act", "dve", "pool", "sp")

    def __init__(self, nc):
        self.nc = nc
        self.prog = {e: [] for e in self.ENG}
        self.esem = {e: nc.alloc_semaphore("es_" + e) for e in ("pe", "act", "dve", "pool")}
        self.cnt = {e: 0 for e in self.ENG}
        self.known = {e: {} for e in self.ENG}
        self.semobj = {"es_" + e: s for e, s in self.esem.items()}
        self.nd = 0
        self.nwaits = 0
        self.out_events = []

    def _deps(self, eng, reads, writes, adds):
        need = {}
        own = "es_" + eng
        kn = self.known[eng]

        def add(ev, raw):
            k, v, clk = ev
            if k == own and (eng == "pe" or not raw):
                return
            if kn.get(k, 0) >= v:
                return
            if need.get(k, (0, None))[0] < v:
                need[k] = (v, clk)

        for res in reads:
            for ev in res.w:
                add(ev, True)
        for res in writes:
            for ev in res.w:
                add(ev, False)
            for ev in res.r:
                add(ev, False)
        for res in adds:
            for ev in res.r:
                add(ev, False)
        waits = []
        for k, (v, clk) in sorted(need.items(), key=lambda kv: -len(kv[1][1])):
            if kn.get(k, 0) >= v:
                continue
            waits.append((self.semobj[k], v))
            for kk, vv in clk.items():
                if kn.get(kk, 0) < vv:
                    kn[kk] = vv
            if kn.get(k, 0) < v:
                kn[k] = v
        self.nwaits += len(waits)
        return waits

    def _mark(self, ev, reads, writes, adds):
        k = ev[0]
        for res in writes:
            res.w = [ev]
            res.r = []
        for res in adds:
            res.w = [e for e in res.w if e[0] != k]
            res.w.append(ev)
        for res in reads:
            res.r = [e for e in res.r if e[0] != k]
            res.r.append(ev)

    def op(self, eng, name, reads=(), writes=(), adds=(), **kw):
        waits = self._deps(eng, reads, writes, adds)
        self.cnt[eng] += 1
        n = self.cnt[eng]
        sem = self.esem[eng]

        def run(e, name=name, kw=kw, waits=waits, sem=sem):
            for s, v in waits:
                e.wait_ge(s, v)
            getattr(e, name)(**kw).then_inc(sem, 1)

        self.prog[eng].append(run)
        clk = dict(self.known[eng])
        clk["es_" + eng] = n
        ev = ("es_" + eng, n, clk)
        self._mark(ev, reads, writes, adds)
        return ev

    def dma(self, q, out, in_, sres, reads=(), writes=(), adds=(), is_output=False):
        waits = self._deps(q, reads, writes, adds)
        if sres.dsem is None:
            sres.dsem = {}
            sres.dcnt = {}
        if q not in sres.dsem:
            self.nd += 1
            key = "ds%d" % self.nd
            sres.dsem[q] = key
            sres.dcnt[q] = 0
            self.semobj[key] = self.nc.alloc_semaphore(key)
        sres.dcnt[q] += 16
        dkey = sres.dsem[q]
        dval = sres.dcnt[q]
        sem = self.semobj[dkey]

        def run(e, waits=waits, sem=sem, out=out, in_=in_):
            for s, v in waits:
                e.wait_ge(s, v)
            e.dma_start(out=out, in_=in_).then_inc(sem, 16)

        self.prog[q].append(run)
        ev = (dkey, dval, dict(self.known[q]))
        self._mark(ev, reads, writes, adds)
        self.out_events.append(ev)
        return ev

    def emit(self):
        nc = self.nc
        last = {}
        for k, v, clk in self.out_events:
            last[k] = max(last.get(k, 0), v)
        for e in ("pe", "act", "dve", "pool"):
            if self.cnt[e]:
                last["es_" + e] = self.cnt[e]
        fwaits = [(self.semobj[k], v) for k, v in last.items()]
        prog = self.prog
        with nc.Block() as block:
            @block.tensor
            def _(e):
                for f in prog["pe"]:
                    f(e)

            @block.scalar
            def _(e):
                for f in prog["act"]:
                    f(e)

            @block.vector
            def _(e):
                for f in prog["dve"]:
                    f(e)

            @block.gpsimd
            def _(e):
                for f in prog["pool"]:
                    f(e)

            @block.sync
            def _(e):
                for f in prog["sp"]:
                    f(e)
                for s, v in fwaits:
                    e.wait_ge(s, v)


def rows_of(tt):
    return 128 if tt < 16 else 16


IN_SPECS = [
    ("x_prompt", [2048, 2048]), ("x_sample", [16, 2048]), ("mem_prompt", [256, 2048]),
    ("cache_a_k", [2, 128, 512]), ("cache_a_v", [2, 128, 512]),
    ("cache_b_ckv", [4096, 256]), ("cache_b_krope", [4096, 64]),
    ("cache_c_k", [512, 1536]), ("cache_c_v", [512, 1536]),
    ("cache_mem_k", [4, 256, 512]), ("cache_mem_v", [4, 256, 512]),
    ("t5_bias", [32, 12]), ("norm_g", [4, 2048]), ("w_out", [4, 2048, 2048]),
    ("mem_norm_g", [4, 2048]), ("w_mem_kv", [4, 2048, 1024]),
    ("xq_norm_g", [4, 128]), ("xk_norm_g", [4, 128]),
    ("a_w_in", [2, 2048, 5120]), ("a_q_norm_g", [2, 128]), ("a_k_norm_g", [2, 128]),
    ("a_sink", [2, 12]),
    ("b_w_in", [1, 2048, 3392]), ("b_cq_norm_g", [1, 512]), ("b_w_q_b", [1, 512, 2304]),
    ("b_ckv_norm_g", [1, 256]), ("b_w_kv_b", [1, 256, 3072]),
    ("b_q_norm_g", [1, 192]), ("b_k_norm_g", [1, 192]),
    ("c_w_in", [1, 2048, 7168]), ("c_q_norm_g", [1, 128]), ("c_k_norm_g", [1, 128]),
    ("c_rel_bias", [1, 12, 257]),
    ("k_ident", [128, 128]), ("k_t5oh", [32, 384]), ("k_cos", [128, 17, 32]), ("k_sin", [128, 17, 32]),
]
OUT_SPECS = [
    ("y_prompt", [2048, 2048]), ("y_sample", [16, 2048]),
    ("a_k_prompt", [2, 128, 512]), ("a_v_prompt", [2, 128, 512]),
    ("a_k_sample", [2, 16, 512]), ("a_v_sample", [2, 16, 512]),
    ("b_ckv_prompt", [2048, 256]), ("b_krope_prompt", [2048, 64]),
    ("b_ckv_sample", [16, 256]), ("b_krope_sample", [16, 64]),
    ("c_k_prompt", [512, 1536]), ("c_v_prompt", [512, 1536]),
    ("c_k_sample", [16, 1536]), ("c_v_sample", [16, 1536]),
    ("mem_k_prompt", [4, 256, 512]), ("mem_v_prompt", [4, 256, 512]),
]


class Prog:
    def __init__(self, n_layers=4, stop=None):
        self.n_layers = n_layers
        self.stop = stop
        self.stopped = False
        nc = self.nc = bass.Bass("TRN2", target_bir_lowering=False)
        self.S = Sched(nc)
        self.din = {n: nc.dram_tensor(n, s, F32, kind="ExternalInput") for n, s in IN_SPECS}
        self.dout = {n: nc.dram_tensor(n, s, F32, kind="ExternalOutput") for n, s in OUT_SPECS}
        self.dres = {}
        self._rr = {}
        self.deferred = []
        self.lag = 1
        self.alloc()
        self.prologue()
        for L in range(n_layers):
            if not self.stopped:
                self.layer(L)
        self.S.emit()

    def chk(self, L, ph):
        if self.stop is not None and self.stop == (L, ph):
            self.stopped = True
        return self.stopped

    def rres(self, name):
        if name not in self.dres:
            self.dres[name] = Res(name)
        return self.dres[name]

    def rot(self, key, n):
        i = self._rr.get(key, 0)
        self._rr[key] = i + 1
        return i % n

    def sb(self, name, shape, dt):
        t = self.nc.alloc_sbuf_tensor(name, shape, dt)
        return t, self.rres("sb_" + name)

    def scr(self, name, shape, dt=BF):
        t = self.nc.dram_tensor(name, shape, dt)
        return t, self.rres("dr_" + name)

    def op(self, eng, name, reads=(), writes=(), adds=(), **kw):
        ex = [r for r in reads if r in self.ps_set]
        if ex:
            reads = [r for r in reads if r not in self.ps_set]
            writes = list(writes) + ex
        return self.S.op(eng, name, reads, writes, adds, **kw)

    def dma(self, out, in_, sres, reads=(), writes=(), adds=(), q="sp", is_output=False):
        return self.S.dma(q, out, in_, sres, reads, writes, adds, is_output)

    def newstat(self):
        i = self.rot("stat", 12)
        return self.stat[i], self.stat_r[i]

    def bank(self, b):
        return self.ps[:, b * 512:(b + 1) * 512]

    BANKS = {"acc": [0, 1, 4, 5, 6, 7], "tp": [2, 3], "s": [0, 1, 4, 5], "o": [6, 7]}

    def nb(self, kind):
        lst = self.BANKS[kind]
        b = lst[self.rot("bank_" + kind, len(lst))]
        return self.bank(b), self.ps_r[b]

    def defer(self, fn, delay=1):
        self._seq = getattr(self, "_seq", 0) + 1
        self.deferred.append([delay, self._seq, fn])

    def step(self):
        due = []
        rest = []
        for it in self.deferred:
            it[0] -= 1
            (due if it[0] <= 0 else rest).append(it)
        self.deferred = rest
        for it in sorted(due, key=lambda t: t[1]):
            it[2]()

    def flush(self):
        while self.deferred:
            self.step()

    def alloc(self):
        nc = self.nc
        self.ps = nc.alloc_psum_tensor("ps", [128, 4096], F32)
        self.ps_r = [Res("bank%d" % i) for i in range(8)]
        self.ps_set = set(self.ps_r)
        self.actT, _ = self.sb("actT", [128, 16, NCOL], BF)
        self.act_r = [Res("act%d" % t) for t in range(TT)]
        self.arena = []
        self.arena_r = []
        for i in range(2):
            t, r = self.sb("arena%d" % i, [128, 6528], F32)
            self.arena.append(t)
            self.arena_r.append(r)
        self.xta, self.xta_r = self.sb("xta", [128, 2 * 2176], F32)
        self.xt_r = [Res("xt0"), Res("xt1")]
        self.cq_r = self.xta_r
        self.Rt, self.R_r = self.sb("Rt", [128, 2176], F32)
        self.Rh_r = [Res("Rh0"), Res("Rh1")]
        self.ckvT, self.ckvT_r = self.sb("ckvT", [128, 2, NCOL], BF)
        self.kr_all, self.kr_all_r = self.sb("kr_all", [128, TT, 64], F32)
        self.krss, self.krss_r = self.sb("krss", [128, TT], F32)
        self.krss_c, self.krss_c_r = self.sb("krss_c", [128, 40], F32)
        self.gbm_r = Res("gbm")
        self.zf = []
        self.zf_r = []
        self.of = []
        self.of_r = []
        self.tb = []
        self.tb_r = []
        self.vb = []
        self.vb_r = []
        self.gb = []
        self.gb_r = []
        self.ld = []
        self.ld_r = []
        self.pexp = []
        self.pexp_r = []
        self.ogf = []
        self.ogf_r = []
        self.ckvt = []
        self.ckvt_r = []
        self.krt = []
        self.krt_r = []
        for i in range(2):
            for lst, rl, nm, shp, dt in (
                (self.zf, self.zf_r, "zf", [128, 512], F32), (self.of, self.of_r, "of", [128, 512], F32),
                (self.tb, self.tb_r, "tb", [128, 4, 128], BF), (self.vb, self.vb_r, "vb", [128, 4, 129], BF),
                (self.gb, self.gb_r, "gb", [128, 512], BF), (self.ld, self.ld_r, "ld", [128, 512], F32),
                (self.pexp, self.pexp_r, "pexp", [128, 512], BF), (self.pexp, self.pexp_r, "pexq", [128, 512], BF),
                (self.ogf, self.ogf_r, "ogf", [128, 128], F32), (self.ckvt, self.ckvt_r, "ckvt", [128, 2, 128], BF),
                (self.krt, self.krt_r, "krt", [128, 64], F32),
            ):
                t, r = self.sb("%s%d" % (nm, i), shp, dt)
                lst.append(t)
                rl.append(r)
        for lst, rl, nm, shp, dt in ((self.tb, self.tb_r, "tb", [128, 4, 128], BF), (self.vb, self.vb_r, "vb", [128, 4, 129], BF)):
            t, r = self.sb("%s2" % nm, shp, dt)
            lst.append(t)
            rl.append(r)
        t, r = self.sb("krt2", [128, 64], F32)
        self.krt.append(t)
        self.krt_r.append(r)
        self.sq, self.sq_r = self.sb("sq", [128, 512], F32)
        self.rp, self.rp_r = self.sb("rp", [128, 6, 64], F32)
        self.mkT, self.mkT_r = self.sb("mkT", [128, 4, 256], BF)
        self.mv, self.mv_r = self.sb("mv", [128, 2, 4, 129], BF)
        self.mkTs, self.mkTs_r = self.sb("mkTs", [128, 4, 256], BF)
        self.mvs, self.mvs_r = self.sb("mvs", [128, 2, 4, 129], BF)
        self.kcA, self.kcA_r = self.sb("kcA", [128, 4, 128], BF)
        self.vcA, self.vcA_r = self.sb("vcA", [128, 4, 129], BF)
        self.kcC, self.kcC_r = self.sb("kcC", [128, 512], BF)
        self.vcC, self.vcC_r = self.sb("vcC", [128, 4, 129], BF)
        self.gcolA, self.gcolA_r = self.sb("gcolA", [128, 128], F32)
        self.gcolB, self.gcolB_r = self.sb("gcolB", [128, 32], F32)
        self.gbc, self.gbc_r = self.sb("gbc", [128, 384], F32)
        self.esink, self.esink_r = self.sb("esink", [128, 24], F32)
        self.ident, self.ident_r = self.sb("ident", [128, 128], F32)
        self.cosT, self.cos_r = self.sb("cosT", [128, TT, 32], F32)
        self.sinT, self.sin_r = self.sb("sinT", [128, TT, 32], F32)
        self.cm05, self.cm05_r = self.sb("cm05", [128, 4], F32)
        self.stat = []
        self.stat_r = []
        for i in range(12):
            t, r = self.sb("stat%d" % i, [128, 16], F32)
            self.stat.append(t)
            self.stat_r.append(r)
        self.xres = [self.scr("xres%d" % i, [2064, 2048], F32) for i in range(2)]
        self.qT_s = self.scr("qT_s", [12, 128, NCOL])
        self.q192_s = self.scr("q192_s", [12, 192, NCOL])
        self.k192_s = self.scr("k192_s", [12, 192, NCOL])
        self.k192s_s = self.scr("k192s_s", [12, 192, 4224])
        self.xqT_s = self.scr("xqT_s", [4, 128, NCOL])
        self.kT_s = self.scr("kT_s", [12, 128, NCOL])
        self.v_s = self.scr("v_s", [12, TT, 128, 129])
        self.gate_s = self.scr("gate_s", [NCOL, 2048])
        self.vs_s = self.scr("vs_s", [12, 33, 128, 129])
        self.FrepA = self.scr("FrepA", [12, 128, 384], F32)
        self.FrepC = self.scr("FrepC", [12, 128, 768], F32)

    def act_ap(self, tt, k):
        return self.actT[:, k, tt * 128: tt * 128 + rows_of(tt)]

    def xt(self, i):
        return self.xta[:, i * 2176: i * 2176 + 2048]

    def cqT(self):
        return self.xta[:, :].bitcast(BF).rearrange("p (k c) -> p k c", k=4)

    def memT(self):
        return self.Rt[:, 0:2048].bitcast(BF).rearrange("p (k c) -> p k c", k=16)

    def Rbf(self):
        return self.Rt[:, :].bitcast(BF)

    def arena_bf(self, i):
        return self.arena[i][:, :].bitcast(BF)

    def wview(self, i, nk, w):
        return self.arena_bf(i)[:, 0:nk * w].rearrange("p (k w) -> p k w", k=nk)

    def prologue(self):
        din = self.din
        self.dma(self.ident[:], din["k_ident"].ap(), self.ident_r, writes=[self.ident_r])
        self.dma(self.cosT[:], din["k_cos"].ap(), self.cos_r, writes=[self.cos_r])
        self.dma(self.sinT[:], din["k_sin"].ap(), self.sin_r, writes=[self.sin_r])
        self.op("pool", "memset", writes=[self.cm05_r], ap=self.cm05[:], constant=-0.5)
        for i in range(len(self.vb)):
            self.op("pool", "memset", writes=[self.vb_r[i]], ap=self.vb[i][:], constant=1.0)
        for t, r in ((self.mv, self.mv_r), (self.mvs, self.mvs_r)):
            self.op("pool", "memset", writes=[r], ap=t[:], constant=1.0)
        for t, r in ((self.vcA, self.vcA_r), (self.vcC, self.vcC_r)):
            self.op("pool", "memset", writes=[r], ap=t[:], constant=1.0)
        ga = self.ld[0]
        gar = self.ld_r[0]
        self.dma(ga[0:64, 0:128], din["norm_g"].ap().rearrange("l (k c) -> (l k) c", c=128), gar, adds=[gar])
        self.dma(ga[64:128, 0:128], din["mem_norm_g"].ap().rearrange("l (k c) -> (l k) c", c=128), gar, adds=[gar])
        gb_ = self.ld[1]
        gbr = self.ld_r[1]
        self.op("dve", "memset", writes=[gbr, self.gbm_r], ap=gb_[0:32, 0:128], constant=0.0)
        rows = [("xq_norm_g", 0, 4, None), ("xk_norm_g", 4, 4, None), ("a_q_norm_g", 8, 2, None),
                ("a_k_norm_g", 10, 2, None), ("c_q_norm_g", 12, 1, None), ("c_k_norm_g", 13, 1, None)]
        for nm, r0, n, _ in rows:
            self.dma(gb_[r0:r0 + n, 0:128], din[nm].ap(), gbr, reads=[self.gbm_r], adds=[gbr])
        self.dma(gb_[14:18, 0:128], din["b_cq_norm_g"].ap().rearrange("o (k c) -> (o k) c", c=128), gbr, reads=[self.gbm_r], adds=[gbr])
        self.dma(gb_[18:20, 0:128], din["b_ckv_norm_g"].ap().rearrange("o (k c) -> (o k) c", c=128), gbr, reads=[self.gbm_r], adds=[gbr])
        self.dma(gb_[20:21, 0:128], din["b_q_norm_g"].ap()[:, 0:128], gbr, reads=[self.gbm_r], adds=[gbr])
        self.dma(gb_[21:22, 0:64], din["b_q_norm_g"].ap()[:, 128:192], gbr, reads=[self.gbm_r], adds=[gbr])
        self.dma(gb_[22:23, 0:128], din["b_k_norm_g"].ap()[:, 0:128], gbr, reads=[self.gbm_r], adds=[gbr])
        self.dma(gb_[23:24, 0:64], din["b_k_norm_g"].ap()[:, 128:192], gbr, reads=[self.gbm_r], adds=[gbr])
        tpb, tpr = self.nb("tp")
        self.op("pe", "transpose", reads=[gar, self.ident_r], writes=[tpr], out=tpb[:, 0:128], in_=ga[:, 0:128],
                identity=self.ident[:])
        self.op("act", "activation", reads=[tpr], writes=[self.gcolA_r], out=self.gcolA[:], in_=tpb[:, 0:128], func=AF.Copy)
        tpb, tpr = self.nb("tp")
        self.op("pe", "transpose", reads=[gbr, self.ident_r], writes=[tpr], out=tpb[:, 0:32], in_=gb_[0:32, 0:128],
                identity=self.ident[0:32, 0:32])
        self.op("act", "activation", reads=[tpr], writes=[self.gcolB_r], out=self.gcolB[:], in_=tpb[:, 0:32], func=AF.Copy)
        self.dma(self.esink[:], din["a_sink"].ap().rearrange("a h -> (a h)").partition_broadcast(128), self.esink_r,
                 writes=[self.esink_r])
        self.op("act", "activation", reads=[self.esink_r], writes=[self.esink_r], out=self.esink[:], in_=self.esink[:],
                func=AF.Exp)
        t5 = self.zf[0]
        t5r = self.zf_r[0]
        oh = self.of[0]
        ohr = self.of_r[0]
        self.dma(t5[0:32, 0:12], din["t5_bias"].ap(), t5r, writes=[t5r])
        self.dma(oh[0:32, 0:384], din["k_t5oh"].ap(), ohr, writes=[ohr])
        ab, ar = self.nb("acc")
        self.op("pe", "matmul", reads=[t5r, ohr], writes=[ar], out=ab[0:12, 0:384], lhsT=t5[0:32, 0:12],
                rhs=oh[0:32, 0:384], start=True, stop=True)
        fa = self.zf[1]
        far = self.zf_r[1]
        self.op("dve", "tensor_copy", reads=[ar], writes=[far], out=fa[0:12, 0:384], in_=ab[0:12, 0:384])
        FA, FAr = self.FrepA
        self.dma(FA.ap(), fa[0:12, 0:384].unsqueeze(1).broadcast_to([12, 128, 384]), far, reads=[far], writes=[FAr])
        fc = self.xta
        fcr = self.xt_r[0]
        self.dma(fc[0:12, 0:257], din["c_rel_bias"].ap()[0], fcr, writes=[fcr])
        self.op("dve", "tensor_copy", reads=[fcr], writes=[fcr], out=fc[0:12, 257:768],
                in_=fc[0:12, 256:257].broadcast_to([12, 511]))
        FC, FCr = self.FrepC
        self.dma(FC.ap(), fc[0:12, 0:768].unsqueeze(1).broadcast_to([12, 128, 768]), fcr, reads=[fcr], writes=[FCr])

    def norm_transpose(self, srcs, gcol0, dst_fn, dres_fn):
        info = {}

        def stage_a1(idx):
            src, rows, key = srcs[idx]
            i = self.rot("xt", 2)
            xt = self.xt(i)
            xr = self.xt_r[i]
            self.dma(xt[0:rows, :], src, xr, writes=[xr, self.xta_r])
            st, sr = self.newstat()
            for c in range(4):
                self.op("act", "activation", reads=[xr], writes=[self.ps_r[4 + c], sr], out=self.bank(4 + c)[0:rows, :],
                        in_=xt[0:rows, c * 512:(c + 1) * 512], func=AF.Square, accum_out=st[0:rows, 4 + c:5 + c])
            info[idx] = (xt, xr, st, sr)

        def stage_a2(idx):
            src, rows, key = srcs[idx]
            xt, xr, st, sr = info[idx]
            self.op("dve", "tensor_reduce", reads=[sr], writes=[sr], out=st[0:rows, 0:1], in_=st[0:rows, 4:8], axis=AX.X,
                    op=ALU.add)
            self.op("dve", "tensor_scalar", reads=[sr], writes=[sr], out=st[0:rows, 1:2], in0=st[0:rows, 0:1],
                    scalar1=1.0 / D, scalar2=EPS, op0=ALU.mult, op1=ALU.add)
            self.op("pool", "tensor_tensor", reads=[sr, self.cm05_r], writes=[sr], out=st[0:rows, 2:3],
                    in0=st[0:rows, 1:2], in1=self.cm05[0:rows, 0:1], op=ALU.pow)

        def stage_b(idx):
            src, rows, key = srcs[idx]
            xt, xr, st, sr = info.pop(idx)
            self.op("dve", "tensor_scalar", reads=[sr, xr], writes=[xr], out=xt[0:rows, :], in0=xt[0:rows, :],
                    scalar1=st[0:rows, 2:3], scalar2=None, op0=ALU.mult)
            for j in range(4):
                tpb, tpr = self.nb("tp")
                tpv = tpb.rearrange("p (a b) -> p a b", a=4)
                for c in range(4):
                    k = 4 * j + c
                    self.op("pe", "transpose", reads=[xr, self.ident_r, self.xta_r], writes=[tpr], out=tpv[:, c, 0:rows],
                            in_=xt[0:rows, k * 128:(k + 1) * 128], identity=self.ident[0:rows, 0:rows])
                if j == 3:
                    for c in range(4):
                        k = 4 * j + c
                        self.op("act", "activation", reads=[tpr, self.gcolA_r], writes=[dres_fn(key)],
                                out=dst_fn(key, j)[:, c, :], in_=tpv[:, c, 0:rows], func=AF.Copy,
                                scale=self.gcolA[:, gcol0 + k: gcol0 + k + 1])
                else:
                    g = self.gcolA[:, gcol0 + 4 * j: gcol0 + 4 * j + 4].unsqueeze(2).broadcast_to([128, 4, rows])
                    self.op("dve", "tensor_tensor", reads=[tpr, self.gcolA_r], writes=[dres_fn(key)], out=dst_fn(key, j),
                            in0=tpv[:, 0:4, 0:rows], in1=g, op=ALU.mult)

        n = len(srcs)
        stage_a1(0)
        stage_a2(0)
        for idx in range(n):
            if idx + 1 < n:
                stage_a1(idx + 1)
            stage_b(idx)
            if idx + 1 < n:
                stage_a2(idx + 1)

    def proj(self, W, wcols, nk, blocks, tiles, act_fn, epilogue, tile_outer=False):
        wh, woff = W
        if tile_outer:
            ai = self.rot("arena", 2)
            ar = self.arena_r[ai]
            tot = sum(w for _, w in blocks)
            wv = self.arena_bf(ai)[:, 0:nk * tot].rearrange("p (k w) -> p k w", k=nk)
            pos = 0
            wpos = []
            for (c0, w) in blocks:
                src = AP(wh, woff + c0, [[wcols, 128], [128 * wcols, nk], [1, w]])
                self.dma(wv[:, :, pos:pos + w], src, ar, adds=[ar], q="pool")
                wpos.append(pos)
                pos += w
            for (key, rows, rd) in tiles:
                for bi, (c0, w) in enumerate(blocks):
                    acc, accr = self.nb("acc")
                    for k in range(nk):
                        self.op("pe", "matmul", reads=[ar] + rd, writes=[accr], out=acc[0:rows, 0:w],
                                lhsT=act_fn(key, k), rhs=wv[:, k, wpos[bi]:wpos[bi] + w], start=(k == 0), stop=(k == nk - 1))
                    self.step()
                    epilogue(key, rows, bi, acc[0:rows, 0:w], accr)
            self.flush()
            return
        def issue_w(bi):
            c0, w = blocks[bi]
            ai = self.rot("arena", 2)
            ar = self.arena_r[ai]
            wv = self.wview(ai, nk, w)
            src = AP(wh, woff + c0, [[wcols, 128], [128 * wcols, nk], [1, w]])
            self.dma(wv, src, ar, writes=[ar], q="pool")
            return ar, wv

        cur = issue_w(0)
        for bi, (c0, w) in enumerate(blocks):
            nxt = issue_w(bi + 1) if bi + 1 < len(blocks) else None
            ar, wv = cur
            cur = nxt
            for (key, rows, rd) in tiles:
                acc, accr = self.nb("acc")
                for k in range(nk):
                    self.op("pe", "matmul", reads=[ar] + rd, writes=[accr], out=acc[0:rows, 0:w],
                            lhsT=act_fn(key, k), rhs=wv[:, k, 0:w], start=(k == 0), stop=(k == nk - 1))
                self.step()
                epilogue(key, rows, bi, acc[0:rows, 0:w], accr)
        self.flush()

    def headnorm(self, acc, accr, rows, nh, hd, extra_ss=None):
        w = nh * hd
        self.op("act", "activation", reads=[accr], writes=[self.sq_r], out=self.sq[0:rows, 0:w], in_=acc, func=AF.Square)
        st, sr = self.newstat()
        self.op("dve", "tensor_reduce", reads=[self.sq_r], writes=[sr], out=st[0:rows, 0:nh],
                in_=self.sq[0:rows, 0:w].rearrange("p (a b) -> p a b", a=nh), axis=AX.X, op=ALU.add)
        self.op("dve", "tensor_scalar", reads=[sr], writes=[sr], out=st[0:rows, 4:4 + nh], in0=st[0:rows, 0:nh],
                scalar1=1.0 / hd, scalar2=EPS, op0=ALU.mult, op1=ALU.add)
        self.op("pool", "tensor_tensor", reads=[sr, self.cm05_r], writes=[sr], out=st[0:rows, 8:8 + nh],
                in0=st[0:rows, 4:4 + nh], in1=self.cm05[0:rows, 0:nh], op=ALU.pow)
        i = self.rot("zf", 2)
        zf = self.zf[i]
        zr = self.zf_r[i]
        zv = zf[0:rows, 0:w].rearrange("p (a b) -> p a b", a=nh)
        self.defer(lambda: self.op("dve", "tensor_tensor", reads=[accr, sr], writes=[zr], out=zv,
                                   in0=acc.rearrange("p (a b) -> p a b", a=nh),
                                   in1=st[0:rows, 8:8 + nh].unsqueeze(2).broadcast_to([rows, nh, hd]), op=ALU.mult), 1)
        return zv, zr

    def transpose_out(self, zv, zr, rows, nh, gcol, dst_sb=None, dst_res=None, dst_dram=None, dram_res=None):
        self.defer(lambda: self._transpose_out(zv, zr, rows, nh, gcol, dst_sb, dst_res, dst_dram, dram_res), 3)

    def _transpose_out(self, zv, zr, rows, nh, gcol, dst_sb, dst_res, dst_dram, dram_res):
        tpb, tpr = self.nb("tp")
        tpv = tpb.rearrange("p (a b) -> p a b", a=4)
        for c in range(nh):
            self.op("pe", "transpose", reads=[zr, self.ident_r], writes=[tpr], out=tpv[:, c, 0:rows], in_=zv[:, c, :],
                    identity=self.ident[0:rows, 0:rows])
        if dst_sb is not None:
            self.op("act", "activation", reads=[tpr, self.gcolB_r], writes=[dst_res], out=dst_sb, in_=tpv[:, 0:nh, 0:rows],
                    func=AF.Copy, scale=gcol)
            return
        i = self.rot("tb", 3)
        tb = self.tb[i]
        tr = self.tb_r[i]
        self.op("act", "activation", reads=[tpr, self.gcolB_r], writes=[tr], out=tb[:, 0:nh, 0:rows],
                in_=tpv[:, 0:nh, 0:rows], func=AF.Copy, scale=gcol)
        self.dma(dst_dram, tb[:, 0:nh, 0:rows], tr, reads=[tr], adds=[dram_res])

    def rows_out(self, src, src_res, rows, w, dst):
        i = self.rot("of", 2)
        of = self.of[i]
        orr = self.of_r[i]
        self.op("dve", "tensor_copy", reads=[src_res], writes=[orr], out=of[0:rows, 0:w], in_=src)
        self.dma(dst, of[0:rows, 0:w], orr, reads=[orr], is_output=True)

    def gain_rows_out(self, zv, zr, rows, nh, g_ap, dst):
        self.defer(lambda: self._gain_rows_out(zv, zr, rows, nh, g_ap, dst), 1)

    def _gain_rows_out(self, zv, zr, rows, nh, g_ap, dst):
        i = self.rot("of", 2)
        of = self.of[i]
        orr = self.of_r[i]
        self.op("dve", "tensor_tensor", reads=[zr, self.gbc_r], writes=[orr],
                out=of[0:rows, 0:nh * 128].rearrange("p (a b) -> p a b", a=nh), in0=zv,
                in1=g_ap.unsqueeze(1).broadcast_to([rows, nh, 128]), op=ALU.mult)
        self.dma(dst, of[0:rows, 0:nh * 128], orr, reads=[orr], is_output=True)

    def v_out(self, accv, accr, rows, nh, dst_dram, dram_res, dst_sb=None, dst_res=None):
        if dst_sb is not None:
            self.op("act", "activation", reads=[accr], writes=[dst_res], out=dst_sb, in_=accv, func=AF.Copy)
            return
        i = self.rot("vb", 3)
        vb = self.vb[i]
        vr = self.vb_r[i]
        self.op("act", "activation", reads=[accr], writes=[vr], out=vb[0:rows, 0:nh, 0:128], in_=accv, func=AF.Copy)
        self.dma(dst_dram, vb[0:rows, 0:nh, :], vr, reads=[vr], adds=[dram_res])

    def gate_out(self, acc, accr, rows, w, tt, gc0):
        i = self.rot("gb", 2)
        gb = self.gb[i]
        gr = self.gb_r[i]
        self.op("act", "activation", reads=[accr], writes=[gr], out=gb[0:rows, 0:w], in_=acc, func=AF.Silu)
        G, Gr = self.gate_s
        self.dma(AP(G, tt * 128 * 2048 + gc0, [[2048, rows], [1, w]]), gb[0:rows, 0:w], gr, reads=[gr], adds=[Gr])

    def qk_store(self, scr, h0, nh, tt, rows, width=NCOL, dpart=128, hrows=None, r0=0):
        t, _ = scr
        hrows = hrows or dpart
        return AP(t, (h0 * hrows + r0) * width + tt * 128, [[width, dpart], [hrows * width, nh], [1, rows]])

    def v_store(self, scr, h0, nh, tile, rows, ntiles=TT):
        t, _ = scr
        return AP(t, (h0 * ntiles + tile) * 128 * 129, [[129, rows], [ntiles * 128 * 129, nh], [1, 129]])

    def attn(self, kts, nq, scale, sink_ap, gate_ap, gate_reads, dst_ap, dst_res):
        chunks = []
        cur = []
        for kt in kts:
            if cur:
                p = cur[-1]
                brk = (len(cur) * nq >= 512 or kt["nk"] != p["nk"] or (kt["bias"] is None) != (p["bias"] is None)
                       or (kt["bias"] is not None and kt["bias"][1] != p["bias"][1] + 1))
                if brk:
                    chunks.append(cur)
                    cur = []
            cur.append(kt)
        chunks.append(cur)
        assert len(chunks) <= 4
        ob, orr = self.nb("o")
        staged = []
        for ci, ch in enumerate(chunks):
            sbk, sr = self.nb("s")
            nk = ch[0]["nk"]
            n = len(ch)
            for i, kt in enumerate(ch):
                m = len(kt["qk"])
                for pi, (l, r) in enumerate(kt["qk"]):
                    self.op("pe", "matmul", reads=kt["reads"], writes=[sr], out=sbk[0:nk, i * nq:(i + 1) * nq], lhsT=l, rhs=r,
                            start=(pi == 0), stop=(pi == m - 1))
            if ci == 0:
                self.step()
            ip = self.rot("pexp", 4)
            pe_t = self.pexp[ip]
            per = self.pexp_r[ip]
            if ch[0]["bias"] is not None:
                j = self.rot("zf", 2)
                tm = self.zf[j]
                tmr = self.zf_r[j]
                self.op("act", "activation", reads=[sr], writes=[tmr], out=tm[0:nk, 0:n * nq], in_=sbk[0:nk, 0:n * nq],
                        func=AF.Exp, scale=scale)
                bv, s0 = ch[0]["bias"]
                self.op("dve", "tensor_tensor", reads=[tmr] + ch[0]["reads"], writes=[per],
                        out=pe_t[0:nk, 0:n * nq].rearrange("p (a b) -> p a b", a=n),
                        in0=tm[0:nk, 0:n * nq].rearrange("p (a b) -> p a b", a=n), in1=bv[0:nk, s0:s0 + n, 0:nq], op=ALU.mult)
            else:
                self.op("act", "activation", reads=[sr], writes=[per], out=pe_t[0:nk, 0:n * nq], in_=sbk[0:nk, 0:n * nq],
                        func=AF.Exp, scale=scale)
            for i, kt in enumerate(ch):
                if kt.get("diag"):
                    self.op("pool", "memset", writes=[per], ap=pe_t[64:128, i * nq:i * nq + 64], constant=0.0)
            staged.append((ch, pe_t, per, nk))
        tot = len(kts)

        def part2():
            idx = 0
            for ch, pe_t, per, nk in staged:
                for i, kt in enumerate(ch):
                    self.op("pe", "matmul", reads=[per] + kt["reads"], writes=[orr], out=ob[0:nq, 0:129],
                            lhsT=pe_t[0:nk, i * nq:(i + 1) * nq], rhs=kt["v"], start=(idx == 0), stop=(idx == tot - 1))
                    idx += 1
            self.attn_post(ob, orr, nq, sink_ap, gate_ap, gate_reads, dst_ap, dst_res, tail_delay=(1 if dfr else 2))

        dfr = len(chunks) <= 2
        if dfr:
            self.defer(part2, 1)
        else:
            part2()

    def attn_post(self, ob, orr, nq, sink_ap, gate_ap, gate_reads, dst_ap, dst_res, tail_delay=2):
        st, sr = self.newstat()
        if sink_ap is not None:
            self.op("dve", "tensor_tensor", reads=[orr, self.esink_r], writes=[sr], out=st[0:nq, 0:1],
                    in0=ob[0:nq, 128:129], in1=sink_ap, op=ALU.add)
            self.op("dve", "reciprocal", reads=[sr], writes=[sr], out=st[0:nq, 1:2], in_=st[0:nq, 0:1])
        else:
            self.op("dve", "reciprocal", reads=[orr], writes=[sr], out=st[0:nq, 1:2], in_=ob[0:nq, 128:129])
        i = self.rot("ogf", 2)
        og = self.ogf[i]
        ogr = self.ogf_r[i]
        self.op("dve", "scalar_tensor_tensor", reads=[orr, sr] + gate_reads, writes=[ogr], out=og[0:nq, :],
                in0=ob[0:nq, 0:128], scalar=st[0:nq, 1:2], in1=gate_ap, op0=ALU.mult, op1=ALU.mult)

        def tail():
            tpb, tpr = self.nb("tp")
            self.op("pe", "transpose", reads=[ogr, self.ident_r], writes=[tpr], out=tpb[:, 0:nq], in_=og[0:nq, :],
                    identity=self.ident[0:nq, 0:nq])
            self.op("act", "activation", reads=[tpr], writes=[dst_res], out=dst_ap, in_=tpb[:, 0:nq], func=AF.Copy)
        self.defer(tail, tail_delay)

    A_Q, A_G, A_K, A_V, A_B = 0, 2176, 4352, 6528, 8736

    def load_head(self, ai, q_src, gate_col, k_src, v_src, kw=NCOL, with_kv=True):
        ar = self.arena_r[ai]
        ab = self.arena_bf(ai)
        qh, qoff, qres = q_src
        self.dma(ab[:, self.A_Q:self.A_Q + 2064], AP(qh, qoff, [[NCOL, 128], [1, 2064]]), ar, reads=[qres], adds=[ar])
        G, Gr = self.gate_s
        gv = ab[:, self.A_G:self.A_G + 2176].rearrange("p (t c) -> p t c", t=TT)
        self.dma(gv[:, 0:16, :], AP(G, gate_col, [[2048, 128], [128 * 2048, 16], [1, 128]]), ar, reads=[Gr], adds=[ar])
        self.dma(gv[0:16, 16, :], AP(G, 2048 * 2048 + gate_col, [[2048, 16], [1, 128]]), ar, reads=[Gr], adds=[ar])
        if with_kv:
            kh, koff, kres = k_src
            self.dma(ab[:, self.A_K:self.A_K + 2064], AP(kh, koff, [[kw, 128], [1, 2064]]), ar, reads=[kres], adds=[ar])
            vh, voff, vres = v_src
            vv = ab[:, self.A_V:self.A_V + TT * 129].rearrange("p (t c) -> p t c", t=TT)
            self.dma(vv[:, 0:16, :], AP(vh, voff, [[129, 128], [128 * 129, 16], [1, 129]]), ar, reads=[vres], adds=[ar])
            self.dma(vv[0:16, 16, :], AP(vh, voff + 16 * 128 * 129, [[129, 16], [1, 129]]), ar, reads=[vres], adds=[ar])
        return ab, gv

    def bias_view(self, ai, n):
        f0 = self.A_B // 2
        return self.arena[ai][:, f0:f0 + n * 128].rearrange("p (a b) -> p a b", a=n)

    def load_bias(self, ai, Frep, L, h, offs_nk_nq):
        ar = self.arena_r[ai]
        bv = self.bias_view(ai, len(offs_nk_nq))
        Ft, Fr = Frep
        for i, (off, nk, nq) in enumerate(offs_nk_nq):
            self.dma(bv[0:nk, i, 0:nq], AP(Ft, h * 128 * L + off, [[L - 1, nk], [1, nq]]), ar, reads=[Fr], adds=[ar])
        return bv

    def layer(self, L):
        kind, j = L % 3, L // 3
        din = self.din
        if kind == 0:
            self.dma(self.gbc[:, 0:128], din["a_k_norm_g"].ap()[j].partition_broadcast(128), self.gbc_r, writes=[self.gbc_r])
        elif kind == 1:
            self.dma(self.gbc[:, 0:256], din["b_ckv_norm_g"].ap()[0].partition_broadcast(128), self.gbc_r, writes=[self.gbc_r])
        else:
            self.dma(self.gbc[:, 0:128], din["c_k_norm_g"].ap()[0].partition_broadcast(128), self.gbc_r, writes=[self.gbc_r])
        self.dma(self.gbc[:, 256:384], din["xk_norm_g"].ap()[L].partition_broadcast(128), self.gbc_r, adds=[self.gbc_r])
        if L == 0:
            xp = din["x_prompt"]
            xs = din["x_sample"]
            srcs = [(AP(xp, t * 128 * D, [[D, 128], [1, D]]), 128, t) for t in range(16)]
            srcs.append((AP(xs, 0, [[D, 16], [1, D]]), 16, 16))
            xrd = []
        else:
            xh, xr_ = self.xres[(L - 1) % 2]
            srcs = [(AP(xh, t * 128 * D, [[D, rows_of(t)], [1, D]]), rows_of(t), t) for t in range(TT)]
            xrd = [xr_]
        self._xsrc = srcs
        self.norm_transpose_x(srcs, xrd, L * 16)
        if self.chk(L, "N"):
            return
        self.mem_phase(L)
        if self.chk(L, "M"):
            return
        if kind == 0:
            self.layer_a(L, j)
        elif kind == 1:
            self.layer_b(L)
        else:
            self.layer_c(L)
        if self.stopped or self.chk(L, "T"):
            return
        self.mem_heads(L)
        if self.chk(L, "X"):
            return
        self.out_phase(L)

    def norm_transpose_x(self, srcs, xrd, gcol0):
        S = self
        orig = S.dma

        def dst(key, jj):
            rows = rows_of(key)
            return S.actT[:, 4 * jj:4 * jj + 4, key * 128: key * 128 + rows]

        if xrd:
            def dma2(out, in_, sres, reads=(), writes=(), adds=(), q="sp", is_output=False):
                return orig(out, in_, sres, reads=list(reads) + xrd, writes=writes, adds=adds, q=q, is_output=is_output)
            S.dma = dma2
        try:
            S.norm_transpose(srcs, gcol0, dst, lambda key: S.act_r[key])
        finally:
            S.dma = orig

    def mem_phase(self, L):
        din = self.din
        mp = din["mem_prompt"]
        srcs = [(AP(mp, t * 128 * D, [[D, 128], [1, D]]), 128, t) for t in range(2)]
        memT = self.memT()
        self.norm_transpose(srcs, 64 + L * 16, lambda key, jj: memT[:, 4 * jj:4 * jj + 4, key * 128:(key + 1) * 128],
                            lambda key: self.R_r)
        if self.chk(L, "M1"):
            return
        mko = self.dout["mem_k_prompt"]
        mvo = self.dout["mem_v_prompt"]

        dbg = "abcde"

        def ep(key, rows, bi, acc, accr):
            if bi == 0:
                if "a" not in dbg:
                    return
                zv, zr = self.headnorm(acc, accr, rows, 4, 128)
                if "b" in dbg:
                    self.transpose_out(zv, zr, rows, 4, self.gcolB[:, 4 + L:5 + L],
                                       dst_sb=self.mkT[:, :, key * 128:(key + 1) * 128], dst_res=self.mkT_r)
                if "c" in dbg:
                    self.gain_rows_out(zv, zr, rows, 4, self.gbc[0:rows, 256:384],
                                       AP(mko, L * 256 * 512 + key * 128 * 512, [[512, rows], [1, 512]]))
            else:
                accv = acc.rearrange("p (a b) -> p a b", a=4)
                if "d" in dbg:
                    self.v_out(accv, accr, rows, 4, None, None, dst_sb=self.mv[:, key, :, 0:128], dst_res=self.mv_r)
                if "e" in dbg:
                    self.rows_out(acc, accr, rows, 512, AP(mvo, L * 256 * 512 + key * 128 * 512, [[512, rows], [1, 512]]))

        self.proj((din["w_mem_kv"], L * 2048 * 1024), 1024, 16, [(0, 512), (512, 512)],
                  [(t, 128, [self.R_r]) for t in range(2)], lambda key, k: memT[:, k, key * 128:(key + 1) * 128], ep)
        if self.chk(L, "M2"):
            return
        ck = din["cache_mem_k"]
        cv = din["cache_mem_v"]
        for t in range(2):
            i = self.rot("ld", 2)
            ld = self.ld[i]
            lr = self.ld_r[i]
            self.dma(ld[:, :], AP(ck, L * 256 * 512 + t * 128 * 512, [[512, 128], [1, 512]]), lr, writes=[lr])
            tpb, tpr = self.nb("tp")
            tpv = tpb.rearrange("p (a b) -> p a b", a=4)
            for c in range(4):
                self.op("pe", "transpose", reads=[lr, self.ident_r], writes=[tpr], out=tpv[:, c, :],
                        in_=ld[:, c * 128:(c + 1) * 128], identity=self.ident[:])
            self.op("act", "activation", reads=[tpr], writes=[self.mkTs_r], out=self.mkTs[:, :, t * 128:(t + 1) * 128],
                    in_=tpv[:, 0:4, :], func=AF.Copy)
            i = self.rot("ld", 2)
            ld = self.ld[i]
            lr = self.ld_r[i]
            self.dma(ld[:, :], AP(cv, L * 256 * 512 + t * 128 * 512, [[512, 128], [1, 512]]), lr, writes=[lr])
            self.op("dve", "tensor_copy", reads=[lr], writes=[self.mvs_r], out=self.mvs[:, t, :, 0:128],
                    in_=ld[:, :].rearrange("p (a b) -> p a b", a=4))

    def x_tiles(self):
        return [(t, rows_of(t), [self.act_r[t]]) for t in range(TT)]

    def layer_a(self, L, j):
        din = self.din
        dout = self.dout
        gq = self.gcolB[:, 8 + j:9 + j]
        gk = self.gcolB[:, 10 + j:11 + j]
        gxq = self.gcolB[:, L:L + 1]
        blocks = [(c * 512, 512) for c in range(10)]

        def ep(tt, rows, bi, acc, accr):
            if bi < 3:
                zv, zr = self.headnorm(acc, accr, rows, 4, 128)
                self.transpose_out(zv, zr, rows, 4, gq, dst_dram=self.qk_store(self.qT_s, bi * 4, 4, tt, rows),
                                   dram_res=self.qT_s[1])
            elif bi == 3:
                zv, zr = self.headnorm(acc, accr, rows, 4, 128)
                self.transpose_out(zv, zr, rows, 4, gk, dst_dram=self.qk_store(self.kT_s, 0, 4, tt, rows),
                                   dram_res=self.kT_s[1])
                if tt == 15:
                    self.gain_rows_out(zv, zr, rows, 4, self.gbc[0:rows, 0:128],
                                       AP(dout["a_k_prompt"], j * 128 * 512, [[512, 128], [1, 512]]))
                elif tt == 16:
                    self.gain_rows_out(zv, zr, rows, 4, self.gbc[0:rows, 0:128],
                                       AP(dout["a_k_sample"], j * 16 * 512, [[512, 16], [1, 512]]))
            elif bi == 4:
                accv = acc.rearrange("p (a b) -> p a b", a=4)
                self.v_out(accv, accr, rows, 4, self.v_store(self.v_s, 0, 4, tt, rows), self.v_s[1])
                if tt == 15:
                    self.rows_out(acc, accr, rows, 512, AP(dout["a_v_prompt"], j * 128 * 512, [[512, 128], [1, 512]]))
                elif tt == 16:
                    self.rows_out(acc, accr, rows, 512, AP(dout["a_v_sample"], j * 16 * 512, [[512, 16], [1, 512]]))
            elif bi == 5:
                zv, zr = self.headnorm(acc, accr, rows, 4, 128)
                self.transpose_out(zv, zr, rows, 4, gxq, dst_dram=self.qk_store(self.xqT_s, 0, 4, tt, rows),
                                   dram_res=self.xqT_s[1])
            else:
                self.gate_out(acc, accr, rows, 512, tt, (bi - 6) * 512)

        self.proj((din["a_w_in"], j * 2048 * 5120), 5120, 16, blocks, self.x_tiles(), self.act_ap, ep)
        if self.chk(L, "P"):
            return
        i = self.rot("ld", 2)
        ld = self.ld[i]
        lr = self.ld_r[i]
        self.dma(ld[:, :], AP(din["cache_a_k"], j * 128 * 512, [[512, 128], [1, 512]]), lr, writes=[lr])
        tpb, tpr = self.nb("tp")
        tpv = tpb.rearrange("p (a b) -> p a b", a=4)
        for c in range(4):
            self.op("pe", "transpose", reads=[lr, self.ident_r], writes=[tpr], out=tpv[:, c, :],
                    in_=ld[:, c * 128:(c + 1) * 128], identity=self.ident[:])
        self.op("act", "activation", reads=[tpr], writes=[self.kcA_r], out=self.kcA[:], in_=tpv[:, 0:4, :], func=AF.Copy)
        i = self.rot("ld", 2)
        ld = self.ld[i]
        lr = self.ld_r[i]
        self.dma(ld[:, :], AP(din["cache_a_v"], j * 128 * 512, [[512, 128], [1, 512]]), lr, writes=[lr])
        self.op("dve", "tensor_copy", reads=[lr], writes=[self.vcA_r], out=self.vcA[:, :, 0:128],
                in_=ld[:, :].rearrange("p (a b) -> p a b", a=4))
        sc = 128 ** -0.5
        for h in range(12):
            kh = h // 3
            ai = self.rot("arena", 2)
            ar = self.arena_r[ai]
            ab, gv = self.load_head(ai, (self.qT_s[0], h * 128 * NCOL, self.qT_s[1]), h * 128,
                                    (self.kT_s[0], kh * 128 * NCOL, self.kT_s[1]),
                                    (self.v_s[0], kh * TT * 128 * 129, self.v_s[1]))
            bv = self.load_bias(ai, self.FrepA, 384, h,
                                [(256, 128, 128), (128, 128, 128), (256, 128, 16), (128, 16, 16)])
            self.op("pool", "memset", writes=[ar], ap=bv[64:128, 1, 0:64], constant=NEG)
            self.op("pool", "memset", writes=[ar], ap=bv[0:64, 0, 64:128], constant=NEG)
            self.op("act", "activation", writes=[ar], out=bv[:, 0:2, :], in_=bv[:, 0:2, :], func=AF.Exp)
            self.op("act", "activation", writes=[ar], out=bv[:, 2, 0:16], in_=bv[:, 2, 0:16], func=AF.Exp)
            self.op("act", "activation", writes=[ar], out=bv[0:16, 3, 0:16], in_=bv[0:16, 3, 0:16], func=AF.Exp)
            q = ab[:, self.A_Q:self.A_Q + NCOL]
            k = ab[:, self.A_K:self.A_K + NCOL]
            vv = ab[:, self.A_V:self.A_V + TT * 129].rearrange("p (t c) -> p t c", t=TT)
            sink = self.esink[:, j * 12 + h: j * 12 + h + 1]
            for t in range(16):
                kts = []
                if t >= 1:
                    kts.append(dict(qk=[(k[:, (t - 1) * 128:t * 128], q[:, t * 128:(t + 1) * 128])], nk=128,
                                    v=vv[:, t - 1, :], bias=(bv, 0), reads=[ar]))
                kts.append(dict(qk=[(k[:, t * 128:(t + 1) * 128], q[:, t * 128:(t + 1) * 128])], nk=128,
                                v=vv[:, t, :], bias=(bv, 1), reads=[ar]))
                self.attn(kts, 128, sc, sink, gv[:, t, :], [ar], self.actT[:, h, t * 128:(t + 1) * 128], self.act_r[t])
            qs = q[:, 2048:2064]
            kts = [dict(qk=[(self.kcA[:, kh, :], qs)], nk=128, v=self.vcA[:, kh, :], bias=(bv, 2),
                        reads=[ar, self.kcA_r, self.vcA_r]),
                   dict(qk=[(k[:, 2048:2064], qs)], nk=16, v=vv[0:16, 16, :], bias=(bv, 3), reads=[ar])]
            self.attn(kts, 16, sc, sink[0:16, :], gv[0:16, 16, :], [ar], self.actT[:, h, 2048:2064], self.act_r[16])
            self.flush()

    def layer_c(self, L):
        din = self.din
        dout = self.dout
        gq = self.gcolB[:, 12:13]
        gk = self.gcolB[:, 13:14]
        gxq = self.gcolB[:, L:L + 1]
        blocks = [(c * 512, 512) for c in range(14)]

        def ep(tt, rows, bi, acc, accr):
            if bi < 3:
                zv, zr = self.headnorm(acc, accr, rows, 4, 128)
                self.transpose_out(zv, zr, rows, 4, gq, dst_dram=self.qk_store(self.qT_s, bi * 4, 4, tt, rows),
                                   dram_res=self.qT_s[1])
            elif bi < 6:
                b = bi - 3
                zv, zr = self.headnorm(acc, accr, rows, 4, 128)
                self.transpose_out(zv, zr, rows, 4, gk, dst_dram=self.qk_store(self.kT_s, b * 4, 4, tt, rows),
                                   dram_res=self.kT_s[1])
                if 12 <= tt < 16:
                    self.gain_rows_out(zv, zr, rows, 4, self.gbc[0:rows, 0:128],
                                       AP(dout["c_k_prompt"], (tt - 12) * 128 * 1536 + b * 512, [[1536, 128], [1, 512]]))
                elif tt == 16:
                    self.gain_rows_out(zv, zr, rows, 4, self.gbc[0:rows, 0:128],
                                       AP(dout["c_k_sample"], b * 512, [[1536, 16], [1, 512]]))
            elif bi < 9:
                b = bi - 6
                accv = acc.rearrange("p (a b) -> p a b", a=4)
                self.v_out(accv, accr, rows, 4, self.v_store(self.v_s, b * 4, 4, tt, rows), self.v_s[1])
                if 12 <= tt < 16:
                    self.rows_out(acc, accr, rows, 512,
                                  AP(dout["c_v_prompt"], (tt - 12) * 128 * 1536 + b * 512, [[1536, 128], [1, 512]]))
                elif tt == 16:
                    self.rows_out(acc, accr, rows, 512, AP(dout["c_v_sample"], b * 512, [[1536, 16], [1, 512]]))
            elif bi == 9:
                zv, zr = self.headnorm(acc, accr, rows, 4, 128)
                self.transpose_out(zv, zr, rows, 4, gxq, dst_dram=self.qk_store(self.xqT_s, 0, 4, tt, rows),
                                   dram_res=self.xqT_s[1])
            else:
                self.gate_out(acc, accr, rows, 512, tt, (bi - 10) * 512)

        self.proj((din["c_w_in"], 0), 7168, 16, blocks, self.x_tiles(), self.act_ap, ep)
        if self.chk(L, "P"):
            return
        sc = 128 ** -0.5
        ck = din["cache_c_k"]
        cv = din["cache_c_v"]
        for h in range(12):
            ai = self.rot("arena", 2)
            ar = self.arena_r[ai]
            ab, gv = self.load_head(ai, (self.qT_s[0], h * 128 * NCOL, self.qT_s[1]), h * 128,
                                    (self.kT_s[0], h * 128 * NCOL, self.kT_s[1]),
                                    (self.v_s[0], h * TT * 128 * 129, self.v_s[1]))
            bv = self.load_bias(ai, self.FrepC, 768, h, [(128 + 128 * (4 - i), 128, 128) for i in range(5)])
            f0 = self.A_B // 2 + 640
            bs = self.arena[ai][:, f0:f0 + 80].rearrange("p (a b) -> p a b", a=5)
            Ft, Fr = self.FrepC
            for i_, (off_, nk_) in enumerate([(640 - 128 * jc, 128) for jc in range(4)] + [(128, 16)]):
                self.dma(bs[0:nk_, i_, 0:16], AP(Ft, h * 128 * 768 + off_, [[767, nk_], [1, 16]]), ar, reads=[Fr], adds=[ar])
            self.op("pool", "memset", writes=[ar], ap=bv[64:128, 4, 0:64], constant=NEG)
            self.op("pool", "memset", writes=[ar], ap=bv[0:64, 0, 64:128], constant=NEG)
            self.op("act", "activation", writes=[ar], out=bv[:, 0:5, :], in_=bv[:, 0:5, :], func=AF.Exp)
            self.op("act", "activation", writes=[ar], out=bs[:, 0:4, :], in_=bs[:, 0:4, :], func=AF.Exp)
            self.op("act", "activation", writes=[ar], out=bs[0:16, 4, :], in_=bs[0:16, 4, :], func=AF.Exp)
            q = ab[:, self.A_Q:self.A_Q + NCOL]
            k = ab[:, self.A_K:self.A_K + NCOL]
            vv = ab[:, self.A_V:self.A_V + TT * 129].rearrange("p (t c) -> p t c", t=TT)
            for t in range(16):
                kts = []
                for o in range(4, -1, -1):
                    jt = t - o
                    if jt < 0:
                        continue
                    kts.append(dict(qk=[(k[:, jt * 128:(jt + 1) * 128], q[:, t * 128:(t + 1) * 128])], nk=128,
                                    v=vv[:, jt, :], bias=(bv, 4 - o), reads=[ar]))
                self.attn(kts, 128, sc, None, gv[:, t, :], [ar], self.actT[:, h, t * 128:(t + 1) * 128], self.act_r[t])
            self.flush()
            i = self.rot("ld", 2)
            ld = self.ld[i]
            lr = self.ld_r[i]
            self.dma(ld[:, :].rearrange("p (a b) -> p a b", a=4), AP(ck, h * 128, [[1536, 128], [128 * 1536, 4], [1, 128]]),
                     lr, writes=[lr])
            tpb, tpr = self.nb("tp")
            tpv = tpb.rearrange("p (a b) -> p a b", a=4)
            for c in range(4):
                self.op("pe", "transpose", reads=[lr, self.ident_r], writes=[tpr], out=tpv[:, c, :],
                        in_=ld[:, c * 128:(c + 1) * 128], identity=self.ident[:])
            self.op("act", "activation", reads=[tpr], writes=[self.kcC_r], out=self.kcC[:], in_=tpb[:, 0:512], func=AF.Copy)
            i = self.rot("ld", 2)
            ld = self.ld[i]
            lr = self.ld_r[i]
            self.dma(ld[:, :].rearrange("p (a b) -> p a b", a=4), AP(cv, h * 128, [[1536, 128], [128 * 1536, 4], [1, 128]]),
                     lr, writes=[lr])
            self.op("dve", "tensor_copy", reads=[lr], writes=[self.vcC_r], out=self.vcC[:, :, 0:128],
                    in_=ld[:, :].rearrange("p (a b) -> p a b", a=4))
            qs = q[:, 2048:2064]
            kts = []
            for jc in range(4):
                kts.append(dict(qk=[(self.kcC[:, jc * 128:(jc + 1) * 128], qs)], nk=128, v=self.vcC[:, jc, :],
                                bias=(bs, jc), reads=[ar, self.kcC_r, self.vcC_r]))
            kts.append(dict(qk=[(k[:, 2048:2064], qs)], nk=16, v=vv[0:16, 16, :], bias=(bs, 4), reads=[ar]))
            self.attn(kts, 16, sc, None, gv[0:16, 16, :], [ar], self.actT[:, h, 2048:2064], self.act_r[16])
            self.flush()

    def rope(self, x1, x2, rd, rows, nh, tt, scale_ap, out1, out2, wr):
        rp = self.rp
        rr = self.rp_r
        cs = self.cosT[0:rows, tt, :].unsqueeze(1).broadcast_to([rows, nh, 32])
        sn = self.sinT[0:rows, tt, :].unsqueeze(1).broadcast_to([rows, nh, 32])
        def t(i):
            return rp[0:rows, i, 0:nh * 32].rearrange("p (a b) -> p a b", a=nh)
        crd = [self.cos_r, self.sin_r]
        self.op("dve", "tensor_tensor", reads=rd + crd, writes=[rr], out=t(0), in0=x1, in1=cs, op=ALU.mult)
        self.op("dve", "tensor_tensor", reads=rd + crd, writes=[rr], out=t(1), in0=x2, in1=sn, op=ALU.mult)
        self.op("dve", "tensor_tensor", reads=rd + crd, writes=[rr], out=t(2), in0=x1, in1=sn, op=ALU.mult)
        self.op("dve", "tensor_tensor", reads=rd + crd, writes=[rr], out=t(3), in0=x2, in1=cs, op=ALU.mult)
        if scale_ap is None:
            self.op("dve", "tensor_tensor", reads=[rr], writes=wr, out=out1, in0=t(0), in1=t(1), op=ALU.subtract)
            self.op("dve", "tensor_tensor", reads=[rr], writes=wr, out=out2, in0=t(2), in1=t(3), op=ALU.add)
        else:
            self.op("dve", "tensor_tensor", reads=[rr], writes=[rr], out=t(4), in0=t(0), in1=t(1), op=ALU.subtract)
            self.op("dve", "tensor_tensor", reads=[rr], writes=[rr], out=t(5), in0=t(2), in1=t(3), op=ALU.add)
            self.op("dve", "tensor_tensor", reads=[rr] + rd, writes=wr, out=out1, in0=t(4), in1=scale_ap, op=ALU.mult)
            self.op("dve", "tensor_tensor", reads=[rr] + rd, writes=wr, out=out2, in0=t(5), in1=scale_ap, op=ALU.mult)

    def layer_b(self, L):
        din = self.din
        dout = self.dout
        gxq = self.gcolB[:, L:L + 1]
        cqT = self.cqT()
        blocks = [(0, 512), (512, 320), (832, 512)] + [(1344 + c * 512, 512) for c in range(4)]

        def ep(tt, rows, bi, acc, accr):
            if bi == 0:
                zv, zr = self.headnorm(acc, accr, rows, 1, 512)
                z4 = zv.rearrange("p a (c b) -> p (a c) b", c=4)

                def tail():
                    tpb, tpr = self.nb("tp")
                    tpv = tpb.rearrange("p (a b) -> p a b", a=4)
                    for c in range(4):
                        self.op("pe", "transpose", reads=[zr, self.ident_r], writes=[tpr], out=tpv[:, c, 0:rows],
                                in_=z4[:, c, :], identity=self.ident[0:rows, 0:rows])
                    g = self.gcolB[:, 14:18].unsqueeze(2).broadcast_to([128, 4, rows])
                    self.op("dve", "tensor_tensor", reads=[tpr, self.gcolB_r], adds=[self.cq_r],
                            out=cqT[:, :, tt * 128:tt * 128 + rows], in0=tpv[:, 0:4, 0:rows], in1=g, op=ALU.mult)
                self.defer(tail, 3)
            elif bi == 1:
                zv, zr = self.headnorm(acc[:, 0:256], accr, rows, 1, 256)
                i = self.rot("of", 2)
                of = self.of[i]
                orr = self.of_r[i]
                if tt < 16:
                    dst_ckv = AP(dout["b_ckv_prompt"], tt * 128 * 256, [[256, 128], [1, 256]])
                else:
                    dst_ckv = AP(dout["b_ckv_sample"], 0, [[256, 16], [1, 256]])

                def s1():
                    self.op("dve", "tensor_tensor", reads=[zr, self.gbc_r], writes=[orr], out=of[0:rows, 0:256],
                            in0=zv.rearrange("p a b -> p (a b)"), in1=self.gbc[0:rows, 0:256], op=ALU.mult)
                    self.dma(dst_ckv, of[0:rows, 0:256], orr, reads=[orr], is_output=True)
                self.defer(s1, 1)

                def tail():
                    tpb, tpr = self.nb("tp")
                    tpv = tpb.rearrange("p (a b) -> p a b", a=4)
                    for c in range(2):
                        self.op("pe", "transpose", reads=[orr, self.ident_r], writes=[tpr], out=tpv[:, c, 0:rows],
                                in_=of[0:rows, c * 128:(c + 1) * 128], identity=self.ident[0:rows, 0:rows])
                    self.op("act", "activation", reads=[tpr], adds=[self.ckvT_r], out=self.ckvT[:, :, tt * 128:tt * 128 + rows],
                            in_=tpv[:, 0:2, 0:rows], func=AF.Copy)
                self.defer(tail, 3)
                x1 = acc[:, 256:288].unsqueeze(1)
                x2 = acc[:, 288:320].unsqueeze(1)
                o1 = self.kr_all[0:rows, tt, 0:32].unsqueeze(1)
                o2 = self.kr_all[0:rows, tt, 32:64].unsqueeze(1)
                self.rope(x1, x2, [accr], rows, 1, tt, None, o1, o2, [self.kr_all_r])
                if tt < 16:
                    dst = AP(dout["b_krope_prompt"], tt * 128 * 64, [[64, 128], [1, 64]])
                else:
                    dst = AP(dout["b_krope_sample"], 0, [[64, 16], [1, 64]])
                self.dma(dst, self.kr_all[0:rows, tt, :], self.kr_all_r, reads=[self.kr_all_r], is_output=True)
                self.op("act", "activation", reads=[self.kr_all_r], writes=[self.sq_r, self.krss_r], out=self.sq[0:rows, 0:64],
                        in_=self.kr_all[0:rows, tt, :], func=AF.Square, accum_out=self.krss[0:rows, tt:tt + 1])
                self.op("dve", "tensor_scalar", reads=[self.krss_r], writes=[self.krss_r], out=self.krss[0:rows, tt:tt + 1],
                        in0=self.krss[0:rows, tt:tt + 1], scalar1=1.0 / 192, scalar2=EPS, op0=ALU.mult, op1=ALU.add)
            elif bi == 2:
                zv, zr = self.headnorm(acc, accr, rows, 4, 128)
                self.transpose_out(zv, zr, rows, 4, gxq, dst_dram=self.qk_store(self.xqT_s, 0, 4, tt, rows),
                                   dram_res=self.xqT_s[1])
            else:
                self.gate_out(acc, accr, rows, 512, tt, (bi - 3) * 512)

        self.proj((din["b_w_in"], 0), 3392, 16, blocks, self.x_tiles(), self.act_ap, ep)
        if self.chk(L, "P"):
            return

        def epq(tt, rows, bi, acc, accr):
            h0 = bi * 2
            self.op("act", "activation", reads=[accr], writes=[self.sq_r], out=self.sq[0:rows, 0:384], in_=acc, func=AF.Square)
            st, sr = self.newstat()
            self.op("dve", "tensor_reduce", reads=[self.sq_r], writes=[sr], out=st[0:rows, 0:2],
                    in_=self.sq[0:rows, 0:384].rearrange("p (a b) -> p a b", a=2), axis=AX.X, op=ALU.add)
            self.op("dve", "tensor_scalar", reads=[sr], writes=[sr], out=st[0:rows, 4:6], in0=st[0:rows, 0:2],
                    scalar1=1.0 / 192, scalar2=EPS, op0=ALU.mult, op1=ALU.add)
            self.op("pool", "tensor_tensor", reads=[sr, self.cm05_r], writes=[sr], out=st[0:rows, 8:10],
                    in0=st[0:rows, 4:6], in1=self.cm05[0:rows, 0:2], op=ALU.pow)
            i = self.rot("zf", 2)
            zf = self.zf[i]
            zr = self.zf_r[i]
            a3 = acc.rearrange("p (a b) -> p a b", a=2)
            z3 = zf[0:rows, 0:384].rearrange("p (a b) -> p a b", a=2)
            def s1():
                self.op("dve", "tensor_tensor", reads=[accr, sr], writes=[zr], out=z3[:, :, 0:128], in0=a3[:, :, 0:128],
                        in1=st[0:rows, 8:10].unsqueeze(2).broadcast_to([rows, 2, 128]), op=ALU.mult)
                rs32 = st[0:rows, 8:10].unsqueeze(2).broadcast_to([rows, 2, 32])
                self.rope(a3[:, :, 128:160], a3[:, :, 160:192], [accr, sr], rows, 2, tt, rs32, z3[:, :, 128:160],
                          z3[:, :, 160:192], [zr])
            self.defer(s1, 1)

            def tail():
                tpb, tpr = self.nb("tp")
                tpv = tpb.rearrange("p (a b) -> p a b", a=4)
                for c in range(2):
                    self.op("pe", "transpose", reads=[zr, self.ident_r], writes=[tpr], out=tpv[:, c, 0:rows],
                            in_=z3[:, c, 0:128], identity=self.ident[0:rows, 0:rows])
                    self.op("pe", "transpose", reads=[zr, self.ident_r], writes=[tpr], out=tpv[0:64, 2 + c, 0:rows],
                            in_=z3[:, c, 128:192], identity=self.ident[0:rows, 0:rows])
                i = self.rot("tb", 3)
                tb = self.tb[i]
                tr = self.tb_r[i]
                self.op("act", "activation", reads=[tpr, self.gcolB_r], writes=[tr], out=tb[:, 0:2, 0:rows],
                        in_=tpv[:, 0:2, 0:rows], func=AF.Copy, scale=self.gcolB[:, 20:21])
                self.op("act", "activation", reads=[tpr, self.gcolB_r], writes=[tr], out=tb[0:64, 2:4, 0:rows],
                        in_=tpv[0:64, 2:4, 0:rows], func=AF.Copy, scale=self.gcolB[0:64, 21:22])
                self.dma(self.qk_store(self.q192_s, h0, 2, tt, rows, hrows=192), tb[:, 0:2, 0:rows], tr, reads=[tr],
                         adds=[self.q192_s[1]])
                self.dma(self.qk_store(self.q192_s, h0, 2, tt, rows, dpart=64, hrows=192, r0=128), tb[0:64, 2:4, 0:rows], tr,
                         reads=[tr], adds=[self.q192_s[1]])
            self.defer(tail, 3)

        self.proj((din["b_w_q_b"], 0), 2304, 4, [(c * 384, 384) for c in range(6)],
                  [(t, rows_of(t), [self.cq_r]) for t in range(TT)],
                  lambda key, k: cqT[:, k, key * 128:key * 128 + rows_of(key)], epq)

        def epkv(key, rows, bi, acc, accr):
            kind_, idx = key
            h0 = bi * 2
            if kind_ == "c":
                kr = self.krt[idx % 3][0:rows, :]
                krr = [self.krt_r[idx % 3]]
                ssap = self.krss_c[0:rows, idx:idx + 1]
                ssr = [self.krss_c_r]
            else:
                kr = self.kr_all[0:rows, idx, :]
                krr = [self.kr_all_r]
                ssap = self.krss[0:rows, idx:idx + 1]
                ssr = [self.krss_r]
            a4 = acc.rearrange("p (a b) -> p a b", a=2)
            sq2 = self.sq[0:rows, 0:256].rearrange("p (a b) -> p a b", a=2)
            self.op("act", "activation", reads=[accr], writes=[self.sq_r], out=sq2, in_=a4[:, :, 0:128], func=AF.Square)
            st, sr = self.newstat()
            self.op("dve", "tensor_reduce", reads=[self.sq_r], writes=[sr], out=st[0:rows, 0:2], in_=sq2, axis=AX.X, op=ALU.add)
            self.op("dve", "tensor_scalar", reads=[sr] + ssr, writes=[sr], out=st[0:rows, 4:6], in0=st[0:rows, 0:2],
                    scalar1=1.0 / 192, scalar2=ssap, op0=ALU.mult, op1=ALU.add)
            self.op("pool", "tensor_tensor", reads=[sr, self.cm05_r], writes=[sr], out=st[0:rows, 8:10],
                    in0=st[0:rows, 4:6], in1=self.cm05[0:rows, 0:2], op=ALU.pow)
            rs2 = st[0:rows, 8:10].unsqueeze(2)
            i = self.rot("zf", 2)
            zf = self.zf[i]
            zr = self.zf_r[i]
            z3 = zf[0:rows, 0:384].rearrange("p (a b) -> p a b", a=2)
            def s1():
                self.op("dve", "tensor_tensor", reads=[accr, sr], writes=[zr], out=z3[:, :, 0:128], in0=a4[:, :, 0:128],
                        in1=rs2.broadcast_to([rows, 2, 128]), op=ALU.mult)
                self.op("dve", "tensor_tensor", reads=krr + [sr], writes=[zr], out=z3[:, :, 128:192],
                        in0=kr.unsqueeze(1).broadcast_to([rows, 2, 64]), in1=rs2.broadcast_to([rows, 2, 64]), op=ALU.mult)
            self.defer(s1, 1)
            if kind_ == "p":
                kd, vd, tile, width, nt = self.k192_s, self.v_s, idx, NCOL, TT
            else:
                tile = idx if kind_ == "c" else 32
                kd, vd, width, nt = self.k192s_s, self.vs_s, 4224, 33

            def tail():
                tpb, tpr = self.nb("tp")
                tpv = tpb.rearrange("p (a b) -> p a b", a=4)
                for c in range(2):
                    self.op("pe", "transpose", reads=[zr, self.ident_r], writes=[tpr], out=tpv[:, c, 0:rows],
                            in_=z3[:, c, 0:128], identity=self.ident[0:rows, 0:rows])
                    self.op("pe", "transpose", reads=[zr, self.ident_r], writes=[tpr], out=tpv[0:64, 2 + c, 0:rows],
                            in_=z3[:, c, 128:192], identity=self.ident[0:rows, 0:rows])
                i = self.rot("tb", 3)
                tb = self.tb[i]
                tr = self.tb_r[i]
                self.op("act", "activation", reads=[tpr, self.gcolB_r], writes=[tr], out=tb[:, 0:2, 0:rows],
                        in_=tpv[:, 0:2, 0:rows], func=AF.Copy, scale=self.gcolB[:, 22:23])
                self.op("act", "activation", reads=[tpr, self.gcolB_r], writes=[tr], out=tb[0:64, 2:4, 0:rows],
                        in_=tpv[0:64, 2:4, 0:rows], func=AF.Copy, scale=self.gcolB[0:64, 23:24])
                self.dma(self.qk_store(kd, h0, 2, tile, rows, width=width, hrows=192), tb[:, 0:2, 0:rows], tr, reads=[tr],
                         adds=[kd[1]])
                self.dma(self.qk_store(kd, h0, 2, tile, rows, width=width, dpart=64, hrows=192, r0=128), tb[0:64, 2:4, 0:rows],
                         tr, reads=[tr], adds=[kd[1]])
            self.defer(tail, 3)
            self.v_out(a4[:, :, 128:256], accr, rows, 2, self.v_store(vd, h0, 2, tile, rows, ntiles=nt), vd[1])

        tiles = [(("p", t), 128, [self.ckvT_r]) for t in range(16)] + [(("s", 16), 16, [self.ckvT_r])]

        def actkv(key, k):
            kind_, idx = key
            if kind_ == "c":
                return self.ckvt[idx % 2][:, k, :]
            rows = rows_of(idx)
            return self.ckvT[:, k, idx * 128: idx * 128 + rows]

        cache_tiles = []
        for jc in range(32):
            cache_tiles.append((("c", jc), 128, [self.ckvt_r[jc % 2]]))
        self._kv_prep_pending = True
        self.proj_kv((din["b_w_kv_b"], 0), tiles, cache_tiles, actkv, epkv)

        sc = 192 ** -0.5
        QB, KB = 8736, 10912
        for h in range(12):
            ai = self.rot("arena", 2)
            ar = self.arena_r[ai]
            ab = self.arena_bf(ai)
            Q, Qr = self.q192_s
            Kt, Kr = self.k192_s
            self.dma(ab[0:96, self.A_Q:self.A_Q + 2048], AP(Q, h * 192 * NCOL, [[NCOL, 96], [1, 2048]]), ar, reads=[Qr], adds=[ar])
            self.dma(ab[0:96, QB:QB + 2048], AP(Q, (h * 192 + 96) * NCOL, [[NCOL, 96], [1, 2048]]), ar, reads=[Qr], adds=[ar])
            self.dma(ab[0:96, self.A_K:self.A_K + 2048], AP(Kt, h * 192 * NCOL, [[NCOL, 96], [1, 2048]]), ar, reads=[Kr], adds=[ar])
            self.dma(ab[0:96, KB:KB + 2048], AP(Kt, (h * 192 + 96) * NCOL, [[NCOL, 96], [1, 2048]]), ar, reads=[Kr], adds=[ar])
            G, Gr = self.gate_s
            gv = ab[:, self.A_G:self.A_G + 2176].rearrange("p (t c) -> p t c", t=TT)
            self.dma(gv[:, 0:16, :], AP(G, h * 128, [[2048, 128], [128 * 2048, 16], [1, 128]]), ar, reads=[Gr], adds=[ar])
            vv = ab[:, self.A_V:self.A_V + TT * 129].rearrange("p (t c) -> p t c", t=TT)
            self.dma(vv[:, 0:16, :], AP(self.v_s[0], h * TT * 128 * 129, [[129, 128], [128 * 129, 16], [1, 129]]), ar,
                     reads=[self.v_s[1]], adds=[ar])
            qa = ab[0:96, self.A_Q:self.A_Q + NCOL]
            qb = ab[0:96, QB:QB + NCOL]
            ka = ab[0:96, self.A_K:self.A_K + 2048]
            kb = ab[0:96, KB:KB + 2048]
            for t in range(16):
                kts = []
                for jt in range(t + 1):
                    kts.append(dict(qk=[(ka[:, jt * 128:(jt + 1) * 128], qa[:, t * 128:(t + 1) * 128]),
                                        (kb[:, jt * 128:(jt + 1) * 128], qb[:, t * 128:(t + 1) * 128])], nk=128,
                                    v=vv[:, jt, :], bias=None, diag=(jt == t), reads=[ar]))
                self.attn(kts, 128, sc, None, gv[:, t, :], [ar], self.actT[:, h, t * 128:(t + 1) * 128], self.act_r[t])
            self.flush()
        KA0, KB0, V0, QA0, QB0, G0 = 0, 4224, 8448, 12708, 12724, 12740
        for h in range(12):
            ai = self.rot("arena", 2)
            ar = self.arena_r[ai]
            ab = self.arena_bf(ai)
            Q, Qr = self.q192_s
            Kt, Kr = self.k192s_s
            ka = ab[0:96, KA0:KA0 + 4224]
            kb = ab[0:96, KB0:KB0 + 4224]
            vv = ab[:, V0:V0 + 33 * 129].rearrange("p (t c) -> p t c", t=33)
            qa = ab[0:96, QA0:QA0 + 16]
            qb = ab[0:96, QB0:QB0 + 16]
            gt = ab[0:16, G0:G0 + 128]
            self.dma(ka[:, 0:4112], AP(Kt, h * 192 * 4224, [[4224, 96], [1, 4112]]), ar, reads=[Kr], adds=[ar])
            self.dma(kb[:, 0:4112], AP(Kt, (h * 192 + 96) * 4224, [[4224, 96], [1, 4112]]), ar, reads=[Kr], adds=[ar])
            self.dma(vv[:, 0:32, :], AP(self.vs_s[0], h * 33 * 128 * 129, [[129, 128], [128 * 129, 32], [1, 129]]), ar,
                     reads=[self.vs_s[1]], adds=[ar])
            self.dma(vv[0:16, 32, :], AP(self.vs_s[0], (h * 33 + 32) * 128 * 129, [[129, 16], [1, 129]]), ar,
                     reads=[self.vs_s[1]], adds=[ar])
            self.dma(qa, AP(Q, h * 192 * NCOL + 2048, [[NCOL, 96], [1, 16]]), ar, reads=[Qr], adds=[ar])
            self.dma(qb, AP(Q, (h * 192 + 96) * NCOL + 2048, [[NCOL, 96], [1, 16]]), ar, reads=[Qr], adds=[ar])
            self.dma(gt, AP(self.gate_s[0], 2048 * 2048 + h * 128, [[2048, 16], [1, 128]]), ar, reads=[self.gate_s[1]], adds=[ar])
            s0, s0r = self.nb("s")
            s1, s1r = self.nb("s")
            for jc in range(32):
                self.op("pe", "matmul", reads=[ar], writes=[s0r], out=s0[:, jc * 16:(jc + 1) * 16],
                        lhsT=ka[:, jc * 128:(jc + 1) * 128], rhs=qa, start=True, stop=False)
                self.op("pe", "matmul", reads=[ar], writes=[s0r], out=s0[:, jc * 16:(jc + 1) * 16],
                        lhsT=kb[:, jc * 128:(jc + 1) * 128], rhs=qb, start=False, stop=True)
            self.op("pe", "matmul", reads=[ar], writes=[s1r], out=s1[0:16, 0:16], lhsT=ka[:, 4096:4112], rhs=qa,
                    start=True, stop=False)
            self.op("pe", "matmul", reads=[ar], writes=[s1r], out=s1[0:16, 0:16], lhsT=kb[:, 4096:4112], rhs=qb,
                    start=False, stop=True)
            p0, p0r = self.pexp[0], self.pexp_r[0]
            p1, p1r = self.pexp[1], self.pexp_r[1]
            self.op("act", "activation", reads=[s0r], writes=[p0r], out=p0[:, 0:512], in_=s0[:, 0:512], func=AF.Exp, scale=sc)
            self.op("act", "activation", reads=[s1r], writes=[p1r], out=p1[0:16, 0:16], in_=s1[0:16, 0:16], func=AF.Exp, scale=sc)
            ob, orr = self.nb("o")
            for jc in range(32):
                self.op("pe", "matmul", reads=[p0r, ar], writes=[orr], out=ob[0:16, 0:129], lhsT=p0[:, jc * 16:(jc + 1) * 16],
                        rhs=vv[:, jc, :], start=(jc == 0), stop=False)
            self.op("pe", "matmul", reads=[p1r, ar], writes=[orr], out=ob[0:16, 0:129], lhsT=p1[0:16, 0:16], rhs=vv[0:16, 32, :],
                    start=False, stop=True)
            self.attn_post(ob, orr, 16, None, gt, [ar], self.actT[:, h, 2048:2064], self.act_r[16])
            self.flush()

    def proj_kv(self, W, tiles, cache_tiles, actkv, epkv):
        din = self.din
        wh, woff = W
        ai = self.rot("arena", 2)
        ar = self.arena_r[ai]
        wv = self.arena_bf(ai)[:, 0:2 * 3072].rearrange("p (k w) -> p k w", k=2)
        for c in range(6):
            src = AP(wh, woff + c * 512, [[3072, 128], [128 * 3072, 2], [1, 512]])
            self.dma(wv[:, :, c * 512:(c + 1) * 512], src, ar, adds=[ar], q="pool")
        ck = din["cache_b_ckv"]
        ckr = din["cache_b_krope"]

        def prep(idx):
            b = idx % 2
            i = self.rot("ld", 2)
            ld = self.ld[i]
            lr = self.ld_r[i]
            self.dma(ld[:, 0:256], AP(ck, idx * 128 * 256, [[256, 128], [1, 256]]), lr, writes=[lr])
            b3 = idx % 3
            self.dma(self.krt[b3][:], AP(ckr, idx * 128 * 64, [[64, 128], [1, 64]]), self.krt_r[b3], writes=[self.krt_r[b3]])
            tpb, tpr = self.nb("tp")
            tpv = tpb.rearrange("p (a b) -> p a b", a=4)
            for c in range(2):
                self.op("pe", "transpose", reads=[lr, self.ident_r], writes=[tpr], out=tpv[:, c, :],
                        in_=ld[:, c * 128:(c + 1) * 128], identity=self.ident[:])
            self.op("act", "activation", reads=[tpr], writes=[self.ckvt_r[b]], out=self.ckvt[b][:], in_=tpv[:, 0:2, :],
                    func=AF.Copy)
            self.op("act", "activation", reads=[self.krt_r[b3]], writes=[self.rp_r, self.krss_c_r], out=self.rp[:, 0, :],
                    in_=self.krt[b3][:], func=AF.Square, accum_out=self.krss_c[:, idx:idx + 1])
            self.op("dve", "tensor_scalar", reads=[self.krss_c_r], writes=[self.krss_c_r], out=self.krss_c[:, idx:idx + 1],
                    in0=self.krss_c[:, idx:idx + 1], scalar1=1.0 / 192, scalar2=EPS, op0=ALU.mult, op1=ALU.add)

        allt = tiles + cache_tiles
        for ti, (key, rows, rd) in enumerate(allt):
            kind_, idx = key
            if kind_ == "c" and idx == 0:
                prep(0)
            if ti + 1 < len(allt) and allt[ti + 1][0][0] == "c" and allt[ti + 1][0][1] > 0:
                prep(allt[ti + 1][0][1])
            for bi in range(6):
                acc, accr = self.nb("acc")
                for k in range(2):
                    self.op("pe", "matmul", reads=[ar] + rd, writes=[accr], out=acc[0:rows, 0:512], lhsT=actkv(key, k),
                            rhs=wv[:, k, bi * 512:(bi + 1) * 512], start=(k == 0), stop=(k == 1))
                self.step()
                epkv(key, rows, bi, acc[0:rows, 0:512], accr)
        self.flush()

    def mem_heads(self, L):
        sc = 128 ** -0.5
        for hm in range(4):
            ai = self.rot("arena", 2)
            ar = self.arena_r[ai]
            ab, gv = self.load_head(ai, (self.xqT_s[0], hm * 128 * NCOL, self.xqT_s[1]), (12 + hm) * 128, None, None,
                                    with_kv=False)
            q = ab[:, self.A_Q:self.A_Q + NCOL]
            for t in range(16):
                kts = [dict(qk=[(self.mkT[:, hm, jt * 128:(jt + 1) * 128], q[:, t * 128:(t + 1) * 128])], nk=128,
                            v=self.mv[:, jt, hm, :], bias=None, reads=[ar, self.mkT_r, self.mv_r]) for jt in range(2)]
                self.attn(kts, 128, sc, None, gv[:, t, :], [ar], self.actT[:, 12 + hm, t * 128:(t + 1) * 128], self.act_r[t])
            qs = q[:, 2048:2064]
            kts = [dict(qk=[(self.mkTs[:, hm, jt * 128:(jt + 1) * 128], qs)], nk=128, v=self.mvs[:, jt, hm, :], bias=None,
                        reads=[ar, self.mkTs_r, self.mvs_r]) for jt in range(2)]
            self.attn(kts, 16, sc, None, gv[0:16, 16, :], [ar], self.actT[:, 12 + hm, 2048:2064], self.act_r[16])
            self.flush()

    def out_phase(self, L):
        din = self.din
        last = (L == self.n_layers - 1)
        xnew, xnew_r = self.xres[L % 2]
        srcs = self._xsrc
        if L > 0:
            xold_r = [self.xres[(L - 1) % 2][1]]
        else:
            xold_r = []

        pend = {}
        order = [(tt, bi) for bi in range(4) for tt in range(TT)]

        def issue(n):
            tt, bi = order[n]
            rows = rows_of(tt)
            i = self.rot("ld", 2)
            ld = self.ld[i]
            lr = self.ld_r[i]
            s = srcs[tt][0]
            src = AP(s.tensor, s.offset + bi * 512, [[D, rows], [1, 512]])
            self.dma(ld[0:rows, :], src, lr, reads=xold_r, writes=[lr])
            pend[(tt, bi)] = (ld, lr)

        issue(0)

        def ep(tt, rows, bi, acc, accr):
            n = order.index((tt, bi))
            if n + 1 < len(order):
                issue(n + 1)
            ld, lr = pend.pop((tt, bi))
            j = self.rot("of", 2)
            of = self.of[j]
            orr = self.of_r[j]
            self.op("dve", "tensor_tensor", reads=[accr, lr], writes=[orr], out=of[0:rows, :], in0=acc, in1=ld[0:rows, :],
                    op=ALU.add)
            if last:
                if tt < 16:
                    dst = AP(self.dout["y_prompt"], tt * 128 * D + bi * 512, [[D, 128], [1, 512]])
                else:
                    dst = AP(self.dout["y_sample"], bi * 512, [[D, 16], [1, 512]])
                self.dma(dst, of[0:rows, :], orr, reads=[orr], is_output=True)
            else:
                dst = AP(xnew, tt * 128 * D + bi * 512, [[D, rows], [1, 512]])
                self.dma(dst, of[0:rows, :], orr, reads=[orr], adds=[xnew_r])

        self.proj((din["w_out"], L * 2048 * 2048), 2048, 16, [(c * 512, 512) for c in range(4)], self.x_tiles(),
                  self.act_ap, ep)


def _consts():
    ident = np.eye(128, dtype=np.float32)
    rel = np.arange(384) - 128
    half, max_exact = 16, 8
    ret = np.where(rel < 0, half, 0)
    n = np.abs(rel)
    nf = np.maximum(n, 1).astype(np.float32)
    large = max_exact + (np.log(nf / max_exact) / np.float32(np.log(128 / max_exact)) * (half - max_exact)).astype(np.int32)
    large = np.minimum(large, half - 1)
    bucket = ret + np.where(n < max_exact, n, large)
    oh = np.zeros((32, 384), np.float32)
    oh[bucket, np.arange(384)] = 1.0
    pos = np.zeros((128, 17), np.float32)
    for tt in range(16):
        pos[:, tt] = tt * 128 + np.arange(128)
    pos[:, 16] = 4096 + np.arange(128)
    inv = (np.float32(10000.0) ** (-np.arange(32, dtype=np.float32) / np.float32(32))).astype(np.float32)
    ang = (pos[:, :, None] * inv[None, None, :]).astype(np.float32)
    return ident, oh, np.cos(ang).astype(np.float32), np.sin(ang).astype(np.float32)


_PROG = {}


def kernel(**inputs):
    n = 8
    if "p" not in _PROG:
        _PROG["p"] = Prog()
    prog = _PROG["p"]
    ident, oh, cs, sn = _consts()
    per_batch = {"x_prompt": 0, "x_sample": 0, "mem_prompt": 0, "cache_a_k": 1, "cache_a_v": 1, "cache_b_ckv": 1,
                 "cache_b_krope": 1, "cache_c_k": 1, "cache_c_v": 1, "cache_mem_k": 1, "cache_mem_v": 1}
    shapes = dict(IN_SPECS)
    in_maps = []
    for b in range(n):
        m = {}
        for name, shp in IN_SPECS:
            if name.startswith("k_"):
                continue
            a = np.asarray(inputs[name], dtype=np.float32)
            if name in per_batch:
                a = np.take(a, b, axis=per_batch[name])
                if name in ("cache_b_ckv", "cache_b_krope", "cache_c_k", "cache_c_v"):
                    a = a[0]
            m[name] = np.ascontiguousarray(a).reshape(shp)
        m["k_ident"], m["k_t5oh"], m["k_cos"], m["k_sin"] = ident, oh, cs, sn
        in_maps.append(m)
    res = run_bass_kernel_spmd(prog.nc, in_maps, core_ids=list(range(n)))
    r = res.results

    def st(name, shape_fn):
        return np.stack([shape_fn(r[b][name]) for b in range(n)])

    y_p = st("y_prompt", lambda a: a)
    y_s = st("y_sample", lambda a: a)
    def lay(name, nl, rows, kvh):
        return np.stack([r[b][name].reshape(nl, rows, kvh, 128) for b in range(n)], axis=1)
    outs = (
        y_p, y_s,
        lay("a_k_prompt", 2, 128, 4), lay("a_v_prompt", 2, 128, 4), lay("a_k_sample", 2, 16, 4), lay("a_v_sample", 2, 16, 4),
        st("b_ckv_prompt", lambda a: a)[None], st("b_krope_prompt", lambda a: a)[None],
        st("b_ckv_sample", lambda a: a)[None], st("b_krope_sample", lambda a: a)[None],
        lay("c_k_prompt", 1, 512, 12), lay("c_v_prompt", 1, 512, 12), lay("c_k_sample", 1, 16, 12), lay("c_v_sample", 1, 16, 12),
        lay("mem_k_prompt", 4, 256, 4), lay("mem_v_prompt", 4, 256, 4),
    )
    return tuple(np.ascontiguousarray(o, dtype=np.float32) for o in outs)
```

```python
import numpy as np
import concourse.bass as bass
import concourse.mybir as mybir
from concourse.bass_types import AP
from concourse.bass_utils import run_bass_kernel_spmd

F32 = mybir.dt.float32
BF = mybir.dt.bfloat16
AF = mybir.ActivationFunctionType
ALU = mybir.AluOpType
AX = mybir.AxisListType

D = 2048
TT = 17
NCOL = TT * 128
EPS = 1e-6
NEG = -1e30


class Res:
    __slots__ = ("name", "w", "r", "dsem", "dcnt")

    def __init__(self, name):
        self.name = name
        self.w = []
        self.r = []
        self.dsem = None
        self.dcnt = 0


class Sched:
    ENG = ("pe", "act", "dve", "pool", "sp")

    def __init__(self, nc):
        self.nc = nc
        self.prog = {e: [] for e in self.ENG}
        self.esem = {e: nc.alloc_semaphore("es_" + e) for e in ("pe", "act", "dve", "pool")}
        self.cnt = {e: 0 for e in self.ENG}
        self.known = {e: {} for e in self.ENG}
        self.semobj = {"es_" + e: s for e, s in self.esem.items()}
        self.nd = 0
        self.nwaits = 0
        self.out_events = []

    def _deps(self, eng, reads, writes, adds):
        need = {}
        own = "es_" + eng
        kn = self.known[eng]

        def add(ev, raw):
            k, v, clk = ev
            if k == own and (eng == "pe" or not raw):
                return
            if kn.get(k, 0) >= v:
                return
            if need.get(k, (0, None))[0] < v:
                need[k] = (v, clk)

        for res in reads:
            for ev in res.w:
                add(ev, True)
        for res in writes:
            for ev in res.w:
                add(ev, False)
            for ev in res.r:
                add(ev, False)
        for res in adds:
            for ev in res.r:
                add(ev, False)
        waits = []
        for k, (v, clk) in sorted(need.items(), key=lambda kv: -len(kv[1][1])):
            if kn.get(k, 0) >= v:
                continue
            waits.append((self.semobj[k], v))
            for kk, vv in clk.items():
                if kn.get(kk, 0) < vv:
                    kn[kk] = vv
            if kn.get(k, 0) < v:
                kn[k] = v
        self.nwaits += len(waits)
        return waits

    def _mark(self, ev, reads, writes, adds):
        k = ev[0]
        for res in writes:
            res.w = [ev]
            res.r = []
        for res in adds:
            res.w = [e for e in res.w if e[0] != k]
            res.w.append(ev)
        for res in reads:
            res.r = [e for e in res.r if e[0] != k]
            res.r.append(ev)

    def op(self, eng, name, reads=(), writes=(), adds=(), **kw):
        waits = self._deps(eng, reads, writes, adds)
        self.cnt[eng] += 1
        n = self.cnt[eng]
        sem = self.esem[eng]

        def run(e, name=name, kw=kw, waits=waits, sem=sem):
            for s, v in waits:
                e.wait_ge(s, v)
            getattr(e, name)(**kw).then_inc(sem, 1)

        self.prog[eng].append(run)
        clk = dict(self.known[eng])
        clk["es_" + eng] = n
        ev = ("es_" + eng, n, clk)
        self._mark(ev, reads, writes, adds)
        return ev

    def dma(self, q, out, in_, sres, reads=(), writes=(), adds=(), is_output=False):
        waits = self._deps(q, reads, writes, adds)
        if sres.dsem is None:
            sres.dsem = {}
            sres.dcnt = {}
        if q not in sres.dsem:
            self.nd += 1
            key = "ds%d" % self.nd
            sres.dsem[q] = key
            sres.dcnt[q] = 0
            self.semobj[key] = self.nc.alloc_semaphore(key)
        sres.dcnt[q] += 16
        dkey = sres.dsem[q]
        dval = sres.dcnt[q]
        sem = self.semobj[dkey]

        def run(e, waits=waits, sem=sem, out=out, in_=in_):
            for s, v in waits:
                e.wait_ge(s, v)
            e.dma_start(out=out, in_=in_).then_inc(sem, 16)

        self.prog[q].append(run)
        ev = (dkey, dval, dict(self.known[q]))
        self._mark(ev, reads, writes, adds)
        self.out_events.append(ev)
        return ev

    def emit(self):
        nc = self.nc
        last = {}
        for k, v, clk in self.out_events:
            last[k] = max(last.get(k, 0), v)
        for e in ("pe", "act", "dve", "pool"):
            if self.cnt[e]:
                last["es_" + e] = self.cnt[e]
        fwaits = [(self.semobj[k], v) for k, v in last.items()]
        prog = self.prog
        with nc.Block() as block:
            @block.tensor
            def _(e):
                for f in prog["pe"]:
                    f(e)

            @block.scalar
            def _(e):
                for f in prog["act"]:
                    f(e)

            @block.vector
            def _(e):
                for f in prog["dve"]:
                    f(e)

            @block.gpsimd
            def _(e):
                for f in prog["pool"]:
                    f(e)

            @block.sync
            def _(e):
                for f in prog["sp"]:
                    f(e)
                for s, v in fwaits:
                    e.wait_ge(s, v)


def rows_of(tt):
    return 128 if tt < 16 else 16


IN_SPECS = [
    ("x_prompt", [2048, 2048]), ("x_sample", [16, 2048]), ("mem_prompt", [256, 2048]),
    ("cache_a_k", [2, 128, 512]), ("cache_a_v", [2, 128, 512]),
    ("cache_b_ckv", [4096, 256]), ("cache_b_krope", [4096, 64]),
    ("cache_c_k", [512, 1536]), ("cache_c_v", [512, 1536]),
    ("cache_mem_k", [4, 256, 512]), ("cache_mem_v", [4, 256, 512]),
    ("t5_bias", [32, 12]), ("norm_g", [4, 2048]), ("w_out", [4, 2048, 2048]),
    ("mem_norm_g", [4, 2048]), ("w_mem_kv", [4, 2048, 1024]),
    ("xq_norm_g", [4, 128]), ("xk_norm_g", [4, 128]),
    ("a_w_in", [2, 2048, 5120]), ("a_q_norm_g", [2, 128]), ("a_k_norm_g", [2, 128]),
    ("a_sink", [2, 12]),
    ("b_w_in", [1, 2048, 3392]), ("b_cq_norm_g", [1, 512]), ("b_w_q_b", [1, 512, 2304]),
    ("b_ckv_norm_g", [1, 256]), ("b_w_kv_b", [1, 256, 3072]),
    ("b_q_norm_g", [1, 192]), ("b_k_norm_g", [1, 192]),
    ("c_w_in", [1, 2048, 7168]), ("c_q_norm_g", [1, 128]), ("c_k_norm_g", [1, 128]),
    ("c_rel_bias", [1, 12, 257]),
    ("k_ident", [128, 128]), ("k_t5oh", [32, 384]), ("k_cos", [128, 17, 32]), ("k_sin", [128, 17, 32]),
]
OUT_SPECS = [
    ("y_prompt", [2048, 2048]), ("y_sample", [16, 2048]),
    ("a_k_prompt", [2, 128, 512]), ("a_v_prompt", [2, 128, 512]),
    ("a_k_sample", [2, 16, 512]), ("a_v_sample", [2, 16, 512]),
    ("b_ckv_prompt", [2048, 256]), ("b_krope_prompt", [2048, 64]),
    ("b_ckv_sample", [16, 256]), ("b_krope_sample", [16, 64]),
    ("c_k_prompt", [512, 1536]), ("c_v_prompt", [512, 1536]),
    ("c_k_sample", [16, 1536]), ("c_v_sample", [16, 1536]),
    ("mem_k_prompt", [4, 256, 512]), ("mem_v_prompt", [4, 256, 512]),
]


class Prog:
    def __init__(self, n_layers=4, stop=None):
        self.n_layers = n_layers
        self.stop = stop
        self.stopped = False
        nc = self.nc = bass.Bass("TRN2", target_bir_lowering=False)
        self.S = Sched(nc)
        self.din = {n: nc.dram_tensor(n, s, F32, kind="ExternalInput") for n, s in IN_SPECS}
        self.dout = {n: nc.dram_tensor(n, s, F32, kind="ExternalOutput") for n, s in OUT_SPECS}
        self.dres = {}
        self._rr = {}
        self.deferred = []
        self.lag = 1
        self.alloc()
        self.prologue()
        for L in range(n_layers):
            if not self.stopped:
                self.layer(L)
        self.S.emit()

    def chk(self, L, ph):
        if self.stop is not None and self.stop == (L, ph):
            self.stopped = True
        return self.stopped

    def rres(self, name):
        if name not in self.dres:
            self.dres[name] = Res(name)
        return self.dres[name]

    def rot(self, key, n):
        i = self._rr.get(key, 0)
        self._rr[key] = i + 1
        return i % n

    def sb(self, name, shape, dt):
        t = self.nc.alloc_sbuf_tensor(name, shape, dt)
        return t, self.rres("sb_" + name)

    def scr(self, name, shape, dt=BF):
        t = self.nc.dram_tensor(name, shape, dt)
        return t, self.rres("dr_" + name)

    def op(self, eng, name, reads=(), writes=(), adds=(), **kw):
        ex = [r for r in reads if r in self.ps_set]
        if ex:
            reads = [r for r in reads if r not in self.ps_set]
            writes = list(writes) + ex
        return self.S.op(eng, name, reads, writes, adds, **kw)

    def dma(self, out, in_, sres, reads=(), writes=(), adds=(), q="sp", is_output=False):
        return self.S.dma(q, out, in_, sres, reads, writes, adds, is_output)

    def newstat(self):
        i = self.rot("stat", 12)
        return self.stat[i], self.stat_r[i]

    def bank(self, b):
        return self.ps[:, b * 512:(b + 1) * 512]

    BANKS = {"acc": [0, 1, 4, 5, 6, 7], "tp": [2, 3], "s": [0, 1, 4, 5], "o": [6, 7]}

    def nb(self, kind):
        lst = self.BANKS[kind]
        b = lst[self.rot("bank_" + kind, len(lst))]
        return self.bank(b), self.ps_r[b]

    def defer(self, fn, delay=1):
        self._seq = getattr(self, "_seq", 0) + 1
        self.deferred.append([delay, self._seq, fn])

    def step(self):
        due = []
        rest = []
        for it in self.deferred:
            it[0] -= 1
            (due if it[0] <= 0 else rest).append(it)
        self.deferred = rest
        for it in sorted(due, key=lambda t: t[1]):
            it[2]()

    def flush(self):
        while self.deferred:
            self.step()

    def alloc(self):
        nc = self.nc
        self.ps = nc.alloc_psum_tensor("ps", [128, 4096], F32)
        self.ps_r = [Res("bank%d" % i) for i in range(8)]
        self.ps_set = set(self.ps_r)
        self.actT, _ = self.sb("actT", [128, 16, NCOL], BF)
        self.act_r = [Res("act%d" % t) for t in range(TT)]
        self.arena = []
        self.arena_r = []
        for i in range(2):
            t, r = self.sb("arena%d" % i, [128, 6528], F32)
            self.arena.append(t)
            self.arena_r.append(r)
        self.xta, self.xta_r = self.sb("xta", [128, 2 * 2176], F32)
        self.xt_r = [Res("xt0"), Res("xt1")]
        self.cq_r = self.xta_r
        self.Rt, self.R_r = self.sb("Rt", [128, 2176], F32)
        self.Rh_r = [Res("Rh0"), Res("Rh1")]
        self.ckvT, self.ckvT_r = self.sb("ckvT", [128, 2, NCOL], BF)
        self.kr_all, self.kr_all_r = self.sb("kr_all", [128, TT, 64], F32)
        self.krss, self.krss_r = self.sb("krss", [128, TT], F32)
        self.krss_c, self.krss_c_r = self.sb("krss_c", [128, 40], F32)
        self.gbm_r = Res("gbm")
        self.zf = []
        self.zf_r = []
        self.of = []
        self.of_r = []
        self.tb = []
        self.tb_r = []
        self.vb = []
        self.vb_r = []
        self.gb = []
        self.gb_r = []
        self.ld = []
        self.ld_r = []
        self.pexp = []
        self.pexp_r = []
        self.ogf = []
        self.ogf_r = []
        self.ckvt = []
        self.ckvt_r = []
        self.krt = []
        self.krt_r = []
        for i in range(2):
            for lst, rl, nm, shp, dt in (
                (self.zf, self.zf_r, "zf", [128, 512], F32), (self.of, self.of_r, "of", [128, 512], F32),
                (self.tb, self.tb_r, "tb", [128, 4, 128], BF), (self.vb, self.vb_r, "vb", [128, 4, 129], BF),
                (self.gb, self.gb_r, "gb", [128, 512], BF), (self.ld, self.ld_r, "ld", [128, 512], F32),
                (self.pexp, self.pexp_r, "pexp", [128, 512], BF), (self.pexp, self.pexp_r, "pexq", [128, 512], BF),
                (self.ogf, self.ogf_r, "ogf", [128, 128], F32), (self.ckvt, self.ckvt_r, "ckvt", [128, 2, 128], BF),
                (self.krt, self.krt_r, "krt", [128, 64], F32),
            ):
                t, r = self.sb("%s%d" % (nm, i), shp, dt)
                lst.append(t)
                rl.append(r)
        for lst, rl, nm, shp, dt in ((self.tb, self.tb_r, "tb", [128, 4, 128], BF), (self.vb, self.vb_r, "vb", [128, 4, 129], BF)):
            t, r = self.sb("%s2" % nm, shp, dt)
            lst.append(t)
            rl.append(r)
        t, r = self.sb("krt2", [128, 64], F32)
        self.krt.append(t)
        self.krt_r.append(r)
        self.sq, self.sq_r = self.sb("sq", [128, 512], F32)
        self.rp, self.rp_r = self.sb("rp", [128, 6, 64], F32)
        self.mkT, self.mkT_r = self.sb("mkT", [128, 4, 256], BF)
        self.mv, self.mv_r = self.sb("mv", [128, 2, 4, 129], BF)
        self.mkTs, self.mkTs_r = self.sb("mkTs", [128, 4, 256], BF)
        self.mvs, self.mvs_r = self.sb("mvs", [128, 2, 4, 129], BF)
        self.kcA, self.kcA_r = self.sb("kcA", [128, 4, 128], BF)
        self.vcA, self.vcA_r = self.sb("vcA", [128, 4, 129], BF)
        self.kcC, self.kcC_r = self.sb("kcC", [128, 512], BF)
        self.vcC, self.vcC_r = self.sb("vcC", [128, 4, 129], BF)
        self.gcolA, self.gcolA_r = self.sb("gcolA", [128, 128], F32)
        self.gcolB, self.gcolB_r = self.sb("gcolB", [128, 32], F32)
        self.gbc, self.gbc_r = self.sb("gbc", [128, 384], F32)
        self.esink, self.esink_r = self.sb("esink", [128, 24], F32)
        self.ident, self.ident_r = self.sb("ident", [128, 128], F32)
        self.cosT, self.cos_r = self.sb("cosT", [128, TT, 32], F32)
        self.sinT, self.sin_r = self.sb("sinT", [128, TT, 32], F32)
        self.cm05, self.cm05_r = self.sb("cm05", [128, 4], F32)
        self.stat = []
        self.stat_r = []
        for i in range(12):
            t, r = self.sb("stat%d" % i, [128, 16], F32)
            self.stat.append(t)
            self.stat_r.append(r)
        self.xres = [self.scr("xres%d" % i, [2064, 2048], F32) for i in range(2)]
        self.qT_s = self.scr("qT_s", [12, 128, NCOL])
        self.q192_s = self.scr("q192_s", [12, 192, NCOL])
        self.k192_s = self.scr("k192_s", [12, 192, NCOL])
        self.k192s_s = self.scr("k192s_s", [12, 192, 4224])
        self.xqT_s = self.scr("xqT_s", [4, 128, NCOL])
        self.kT_s = self.scr("kT_s", [12, 128, NCOL])
        self.v_s = self.scr("v_s", [12, TT, 128, 129])
        self.gate_s = self.scr("gate_s", [NCOL, 2048])
        self.vs_s = self.scr("vs_s", [12, 33, 128, 129])
        self.FrepA = self.scr("FrepA", [12, 128, 384], F32)
        self.FrepC = self.scr("FrepC", [12, 128, 768], F32)

    def act_ap(self, tt, k):
        return self.actT[:, k, tt * 128: tt * 128 + rows_of(tt)]

    def xt(self, i):
        return self.xta[:, i * 2176: i * 2176 + 2048]

    def cqT(self):
        return self.xta[:, :].bitcast(BF).rearrange("p (k c) -> p k c", k=4)

    def memT(self):
        return self.Rt[:, 0:2048].bitcast(BF).rearrange("p (k c) -> p k c", k=16)

    def Rbf(self):
        return self.Rt[:, :].bitcast(BF)

    def arena_bf(self, i):
        return self.arena[i][:, :].bitcast(BF)

    def wview(self, i, nk, w):
        return self.arena_bf(i)[:, 0:nk * w].rearrange("p (k w) -> p k w", k=nk)

    def prologue(self):
        din = self.din
        self.dma(self.ident[:], din["k_ident"].ap(), self.ident_r, writes=[self.ident_r])
        self.dma(self.cosT[:], din["k_cos"].ap(), self.cos_r, writes=[self.cos_r])
        self.dma(self.sinT[:], din["k_sin"].ap(), self.sin_r, writes=[self.sin_r])
        self.op("pool", "memset", writes=[self.cm05_r], ap=self.cm05[:], constant=-0.5)
        for i in range(len(self.vb)):
            self.op("pool", "memset", writes=[self.vb_r[i]], ap=self.vb[i][:], constant=1.0)
        for t, r in ((self.mv, self.mv_r), (self.mvs, self.mvs_r)):
            self.op("pool", "memset", writes=[r], ap=t[:], constant=1.0)
        for t, r in ((self.vcA, self.vcA_r), (self.vcC, self.vcC_r)):
            self.op("pool", "memset", writes=[r], ap=t[:], constant=1.0)
        ga = self.ld[0]
        gar = self.ld_r[0]
        self.dma(ga[0:64, 0:128], din["norm_g"].ap().rearrange("l (k c) -> (l k) c", c=128), gar, adds=[gar])
        self.dma(ga[64:128, 0:128], din["mem_norm_g"].ap().rearrange("l (k c) -> (l k) c", c=128), gar, adds=[gar])
        gb_ = self.ld[1]
        gbr = self.ld_r[1]
        self.op("dve", "memset", writes=[gbr, self.gbm_r], ap=gb_[0:32, 0:128], constant=0.0)
        rows = [("xq_norm_g", 0, 4, None), ("xk_norm_g", 4, 4, None), ("a_q_norm_g", 8, 2, None),
                ("a_k_norm_g", 10, 2, None), ("c_q_norm_g", 12, 1, None), ("c_k_norm_g", 13, 1, None)]
        for nm, r0, n, _ in rows:
            self.dma(gb_[r0:r0 + n, 0:128], din[nm].ap(), gbr, reads=[self.gbm_r], adds=[gbr])
        self.dma(gb_[14:18, 0:128], din["b_cq_norm_g"].ap().rearrange("o (k c) -> (o k) c", c=128), gbr, reads=[self.gbm_r], adds=[gbr])
        self.dma(gb_[18:20, 0:128], din["b_ckv_norm_g"].ap().rearrange("o (k c) -> (o k) c", c=128), gbr, reads=[self.gbm_r], adds=[gbr])
        self.dma(gb_[20:21, 0:128], din["b_q_norm_g"].ap()[:, 0:128], gbr, reads=[self.gbm_r], adds=[gbr])
        self.dma(gb_[21:22, 0:64], din["b_q_norm_g"].ap()[:, 128:192], gbr, reads=[self.gbm_r], adds=[gbr])
        self.dma(gb_[22:23, 0:128], din["b_k_norm_g"].ap()[:, 0:128], gbr, reads=[self.gbm_r], adds=[gbr])
        self.dma(gb_[23:24, 0:64], din["b_k_norm_g"].ap()[:, 128:192], gbr, reads=[self.gbm_r], adds=[gbr])
        tpb, tpr = self.nb("tp")
        self.op("pe", "transpose", reads=[gar, self.ident_r], writes=[tpr], out=tpb[:, 0:128], in_=ga[:, 0:128],
                identity=self.ident[:])
        self.op("act", "activation", reads=[tpr], writes=[self.gcolA_r], out=self.gcolA[:], in_=tpb[:, 0:128], func=AF.Copy)
        tpb, tpr = self.nb("tp")
        self.op("pe", "transpose", reads=[gbr, self.ident_r], writes=[tpr], out=tpb[:, 0:32], in_=gb_[0:32, 0:128],
                identity=self.ident[0:32, 0:32])
        self.op("act", "activation", reads=[tpr], writes=[self.gcolB_r], out=self.gcolB[:], in_=tpb[:, 0:32], func=AF.Copy)
        self.dma(self.esink[:], din["a_sink"].ap().rearrange("a h -> (a h)").partition_broadcast(128), self.esink_r,
                 writes=[self.esink_r])
        self.op("act", "activation", reads=[self.esink_r], writes=[self.esink_r], out=self.esink[:], in_=self.esink[:],
                func=AF.Exp)
        t5 = self.zf[0]
        t5r = self.zf_r[0]
        oh = self.of[0]
        ohr = self.of_r[0]
        self.dma(t5[0:32, 0:12], din["t5_bias"].ap(), t5r, writes=[t5r])
        self.dma(oh[0:32, 0:384], din["k_t5oh"].ap(), ohr, writes=[ohr])
        ab, ar = self.nb("acc")
        self.op("pe", "matmul", reads=[t5r, ohr], writes=[ar], out=ab[0:12, 0:384], lhsT=t5[0:32, 0:12],
                rhs=oh[0:32, 0:384], start=True, stop=True)
        fa = self.zf[1]
        far = self.zf_r[1]
        self.op("dve", "tensor_copy", reads=[ar], writes=[far], out=fa[0:12, 0:384], in_=ab[0:12, 0:384])
        FA, FAr = self.FrepA
        self.dma(FA.ap(), fa[0:12, 0:384].unsqueeze(1).broadcast_to([12, 128, 384]), far, reads=[far], writes=[FAr])
        fc = self.xta
        fcr = self.xt_r[0]
        self.dma(fc[0:12, 0:257], din["c_rel_bias"].ap()[0], fcr, writes=[fcr])
        self.op("dve", "tensor_copy", reads=[fcr], writes=[fcr], out=fc[0:12, 257:768],
                in_=fc[0:12, 256:257].broadcast_to([12, 511]))
        FC, FCr = self.FrepC
        self.dma(FC.ap(), fc[0:12, 0:768].unsqueeze(1).broadcast_to([12, 128, 768]), fcr, reads=[fcr], writes=[FCr])

    def norm_transpose(self, srcs, gcol0, dst_fn, dres_fn):
        info = {}

        def stage_a1(idx):
            src, rows, key = srcs[idx]
            i = self.rot("xt", 2)
            xt = self.xt(i)
            xr = self.xt_r[i]
            self.dma(xt[0:rows, :], src, xr, writes=[xr, self.xta_r])
            st, sr = self.newstat()
            for c in range(4):
                self.op("act", "activation", reads=[xr], writes=[self.ps_r[4 + c], sr], out=self.bank(4 + c)[0:rows, :],
                        in_=xt[0:rows, c * 512:(c + 1) * 512], func=AF.Square, accum_out=st[0:rows, 4 + c:5 + c])
            info[idx] = (xt, xr, st, sr)

        def stage_a2(idx):
            src, rows, key = srcs[idx]
            xt, xr, st, sr = info[idx]
            self.op("dve", "tensor_reduce", reads=[sr], writes=[sr], out=st[0:rows, 0:1], in_=st[0:rows, 4:8], axis=AX.X,
                    op=ALU.add)
            self.op("dve", "tensor_scalar", reads=[sr], writes=[sr], out=st[0:rows, 1:2], in0=st[0:rows, 0:1],
                    scalar1=1.0 / D, scalar2=EPS, op0=ALU.mult, op1=ALU.add)
            self.op("pool", "tensor_tensor", reads=[sr, self.cm05_r], writes=[sr], out=st[0:rows, 2:3],
                    in0=st[0:rows, 1:2], in1=self.cm05[0:rows, 0:1], op=ALU.pow)

        def stage_b(idx):
            src, rows, key = srcs[idx]
            xt, xr, st, sr = info.pop(idx)
            self.op("dve", "tensor_scalar", reads=[sr, xr], writes=[xr], out=xt[0:rows, :], in0=xt[0:rows, :],
                    scalar1=st[0:rows, 2:3], scalar2=None, op0=ALU.mult)
            for j in range(4):
                tpb, tpr = self.nb("tp")
                tpv = tpb.rearrange("p (a b) -> p a b", a=4)
                for c in range(4):
                    k = 4 * j + c
                    self.op("pe", "transpose", reads=[xr, self.ident_r, self.xta_r], writes=[tpr], out=tpv[:, c, 0:rows],
                            in_=xt[0:rows, k * 128:(k + 1) * 128], identity=self.ident[0:rows, 0:rows])
                if j == 3:
                    for c in range(4):
                        k = 4 * j + c
                        self.op("act", "activation", reads=[tpr, self.gcolA_r], writes=[dres_fn(key)],
                                out=dst_fn(key, j)[:, c, :], in_=tpv[:, c, 0:rows], func=AF.Copy,
                                scale=self.gcolA[:, gcol0 + k: gcol0 + k + 1])
                else:
                    g = self.gcolA[:, gcol0 + 4 * j: gcol0 + 4 * j + 4].unsqueeze(2).broadcast_to([128, 4, rows])
                    self.op("dve", "tensor_tensor", reads=[tpr, self.gcolA_r], writes=[dres_fn(key)], out=dst_fn(key, j),
                            in0=tpv[:, 0:4, 0:rows], in1=g, op=ALU.mult)

        n = len(srcs)
        stage_a1(0)
        stage_a2(0)
        for idx in range(n):
            if idx + 1 < n:
                stage_a1(idx + 1)
            stage_b(idx)
            if idx + 1 < n:
                stage_a2(idx + 1)

    def proj(self, W, wcols, nk, blocks, tiles, act_fn, epilogue, tile_outer=False):
        wh, woff = W
        if tile_outer:
            ai = self.rot("arena", 2)
            ar = self.arena_r[ai]
            tot = sum(w for _, w in blocks)
            wv = self.arena_bf(ai)[:, 0:nk * tot].rearrange("p (k w) -> p k w", k=nk)
            pos = 0
            wpos = []
            for (c0, w) in blocks:
                src = AP(wh, woff + c0, [[wcols, 128], [128 * wcols, nk], [1, w]])
                self.dma(wv[:, :, pos:pos + w], src, ar, adds=[ar], q="pool")
                wpos.append(pos)
                pos += w
            for (key, rows, rd) in tiles:
                for bi, (c0, w) in enumerate(blocks):
                    acc, accr = self.nb("acc")
                    for k in range(nk):
                        self.op("pe", "matmul", reads=[ar] + rd, writes=[accr], out=acc[0:rows, 0:w],
                                lhsT=act_fn(key, k), rhs=wv[:, k, wpos[bi]:wpos[bi] + w], start=(k == 0), stop=(k == nk - 1))
                    self.step()
                    epilogue(key, rows, bi, acc[0:rows, 0:w], accr)
            self.flush()
            return
        def issue_w(bi):
            c0, w = blocks[bi]
            ai = self.rot("arena", 2)
            ar = self.arena_r[ai]
            wv = self.wview(ai, nk, w)
            src = AP(wh, woff + c0, [[wcols, 128], [128 * wcols, nk], [1, w]])
            self.dma(wv, src, ar, writes=[ar], q="pool")
            return ar, wv

        cur = issue_w(0)
        for bi, (c0, w) in enumerate(blocks):
            nxt = issue_w(bi + 1) if bi + 1 < len(blocks) else None
            ar, wv = cur
            cur = nxt
            for (key, rows, rd) in tiles:
                acc, accr = self.nb("acc")
                for k in range(nk):
                    self.op("pe", "matmul", reads=[ar] + rd, writes=[accr], out=acc[0:rows, 0:w],
                            lhsT=act_fn(key, k), rhs=wv[:, k, 0:w], start=(k == 0), stop=(k == nk - 1))
                self.step()
                epilogue(key, rows, bi, acc[0:rows, 0:w], accr)
        self.flush()

    def headnorm(self, acc, accr, rows, nh, hd, extra_ss=None):
        w = nh * hd
        self.op("act", "activation", reads=[accr], writes=[self.sq_r], out=self.sq[0:rows, 0:w], in_=acc, func=AF.Square)
        st, sr = self.newstat()
        self.op("dve", "tensor_reduce", reads=[self.sq_r], writes=[sr], out=st[0:rows, 0:nh],
                in_=self.sq[0:rows, 0:w].rearrange("p (a b) -> p a b", a=nh), axis=AX.X, op=ALU.add)
        self.op("dve", "tensor_scalar", reads=[sr], writes=[sr], out=st[0:rows, 4:4 + nh], in0=st[0:rows, 0:nh],
                scalar1=1.0 / hd, scalar2=EPS, op0=ALU.mult, op1=ALU.add)
        self.op("pool", "tensor_tensor", reads=[sr, self.cm05_r], writes=[sr], out=st[0:rows, 8:8 + nh],
                in0=st[0:rows, 4:4 + nh], in1=self.cm05[0:rows, 0:nh], op=ALU.pow)
        i = self.rot("zf", 2)
        zf = self.zf[i]
        zr = self.zf_r[i]
        zv = zf[0:rows, 0:w].rearrange("p (a b) -> p a b", a=nh)
        self.defer(lambda: self.op("dve", "tensor_tensor", reads=[accr, sr], writes=[zr], out=zv,
                                   in0=acc.rearrange("p (a b) -> p a b", a=nh),
                                   in1=st[0:rows, 8:8 + nh].unsqueeze(2).broadcast_to([rows, nh, hd]), op=ALU.mult), 1)
        return zv, zr

    def transpose_out(self, zv, zr, rows, nh, gcol, dst_sb=None, dst_res=None, dst_dram=None, dram_res=None):
        self.defer(lambda: self._transpose_out(zv, zr, rows, nh, gcol, dst_sb, dst_res, dst_dram, dram_res), 3)

    def _transpose_out(self, zv, zr, rows, nh, gcol, dst_sb, dst_res, dst_dram, dram_res):
        tpb, tpr = self.nb("tp")
        tpv = tpb.rearrange("p (a b) -> p a b", a=4)
        for c in range(nh):
            self.op("pe", "transpose", reads=[zr, self.ident_r], writes=[tpr], out=tpv[:, c, 0:rows], in_=zv[:, c, :],
                    identity=self.ident[0:rows, 0:rows])
        if dst_sb is not None:
            self.op("act", "activation", reads=[tpr, self.gcolB_r], writes=[dst_res], out=dst_sb, in_=tpv[:, 0:nh, 0:rows],
                    func=AF.Copy, scale=gcol)
            return
        i = self.rot("tb", 3)
        tb = self.tb[i]
        tr = self.tb_r[i]
        self.op("act", "activation", reads=[tpr, self.gcolB_r], writes=[tr], out=tb[:, 0:nh, 0:rows],
                in_=tpv[:, 0:nh, 0:rows], func=AF.Copy, scale=gcol)
        self.dma(dst_dram, tb[:, 0:nh, 0:rows], tr, reads=[tr], adds=[dram_res])

    def rows_out(self, src, src_res, rows, w, dst):
        i = self.rot("of", 2)
        of = self.of[i]
        orr = self.of_r[i]
        self.op("dve", "tensor_copy", reads=[src_res], writes=[orr], out=of[0:rows, 0:w], in_=src)
        self.dma(dst, of[0:rows, 0:w], orr, reads=[orr], is_output=True)

    def gain_rows_out(self, zv, zr, rows, nh, g_ap, dst):
        self.defer(lambda: self._gain_rows_out(zv, zr, rows, nh, g_ap, dst), 1)

    def _gain_rows_out(self, zv, zr, rows, nh, g_ap, dst):
        i = self.rot("of", 2)
        of = self.of[i]
        orr = self.of_r[i]
        self.op("dve", "tensor_tensor", reads=[zr, self.gbc_r], writes=[orr],
                out=of[0:rows, 0:nh * 128].rearrange("p (a b) -> p a b", a=nh), in0=zv,
                in1=g_ap.unsqueeze(1).broadcast_to([rows, nh, 128]), op=ALU.mult)
        self.dma(dst, of[0:rows, 0:nh * 128], orr, reads=[orr], is_output=True)

    def v_out(self, accv, accr, rows, nh, dst_dram, dram_res, dst_sb=None, dst_res=None):
        if dst_sb is not None:
            self.op("act", "activation", reads=[accr], writes=[dst_res], out=dst_sb, in_=accv, func=AF.Copy)
            return
        i = self.rot("vb", 3)
        vb = self.vb[i]
        vr = self.vb_r[i]
        self.op("act", "activation", reads=[accr], writes=[vr], out=vb[0:rows, 0:nh, 0:128], in_=accv, func=AF.Copy)
        self.dma(dst_dram, vb[0:rows, 0:nh, :], vr, reads=[vr], adds=[dram_res])

    def gate_out(self, acc, accr, rows, w, tt, gc0):
        i = self.rot("gb", 2)
        gb = self.gb[i]
        gr = self.gb_r[i]
        self.op("act", "activation", reads=[accr], writes=[gr], out=gb[0:rows, 0:w], in_=acc, func=AF.Silu)
        G, Gr = self.gate_s
        self.dma(AP(G, tt * 128 * 2048 + gc0, [[2048, rows], [1, w]]), gb[0:rows, 0:w], gr, reads=[gr], adds=[Gr])

    def qk_store(self, scr, h0, nh, tt, rows, width=NCOL, dpart=128, hrows=None, r0=0):
        t, _ = scr
        hrows = hrows or dpart
        return AP(t, (h0 * hrows + r0) * width + tt * 128, [[width, dpart], [hrows * width, nh], [1, rows]])

    def v_store(self, scr, h0, nh, tile, rows, ntiles=TT):
        t, _ = scr
        return AP(t, (h0 * ntiles + tile) * 128 * 129, [[129, rows], [ntiles * 128 * 129, nh], [1, 129]])

    def attn(self, kts, nq, scale, sink_ap, gate_ap, gate_reads, dst_ap, dst_res):
        chunks = []
        cur = []
        for kt in kts:
            if cur:
                p = cur[-1]
                brk = (len(cur) * nq >= 512 or kt["nk"] != p["nk"] or (kt["bias"] is None) != (p["bias"] is None)
                       or (kt["bias"] is not None and kt["bias"][1] != p["bias"][1] + 1))
                if brk:
                    chunks.append(cur)
                    cur = []
            cur.append(kt)
        chunks.append(cur)
        assert len(chunks) <= 4
        early = len(chunks) > 2
        if early:
            self.step()
        ob, orr = self.nb("o")
        staged = []
        for ci, ch in enumerate(chunks):
            sbk, sr = self.nb("s")
            nk = ch[0]["nk"]
            n = len(ch)
            for i, kt in enumerate(ch):
                m = len(kt["qk"])
                for pi, (l, r) in enumerate(kt["qk"]):
                    self.op("pe", "matmul", reads=kt["reads"], writes=[sr], out=sbk[0:nk, i * nq:(i + 1) * nq], lhsT=l, rhs=r,
                            start=(pi == 0), stop=(pi == m - 1))
            ip = self.rot("pexp", 4)
            pe_t = self.pexp[ip]
            per = self.pexp_r[ip]
            if ch[0]["bias"] is not None:
                j = self.rot("zf", 2)
                tm = self.zf[j]
                tmr = self.zf_r[j]
                self.op("act", "activation", reads=[sr], writes=[tmr], out=tm[0:nk, 0:n * nq], in_=sbk[0:nk, 0:n * nq],
                        func=AF.Exp, scale=scale)
                bv, s0 = ch[0]["bias"]
                self.op("dve", "tensor_tensor", reads=[tmr] + ch[0]["reads"], writes=[per],
                        out=pe_t[0:nk, 0:n * nq].rearrange("p (a b) -> p a b", a=n),
                        in0=tm[0:nk, 0:n * nq].rearrange("p (a b) -> p a b", a=n), in1=bv[0:nk, s0:s0 + n, 0:nq], op=ALU.mult)
            else:
                self.op("act", "activation", reads=[sr], writes=[per], out=pe_t[0:nk, 0:n * nq], in_=sbk[0:nk, 0:n * nq],
                        func=AF.Exp, scale=scale)
            for i, kt in enumerate(ch):
                if kt.get("diag"):
                    self.op("pool", "memset", writes=[per], ap=pe_t[64:128, i * nq:i * nq + 64], constant=0.0)
            staged.append((ch, pe_t, per, nk))
        tot = len(kts)
        if not early:
            self.step()

        def part2():
            idx = 0
            for ch, pe_t, per, nk in staged:
                for i, kt in enumerate(ch):
                    self.op("pe", "matmul", reads=[per] + kt["reads"], writes=[orr], out=ob[0:nq, 0:129],
                            lhsT=pe_t[0:nk, i * nq:(i + 1) * nq], rhs=kt["v"], start=(idx == 0), stop=(idx == tot - 1))
                    idx += 1
            self.attn_post(ob, orr, nq, sink_ap, gate_ap, gate_reads, dst_ap, dst_res, tail_delay=(1 if dfr else 2))

        dfr = len(chunks) <= 2
        if dfr:
            self.defer(part2, 1)
        else:
            part2()

    def attn_post(self, ob, orr, nq, sink_ap, gate_ap, gate_reads, dst_ap, dst_res, tail_delay=2):
        st, sr = self.newstat()
        if sink_ap is not None:
            self.op("dve", "tensor_tensor", reads=[orr, self.esink_r], writes=[sr], out=st[0:nq, 0:1],
                    in0=ob[0:nq, 128:129], in1=sink_ap, op=ALU.add)
            self.op("dve", "reciprocal", reads=[sr], writes=[sr], out=st[0:nq, 1:2], in_=st[0:nq, 0:1])
        else:
            self.op("dve", "reciprocal", reads=[orr], writes=[sr], out=st[0:nq, 1:2], in_=ob[0:nq, 128:129])
        i = self.rot("ogf", 2)
        og = self.ogf[i]
        ogr = self.ogf_r[i]
        self.op("dve", "scalar_tensor_tensor", reads=[orr, sr] + gate_reads, writes=[ogr], out=og[0:nq, :],
                in0=ob[0:nq, 0:128], scalar=st[0:nq, 1:2], in1=gate_ap, op0=ALU.mult, op1=ALU.mult)

        def tail():
            tpb, tpr = self.nb("tp")
            self.op("pe", "transpose", reads=[ogr, self.ident_r], writes=[tpr], out=tpb[:, 0:nq], in_=og[0:nq, :],
                    identity=self.ident[0:nq, 0:nq])
            self.op("act", "activation", reads=[tpr], writes=[dst_res], out=dst_ap, in_=tpb[:, 0:nq], func=AF.Copy)
        self.defer(tail, tail_delay)

    A_Q, A_G, A_K, A_V, A_B = 0, 2176, 4352, 6528, 8736

    def load_head(self, ai, q_src, gate_col, k_src, v_src, kw=NCOL, with_kv=True):
        ar = self.arena_r[ai]
        ab = self.arena_bf(ai)
        qh, qoff, qres = q_src
        self.dma(ab[:, self.A_Q:self.A_Q + 2064], AP(qh, qoff, [[NCOL, 128], [1, 2064]]), ar, reads=[qres], adds=[ar])
        G, Gr = self.gate_s
        gv = ab[:, self.A_G:self.A_G + 2176].rearrange("p (t c) -> p t c", t=TT)
        self.dma(gv[:, 0:16, :], AP(G, gate_col, [[2048, 128], [128 * 2048, 16], [1, 128]]), ar, reads=[Gr], adds=[ar])
        self.dma(gv[0:16, 16, :], AP(G, 2048 * 2048 + gate_col, [[2048, 16], [1, 128]]), ar, reads=[Gr], adds=[ar])
        if with_kv:
            kh, koff, kres = k_src
            self.dma(ab[:, self.A_K:self.A_K + 2064], AP(kh, koff, [[kw, 128], [1, 2064]]), ar, reads=[kres], adds=[ar])
            vh, voff, vres = v_src
            vv = ab[:, self.A_V:self.A_V + TT * 129].rearrange("p (t c) -> p t c", t=TT)
            self.dma(vv[:, 0:16, :], AP(vh, voff, [[129, 128], [128 * 129, 16], [1, 129]]), ar, reads=[vres], adds=[ar])
            self.dma(vv[0:16, 16, :], AP(vh, voff + 16 * 128 * 129, [[129, 16], [1, 129]]), ar, reads=[vres], adds=[ar])
        return ab, gv

    def bias_view(self, ai, n):
        f0 = self.A_B // 2
        return self.arena[ai][:, f0:f0 + n * 128].rearrange("p (a b) -> p a b", a=n)

    def load_bias(self, ai, Frep, L, h, offs_nk_nq):
        ar = self.arena_r[ai]
        bv = self.bias_view(ai, len(offs_nk_nq))
        Ft, Fr = Frep
        for i, (off, nk, nq) in enumerate(offs_nk_nq):
            self.dma(bv[0:nk, i, 0:nq], AP(Ft, h * 128 * L + off, [[L - 1, nk], [1, nq]]), ar, reads=[Fr], adds=[ar])
        return bv

    def layer(self, L):
        kind, j = L % 3, L // 3
        din = self.din
        if kind == 0:
            self.dma(self.gbc[:, 0:128], din["a_k_norm_g"].ap()[j].partition_broadcast(128), self.gbc_r, writes=[self.gbc_r])
        elif kind == 1:
            self.dma(self.gbc[:, 0:256], din["b_ckv_norm_g"].ap()[0].partition_broadcast(128), self.gbc_r, writes=[self.gbc_r])
        else:
            self.dma(self.gbc[:, 0:128], din["c_k_norm_g"].ap()[0].partition_broadcast(128), self.gbc_r, writes=[self.gbc_r])
        self.dma(self.gbc[:, 256:384], din["xk_norm_g"].ap()[L].partition_broadcast(128), self.gbc_r, adds=[self.gbc_r])
        if L == 0:
            xp = din["x_prompt"]
            xs = din["x_sample"]
            srcs = [(AP(xp, t * 128 * D, [[D, 128], [1, D]]), 128, t) for t in range(16)]
            srcs.append((AP(xs, 0, [[D, 16], [1, D]]), 16, 16))
            xrd = []
        else:
            xh, xr_ = self.xres[(L - 1) % 2]
            srcs = [(AP(xh, t * 128 * D, [[D, rows_of(t)], [1, D]]), rows_of(t), t) for t in range(TT)]
            xrd = [xr_]
        self._xsrc = srcs
        self.norm_transpose_x(srcs, xrd, L * 16)
        if self.chk(L, "N"):
            return
        self.mem_phase(L)
        if self.chk(L, "M"):
            return
        if kind == 0:
            self.layer_a(L, j)
        elif kind == 1:
            self.layer_b(L)
        else:
            self.layer_c(L)
        if self.stopped or self.chk(L, "T"):
            return
        self.mem_heads(L)
        if self.chk(L, "X"):
            return
        self.out_phase(L)

    def norm_transpose_x(self, srcs, xrd, gcol0):
        S = self
        orig = S.dma

        def dst(key, jj):
            rows = rows_of(key)
            return S.actT[:, 4 * jj:4 * jj + 4, key * 128: key * 128 + rows]

        if xrd:
            def dma2(out, in_, sres, reads=(), writes=(), adds=(), q="sp", is_output=False):
                return orig(out, in_, sres, reads=list(reads) + xrd, writes=writes, adds=adds, q=q, is_output=is_output)
            S.dma = dma2
        try:
            S.norm_transpose(srcs, gcol0, dst, lambda key: S.act_r[key])
        finally:
            S.dma = orig

    def mem_phase(self, L):
        din = self.din
        mp = din["mem_prompt"]
        srcs = [(AP(mp, t * 128 * D, [[D, 128], [1, D]]), 128, t) for t in range(2)]
        memT = self.memT()
        self.norm_transpose(srcs, 64 + L * 16, lambda key, jj: memT[:, 4 * jj:4 * jj + 4, key * 128:(key + 1) * 128],
                            lambda key: self.R_r)
        if self.chk(L, "M1"):
            return
        mko = self.dout["mem_k_prompt"]
        mvo = self.dout["mem_v_prompt"]

        dbg = "abcde"

        def ep(key, rows, bi, acc, accr):
            if bi == 0:
                if "a" not in dbg:
                    return
                zv, zr = self.headnorm(acc, accr, rows, 4, 128)
                if "b" in dbg:
                    self.transpose_out(zv, zr, rows, 4, self.gcolB[:, 4 + L:5 + L],
                                       dst_sb=self.mkT[:, :, key * 128:(key + 1) * 128], dst_res=self.mkT_r)
                if "c" in dbg:
                    self.gain_rows_out(zv, zr, rows, 4, self.gbc[0:rows, 256:384],
                                       AP(mko, L * 256 * 512 + key * 128 * 512, [[512, rows], [1, 512]]))
            else:
                accv = acc.rearrange("p (a b) -> p a b", a=4)
                if "d" in dbg:
                    self.v_out(accv, accr, rows, 4, None, None, dst_sb=self.mv[:, key, :, 0:128], dst_res=self.mv_r)
                if "e" in dbg:
                    self.rows_out(acc, accr, rows, 512, AP(mvo, L * 256 * 512 + key * 128 * 512, [[512, rows], [1, 512]]))

        self.proj((din["w_mem_kv"], L * 2048 * 1024), 1024, 16, [(0, 512), (512, 512)],
                  [(t, 128, [self.R_r]) for t in range(2)], lambda key, k: memT[:, k, key * 128:(key + 1) * 128], ep)
        if self.chk(L, "M2"):
            return
        ck = din["cache_mem_k"]
        cv = din["cache_mem_v"]
        for t in range(2):
            i = self.rot("ld", 2)
            ld = self.ld[i]
            lr = self.ld_r[i]
            self.dma(ld[:, :], AP(ck, L * 256 * 512 + t * 128 * 512, [[512, 128], [1, 512]]), lr, writes=[lr])
            tpb, tpr = self.nb("tp")
            tpv = tpb.rearrange("p (a b) -> p a b", a=4)
            for c in range(4):
                self.op("pe", "transpose", reads=[lr, self.ident_r], writes=[tpr], out=tpv[:, c, :],
                        in_=ld[:, c * 128:(c + 1) * 128], identity=self.ident[:])
            self.op("act", "activation", reads=[tpr], writes=[self.mkTs_r], out=self.mkTs[:, :, t * 128:(t + 1) * 128],
                    in_=tpv[:, 0:4, :], func=AF.Copy)
            i = self.rot("ld", 2)
            ld = self.ld[i]
            lr = self.ld_r[i]
            self.dma(ld[:, :], AP(cv, L * 256 * 512 + t * 128 * 512, [[512, 128], [1, 512]]), lr, writes=[lr])
            self.op("dve", "tensor_copy", reads=[lr], writes=[self.mvs_r], out=self.mvs[:, t, :, 0:128],
                    in_=ld[:, :].rearrange("p (a b) -> p a b", a=4))

    def x_tiles(self):
        return [(t, rows_of(t), [self.act_r[t]]) for t in range(TT)]

    def layer_a(self, L, j):
        din = self.din
        dout = self.dout
        gq = self.gcolB[:, 8 + j:9 + j]
        gk = self.gcolB[:, 10 + j:11 + j]
        gxq = self.gcolB[:, L:L + 1]
        blocks = [(c * 512, 512) for c in range(10)]

        def ep(tt, rows, bi, acc, accr):
            if bi < 3:
                zv, zr = self.headnorm(acc, accr, rows, 4, 128)
                self.transpose_out(zv, zr, rows, 4, gq, dst_dram=self.qk_store(self.qT_s, bi * 4, 4, tt, rows),
                                   dram_res=self.qT_s[1])
            elif bi == 3:
                zv, zr = self.headnorm(acc, accr, rows, 4, 128)
                self.transpose_out(zv, zr, rows, 4, gk, dst_dram=self.qk_store(self.kT_s, 0, 4, tt, rows),
                                   dram_res=self.kT_s[1])
                if tt == 15:
                    self.gain_rows_out(zv, zr, rows, 4, self.gbc[0:rows, 0:128],
                                       AP(dout["a_k_prompt"], j * 128 * 512, [[512, 128], [1, 512]]))
                elif tt == 16:
                    self.gain_rows_out(zv, zr, rows, 4, self.gbc[0:rows, 0:128],
                                       AP(dout["a_k_sample"], j * 16 * 512, [[512, 16], [1, 512]]))
            elif bi == 4:
                accv = acc.rearrange("p (a b) -> p a b", a=4)
                self.v_out(accv, accr, rows, 4, self.v_store(self.v_s, 0, 4, tt, rows), self.v_s[1])
                if tt == 15:
                    self.rows_out(acc, accr, rows, 512, AP(dout["a_v_prompt"], j * 128 * 512, [[512, 128], [1, 512]]))
                elif tt == 16:
                    self.rows_out(acc, accr, rows, 512, AP(dout["a_v_sample"], j * 16 * 512, [[512, 16], [1, 512]]))
            elif bi == 5:
                zv, zr = self.headnorm(acc, accr, rows, 4, 128)
                self.transpose_out(zv, zr, rows, 4, gxq, dst_dram=self.qk_store(self.xqT_s, 0, 4, tt, rows),
                                   dram_res=self.xqT_s[1])
            else:
                self.gate_out(acc, accr, rows, 512, tt, (bi - 6) * 512)

        self.proj((din["a_w_in"], j * 2048 * 5120), 5120, 16, blocks, self.x_tiles(), self.act_ap, ep)
        if self.chk(L, "P"):
            return
        i = self.rot("ld", 2)
        ld = self.ld[i]
        lr = self.ld_r[i]
        self.dma(ld[:, :], AP(din["cache_a_k"], j * 128 * 512, [[512, 128], [1, 512]]), lr, writes=[lr])
        tpb, tpr = self.nb("tp")
        tpv = tpb.rearrange("p (a b) -> p a b", a=4)
        for c in range(4):
            self.op("pe", "transpose", reads=[lr, self.ident_r], writes=[tpr], out=tpv[:, c, :],
                    in_=ld[:, c * 128:(c + 1) * 128], identity=self.ident[:])
        self.op("act", "activation", reads=[tpr], writes=[self.kcA_r], out=self.kcA[:], in_=tpv[:, 0:4, :], func=AF.Copy)
        i = self.rot("ld", 2)
        ld = self.ld[i]
        lr = self.ld_r[i]
        self.dma(ld[:, :], AP(din["cache_a_v"], j * 128 * 512, [[512, 128], [1, 512]]), lr, writes=[lr])
        self.op("dve", "tensor_copy", reads=[lr], writes=[self.vcA_r], out=self.vcA[:, :, 0:128],
                in_=ld[:, :].rearrange("p (a b) -> p a b", a=4))
        sc = 128 ** -0.5
        for h in range(12):
            kh = h // 3
            ai = self.rot("arena", 2)
            ar = self.arena_r[ai]
            ab, gv = self.load_head(ai, (self.qT_s[0], h * 128 * NCOL, self.qT_s[1]), h * 128,
                                    (self.kT_s[0], kh * 128 * NCOL, self.kT_s[1]),
                                    (self.v_s[0], kh * TT * 128 * 129, self.v_s[1]))
            bv = self.load_bias(ai, self.FrepA, 384, h,
                                [(256, 128, 128), (128, 128, 128), (256, 128, 16), (128, 16, 16)])
            self.op("pool", "memset", writes=[ar], ap=bv[64:128, 1, 0:64], constant=NEG)
            self.op("pool", "memset", writes=[ar], ap=bv[0:64, 0, 64:128], constant=NEG)
            self.op("act", "activation", writes=[ar], out=bv[:, 0:2, :], in_=bv[:, 0:2, :], func=AF.Exp)
            self.op("act", "activation", writes=[ar], out=bv[:, 2, 0:16], in_=bv[:, 2, 0:16], func=AF.Exp)
            self.op("act", "activation", writes=[ar], out=bv[0:16, 3, 0:16], in_=bv[0:16, 3, 0:16], func=AF.Exp)
            q = ab[:, self.A_Q:self.A_Q + NCOL]
            k = ab[:, self.A_K:self.A_K + NCOL]
            vv = ab[:, self.A_V:self.A_V + TT * 129].rearrange("p (t c) -> p t c", t=TT)
            sink = self.esink[:, j * 12 + h: j * 12 + h + 1]
            for t in range(16):
                kts = []
                if t >= 1:
                    kts.append(dict(qk=[(k[:, (t - 1) * 128:t * 128], q[:, t * 128:(t + 1) * 128])], nk=128,
                                    v=vv[:, t - 1, :], bias=(bv, 0), reads=[ar]))
                kts.append(dict(qk=[(k[:, t * 128:(t + 1) * 128], q[:, t * 128:(t + 1) * 128])], nk=128,
                                v=vv[:, t, :], bias=(bv, 1), reads=[ar]))
                self.attn(kts, 128, sc, sink, gv[:, t, :], [ar], self.actT[:, h, t * 128:(t + 1) * 128], self.act_r[t])
            qs = q[:, 2048:2064]
            kts = [dict(qk=[(self.kcA[:, kh, :], qs)], nk=128, v=self.vcA[:, kh, :], bias=(bv, 2),
                        reads=[ar, self.kcA_r, self.vcA_r]),
                   dict(qk=[(k[:, 2048:2064], qs)], nk=16, v=vv[0:16, 16, :], bias=(bv, 3), reads=[ar])]
            self.attn(kts, 16, sc, sink[0:16, :], gv[0:16, 16, :], [ar], self.actT[:, h, 2048:2064], self.act_r[16])
            self.flush()

    def layer_c(self, L):
        din = self.din
        dout = self.dout
        gq = self.gcolB[:, 12:13]
        gk = self.gcolB[:, 13:14]
        gxq = self.gcolB[:, L:L + 1]
        blocks = [(c * 512, 512) for c in range(14)]

        def ep(tt, rows, bi, acc, accr):
            if bi < 3:
                zv, zr = self.headnorm(acc, accr, rows, 4, 128)
                self.transpose_out(zv, zr, rows, 4, gq, dst_dram=self.qk_store(self.qT_s, bi * 4, 4, tt, rows),
                                   dram_res=self.qT_s[1])
            elif bi < 6:
                b = bi - 3
                zv, zr = self.headnorm(acc, accr, rows, 4, 128)
                self.transpose_out(zv, zr, rows, 4, gk, dst_dram=self.qk_store(self.kT_s, b * 4, 4, tt, rows),
                                   dram_res=self.kT_s[1])
                if 12 <= tt < 16:
                    self.gain_rows_out(zv, zr, rows, 4, self.gbc[0:rows, 0:128],
                                       AP(dout["c_k_prompt"], (tt - 12) * 128 * 1536 + b * 512, [[1536, 128], [1, 512]]))
                elif tt == 16:
                    self.gain_rows_out(zv, zr, rows, 4, self.gbc[0:rows, 0:128],
                                       AP(dout["c_k_sample"], b * 512, [[1536, 16], [1, 512]]))
            elif bi < 9:
                b = bi - 6
                accv = acc.rearrange("p (a b) -> p a b", a=4)
                self.v_out(accv, accr, rows, 4, self.v_store(self.v_s, b * 4, 4, tt, rows), self.v_s[1])
                if 12 <= tt < 16:
                    self.rows_out(acc, accr, rows, 512,
                                  AP(dout["c_v_prompt"], (tt - 12) * 128 * 1536 + b * 512, [[1536, 128], [1, 512]]))
                elif tt == 16:
                    self.rows_out(acc, accr, rows, 512, AP(dout["c_v_sample"], b * 512, [[1536, 16], [1, 512]]))
            elif bi == 9:
                zv, zr = self.headnorm(acc, accr, rows, 4, 128)
                self.transpose_out(zv, zr, rows, 4, gxq, dst_dram=self.qk_store(self.xqT_s, 0, 4, tt, rows),
                                   dram_res=self.xqT_s[1])
            else:
                self.gate_out(acc, accr, rows, 512, tt, (bi - 10) * 512)

        self.proj((din["c_w_in"], 0), 7168, 16, blocks, self.x_tiles(), self.act_ap, ep)
        if self.chk(L, "P"):
            return
        sc = 128 ** -0.5
        ck = din["cache_c_k"]
        cv = din["cache_c_v"]
        for h in range(12):
            ai = self.rot("arena", 2)
            ar = self.arena_r[ai]
            ab, gv = self.load_head(ai, (self.qT_s[0], h * 128 * NCOL, self.qT_s[1]), h * 128,
                                    (self.kT_s[0], h * 128 * NCOL, self.kT_s[1]),
                                    (self.v_s[0], h * TT * 128 * 129, self.v_s[1]))
            bv = self.load_bias(ai, self.FrepC, 768, h, [(128 + 128 * (4 - i), 128, 128) for i in range(5)])
            f0 = self.A_B // 2 + 640
            bs = self.arena[ai][:, f0:f0 + 80].rearrange("p (a b) -> p a b", a=5)
            Ft, Fr = self.FrepC
            for i_, (off_, nk_) in enumerate([(640 - 128 * jc, 128) for jc in range(4)] + [(128, 16)]):
                self.dma(bs[0:nk_, i_, 0:16], AP(Ft, h * 128 * 768 + off_, [[767, nk_], [1, 16]]), ar, reads=[Fr], adds=[ar])
            self.op("pool", "memset", writes=[ar], ap=bv[64:128, 4, 0:64], constant=NEG)
            self.op("pool", "memset", writes=[ar], ap=bv[0:64, 0, 64:128], constant=NEG)
            self.op("act", "activation", writes=[ar], out=bv[:, 0:5, :], in_=bv[:, 0:5, :], func=AF.Exp)
            self.op("act", "activation", writes=[ar], out=bs[:, 0:4, :], in_=bs[:, 0:4, :], func=AF.Exp)
            self.op("act", "activation", writes=[ar], out=bs[0:16, 4, :], in_=bs[0:16, 4, :], func=AF.Exp)
            q = ab[:, self.A_Q:self.A_Q + NCOL]
            k = ab[:, self.A_K:self.A_K + NCOL]
            vv = ab[:, self.A_V:self.A_V + TT * 129].rearrange("p (t c) -> p t c", t=TT)
            for t in range(16):
                kts = []
                for o in range(4, -1, -1):
                    jt = t - o
                    if jt < 0:
                        continue
                    kts.append(dict(qk=[(k[:, jt * 128:(jt + 1) * 128], q[:, t * 128:(t + 1) * 128])], nk=128,
                                    v=vv[:, jt, :], bias=(bv, 4 - o), reads=[ar]))
                self.attn(kts, 128, sc, None, gv[:, t, :], [ar], self.actT[:, h, t * 128:(t + 1) * 128], self.act_r[t])
            self.flush()
            i = self.rot("ld", 2)
            ld = self.ld[i]
            lr = self.ld_r[i]
            self.dma(ld[:, :].rearrange("p (a b) -> p a b", a=4), AP(ck, h * 128, [[1536, 128], [128 * 1536, 4], [1, 128]]),
                     lr, writes=[lr])
            tpb, tpr = self.nb("tp")
            tpv = tpb.rearrange("p (a b) -> p a b", a=4)
            for c in range(4):
                self.op("pe", "transpose", reads=[lr, self.ident_r], writes=[tpr], out=tpv[:, c, :],
                        in_=ld[:, c * 128:(c + 1) * 128], identity=self.ident[:])
            self.op("act", "activation", reads=[tpr], writes=[self.kcC_r], out=self.kcC[:], in_=tpb[:, 0:512], func=AF.Copy)
            i = self.rot("ld", 2)
            ld = self.ld[i]
            lr = self.ld_r[i]
            self.dma(ld[:, :].rearrange("p (a b) -> p a b", a=4), AP(cv, h * 128, [[1536, 128], [128 * 1536, 4], [1, 128]]),
                     lr, writes=[lr])
            self.op("dve", "tensor_copy", reads=[lr], writes=[self.vcC_r], out=self.vcC[:, :, 0:128],
                    in_=ld[:, :].rearrange("p (a b) -> p a b", a=4))
            qs = q[:, 2048:2064]
            kts = []
            for jc in range(4):
                kts.append(dict(qk=[(self.kcC[:, jc * 128:(jc + 1) * 128], qs)], nk=128, v=self.vcC[:, jc, :],
                                bias=(bs, jc), reads=[ar, self.kcC_r, self.vcC_r]))
            kts.append(dict(qk=[(k[:, 2048:2064], qs)], nk=16, v=vv[0:16, 16, :], bias=(bs, 4), reads=[ar]))
            self.attn(kts, 16, sc, None, gv[0:16, 16, :], [ar], self.actT[:, h, 2048:2064], self.act_r[16])
            self.flush()

    def rope(self, x1, x2, rd, rows, nh, tt, scale_ap, out1, out2, wr):
        rp = self.rp
        rr = self.rp_r
        cs = self.cosT[0:rows, tt, :].unsqueeze(1).broadcast_to([rows, nh, 32])
        sn = self.sinT[0:rows, tt, :].unsqueeze(1).broadcast_to([rows, nh, 32])
        def t(i):
            return rp[0:rows, i, 0:nh * 32].rearrange("p (a b) -> p a b", a=nh)
        crd = [self.cos_r, self.sin_r]
        self.op("dve", "tensor_tensor", reads=rd + crd, writes=[rr], out=t(0), in0=x1, in1=cs, op=ALU.mult)
        self.op("dve", "tensor_tensor", reads=rd + crd, writes=[rr], out=t(1), in0=x2, in1=sn, op=ALU.mult)
        self.op("dve", "tensor_tensor", reads=rd + crd, writes=[rr], out=t(2), in0=x1, in1=sn, op=ALU.mult)
        self.op("dve", "tensor_tensor", reads=rd + crd, writes=[rr], out=t(3), in0=x2, in1=cs, op=ALU.mult)
        if scale_ap is None:
            self.op("dve", "tensor_tensor", reads=[rr], writes=wr, out=out1, in0=t(0), in1=t(1), op=ALU.subtract)
            self.op("dve", "tensor_tensor", reads=[rr], writes=wr, out=out2, in0=t(2), in1=t(3), op=ALU.add)
        else:
            self.op("dve", "tensor_tensor", reads=[rr], writes=[rr], out=t(4), in0=t(0), in1=t(1), op=ALU.subtract)
            self.op("dve", "tensor_tensor", reads=[rr], writes=[rr], out=t(5), in0=t(2), in1=t(3), op=ALU.add)
            self.op("dve", "tensor_tensor", reads=[rr] + rd, writes=wr, out=out1, in0=t(4), in1=scale_ap, op=ALU.mult)
            self.op("dve", "tensor_tensor", reads=[rr] + rd, writes=wr, out=out2, in0=t(5), in1=scale_ap, op=ALU.mult)

    def layer_b(self, L):
        din = self.din
        dout = self.dout
        gxq = self.gcolB[:, L:L + 1]
        cqT = self.cqT()
        blocks = [(0, 512), (512, 320), (832, 512)] + [(1344 + c * 512, 512) for c in range(4)]

        def ep(tt, rows, bi, acc, accr):
            if bi == 0:
                zv, zr = self.headnorm(acc, accr, rows, 1, 512)
                z4 = zv.rearrange("p a (c b) -> p (a c) b", c=4)

                def tail():
                    tpb, tpr = self.nb("tp")
                    tpv = tpb.rearrange("p (a b) -> p a b", a=4)
                    for c in range(4):
                        self.op("pe", "transpose", reads=[zr, self.ident_r], writes=[tpr], out=tpv[:, c, 0:rows],
                                in_=z4[:, c, :], identity=self.ident[0:rows, 0:rows])
                    g = self.gcolB[:, 14:18].unsqueeze(2).broadcast_to([128, 4, rows])
                    self.op("dve", "tensor_tensor", reads=[tpr, self.gcolB_r], adds=[self.cq_r],
                            out=cqT[:, :, tt * 128:tt * 128 + rows], in0=tpv[:, 0:4, 0:rows], in1=g, op=ALU.mult)
                self.defer(tail, 3)
            elif bi == 1:
                zv, zr = self.headnorm(acc[:, 0:256], accr, rows, 1, 256)
                i = self.rot("of", 2)
                of = self.of[i]
                orr = self.of_r[i]
                if tt < 16:
                    dst_ckv = AP(dout["b_ckv_prompt"], tt * 128 * 256, [[256, 128], [1, 256]])
                else:
                    dst_ckv = AP(dout["b_ckv_sample"], 0, [[256, 16], [1, 256]])

                def s1():
                    self.op("dve", "tensor_tensor", reads=[zr, self.gbc_r], writes=[orr], out=of[0:rows, 0:256],
                            in0=zv.rearrange("p a b -> p (a b)"), in1=self.gbc[0:rows, 0:256], op=ALU.mult)
                    self.dma(dst_ckv, of[0:rows, 0:256], orr, reads=[orr], is_output=True)
                self.defer(s1, 1)

                def tail():
                    tpb, tpr = self.nb("tp")
                    tpv = tpb.rearrange("p (a b) -> p a b", a=4)
                    for c in range(2):
                        self.op("pe", "transpose", reads=[orr, self.ident_r], writes=[tpr], out=tpv[:, c, 0:rows],
                                in_=of[0:rows, c * 128:(c + 1) * 128], identity=self.ident[0:rows, 0:rows])
                    self.op("act", "activation", reads=[tpr], adds=[self.ckvT_r], out=self.ckvT[:, :, tt * 128:tt * 128 + rows],
                            in_=tpv[:, 0:2, 0:rows], func=AF.Copy)
                self.defer(tail, 3)
                x1 = acc[:, 256:288].unsqueeze(1)
                x2 = acc[:, 288:320].unsqueeze(1)
                o1 = self.kr_all[0:rows, tt, 0:32].unsqueeze(1)
                o2 = self.kr_all[0:rows, tt, 32:64].unsqueeze(1)
                self.rope(x1, x2, [accr], rows, 1, tt, None, o1, o2, [self.kr_all_r])
                if tt < 16:
                    dst = AP(dout["b_krope_prompt"], tt * 128 * 64, [[64, 128], [1, 64]])
                else:
                    dst = AP(dout["b_krope_sample"], 0, [[64, 16], [1, 64]])
                self.dma(dst, self.kr_all[0:rows, tt, :], self.kr_all_r, reads=[self.kr_all_r], is_output=True)
                self.op("act", "activation", reads=[self.kr_all_r], writes=[self.sq_r, self.krss_r], out=self.sq[0:rows, 0:64],
                        in_=self.kr_all[0:rows, tt, :], func=AF.Square, accum_out=self.krss[0:rows, tt:tt + 1])
            elif bi == 2:
                zv, zr = self.headnorm(acc, accr, rows, 4, 128)
                self.transpose_out(zv, zr, rows, 4, gxq, dst_dram=self.qk_store(self.xqT_s, 0, 4, tt, rows),
                                   dram_res=self.xqT_s[1])
            else:
                self.gate_out(acc, accr, rows, 512, tt, (bi - 3) * 512)

        self.proj((din["b_w_in"], 0), 3392, 16, blocks, self.x_tiles(), self.act_ap, ep)
        if self.chk(L, "P"):
            return

        def epq(tt, rows, bi, acc, accr):
            h0 = bi * 2
            self.op("act", "activation", reads=[accr], writes=[self.sq_r], out=self.sq[0:rows, 0:384], in_=acc, func=AF.Square)
            st, sr = self.newstat()
            self.op("dve", "tensor_reduce", reads=[self.sq_r], writes=[sr], out=st[0:rows, 0:2],
                    in_=self.sq[0:rows, 0:384].rearrange("p (a b) -> p a b", a=2), axis=AX.X, op=ALU.add)
            self.op("dve", "tensor_scalar", reads=[sr], writes=[sr], out=st[0:rows, 4:6], in0=st[0:rows, 0:2],
                    scalar1=1.0 / 192, scalar2=EPS, op0=ALU.mult, op1=ALU.add)
            self.op("pool", "tensor_tensor", reads=[sr, self.cm05_r], writes=[sr], out=st[0:rows, 8:10],
                    in0=st[0:rows, 4:6], in1=self.cm05[0:rows, 0:2], op=ALU.pow)
            i = self.rot("zf", 2)
            zf = self.zf[i]
            zr = self.zf_r[i]
            a3 = acc.rearrange("p (a b) -> p a b", a=2)
            z3 = zf[0:rows, 0:384].rearrange("p (a b) -> p a b", a=2)
            def s1():
                self.op("dve", "tensor_tensor", reads=[accr, sr], writes=[zr], out=z3[:, :, 0:128], in0=a3[:, :, 0:128],
                        in1=st[0:rows, 8:10].unsqueeze(2).broadcast_to([rows, 2, 128]), op=ALU.mult)
                rs32 = st[0:rows, 8:10].unsqueeze(2).broadcast_to([rows, 2, 32])
                self.rope(a3[:, :, 128:160], a3[:, :, 160:192], [accr, sr], rows, 2, tt, rs32, z3[:, :, 128:160],
                          z3[:, :, 160:192], [zr])
            self.defer(s1, 1)

            def tail():
                tpb, tpr = self.nb("tp")
                tpv = tpb.rearrange("p (a b) -> p a b", a=4)
                for c in range(2):
                    self.op("pe", "transpose", reads=[zr, self.ident_r], writes=[tpr], out=tpv[:, c, 0:rows],
                            in_=z3[:, c, 0:128], identity=self.ident[0:rows, 0:rows])
                    self.op("pe", "transpose", reads=[zr, self.ident_r], writes=[tpr], out=tpv[0:64, 2 + c, 0:rows],
                            in_=z3[:, c, 128:192], identity=self.ident[0:rows, 0:rows])
                i = self.rot("tb", 3)
                tb = self.tb[i]
                tr = self.tb_r[i]
                self.op("act", "activation", reads=[tpr, self.gcolB_r], writes=[tr], out=tb[:, 0:2, 0:rows],
                        in_=tpv[:, 0:2, 0:rows], func=AF.Copy, scale=self.gcolB[:, 20:21])
                self.op("act", "activation", reads=[tpr, self.gcolB_r], writes=[tr], out=tb[0:64, 2:4, 0:rows],
                        in_=tpv[0:64, 2:4, 0:rows], func=AF.Copy, scale=self.gcolB[0:64, 21:22])
                self.dma(self.qk_store(self.q192_s, h0, 2, tt, rows, hrows=192), tb[:, 0:2, 0:rows], tr, reads=[tr],
                         adds=[self.q192_s[1]])
                self.dma(self.qk_store(self.q192_s, h0, 2, tt, rows, dpart=64, hrows=192, r0=128), tb[0:64, 2:4, 0:rows], tr,
                         reads=[tr], adds=[self.q192_s[1]])
            self.defer(tail, 3)

        self.proj((din["b_w_q_b"], 0), 2304, 4, [(c * 384, 384) for c in range(6)],
                  [(t, rows_of(t), [self.cq_r]) for t in range(TT)],
                  lambda key, k: cqT[:, k, key * 128:key * 128 + rows_of(key)], epq)

        def epkv(key, rows, bi, acc, accr):
            kind_, idx = key
            h0 = bi * 2
            if kind_ == "c":
                kr = self.krt[idx % 3][0:rows, :]
                krr = [self.krt_r[idx % 3]]
                ssap = self.krss_c[0:rows, idx:idx + 1]
                ssr = [self.krss_c_r]
            else:
                kr = self.kr_all[0:rows, idx, :]
                krr = [self.kr_all_r]
                ssap = self.krss[0:rows, idx:idx + 1]
                ssr = [self.krss_r]
            a4 = acc.rearrange("p (a b) -> p a b", a=2)
            sq2 = self.sq[0:rows, 0:256].rearrange("p (a b) -> p a b", a=2)
            self.op("act", "activation", reads=[accr], writes=[self.sq_r], out=sq2, in_=a4[:, :, 0:128], func=AF.Square)
            st, sr = self.newstat()
            self.op("dve", "tensor_reduce", reads=[self.sq_r], writes=[sr], out=st[0:rows, 0:2], in_=sq2, axis=AX.X, op=ALU.add)
            self.op("dve", "tensor_scalar", reads=[sr] + ssr, writes=[sr], out=st[0:rows, 12:14], in0=st[0:rows, 0:2],
                    scalar1=ssap, scalar2=None, op0=ALU.add)
            self.op("dve", "tensor_scalar", reads=[sr], writes=[sr], out=st[0:rows, 4:6], in0=st[0:rows, 12:14],
                    scalar1=1.0 / 192, scalar2=EPS, op0=ALU.mult, op1=ALU.add)
            self.op("pool", "tensor_tensor", reads=[sr, self.cm05_r], writes=[sr], out=st[0:rows, 8:10],
                    in0=st[0:rows, 4:6], in1=self.cm05[0:rows, 0:2], op=ALU.pow)
            rs2 = st[0:rows, 8:10].unsqueeze(2)
            i = self.rot("zf", 2)
            zf = self.zf[i]
            zr = self.zf_r[i]
            z3 = zf[0:rows, 0:384].rearrange("p (a b) -> p a b", a=2)
            def s1():
                self.op("dve", "tensor_tensor", reads=[accr, sr], writes=[zr], out=z3[:, :, 0:128], in0=a4[:, :, 0:128],
                        in1=rs2.broadcast_to([rows, 2, 128]), op=ALU.mult)
                self.op("dve", "tensor_tensor", reads=krr + [sr], writes=[zr], out=z3[:, :, 128:192],
                        in0=kr.unsqueeze(1).broadcast_to([rows, 2, 64]), in1=rs2.broadcast_to([rows, 2, 64]), op=ALU.mult)
            self.defer(s1, 1)
            if kind_ == "p":
                kd, vd, tile, width, nt = self.k192_s, self.v_s, idx, NCOL, TT
            else:
                tile = idx if kind_ == "c" else 32
                kd, vd, width, nt = self.k192s_s, self.vs_s, 4224, 33

            def tail():
                tpb, tpr = self.nb("tp")
                tpv = tpb.rearrange("p (a b) -> p a b", a=4)
                for c in range(2):
                    self.op("pe", "transpose", reads=[zr, self.ident_r], writes=[tpr], out=tpv[:, c, 0:rows],
                            in_=z3[:, c, 0:128], identity=self.ident[0:rows, 0:rows])
                    self.op("pe", "transpose", reads=[zr, self.ident_r], writes=[tpr], out=tpv[0:64, 2 + c, 0:rows],
                            in_=z3[:, c, 128:192], identity=self.ident[0:rows, 0:rows])
                i = self.rot("tb", 3)
                tb = self.tb[i]
                tr = self.tb_r[i]
                self.op("act", "activation", reads=[tpr, self.gcolB_r], writes=[tr], out=tb[:, 0:2, 0:rows],
                        in_=tpv[:, 0:2, 0:rows], func=AF.Copy, scale=self.gcolB[:, 22:23])
                self.op("act", "activation", reads=[tpr, self.gcolB_r], writes=[tr], out=tb[0:64, 2:4, 0:rows],
                        in_=tpv[0:64, 2:4, 0:rows], func=AF.Copy, scale=self.gcolB[0:64, 23:24])
                self.dma(self.qk_store(kd, h0, 2, tile, rows, width=width, hrows=192), tb[:, 0:2, 0:rows], tr, reads=[tr],
                         adds=[kd[1]])
                self.dma(self.qk_store(kd, h0, 2, tile, rows, width=width, dpart=64, hrows=192, r0=128), tb[0:64, 2:4, 0:rows],
                         tr, reads=[tr], adds=[kd[1]])
            self.defer(tail, 3)
            self.v_out(a4[:, :, 128:256], accr, rows, 2, self.v_store(vd, h0, 2, tile, rows, ntiles=nt), vd[1])

        tiles = [(("p", t), 128, [self.ckvT_r]) for t in range(16)] + [(("s", 16), 16, [self.ckvT_r])]

        def actkv(key, k):
            kind_, idx = key
            if kind_ == "c":
                return self.ckvt[idx % 2][:, k, :]
            rows = rows_of(idx)
            return self.ckvT[:, k, idx * 128: idx * 128 + rows]

        cache_tiles = []
        for jc in range(32):
            cache_tiles.append((("c", jc), 128, [self.ckvt_r[jc % 2]]))
        self._kv_prep_pending = True
        self.proj_kv((din["b_w_kv_b"], 0), tiles, cache_tiles, actkv, epkv)

        sc = 192 ** -0.5
        QB, KB = 8736, 10912
        for h in range(12):
            ai = self.rot("arena", 2)
            ar = self.arena_r[ai]
            ab = self.arena_bf(ai)
            Q, Qr = self.q192_s
            Kt, Kr = self.k192_s
            self.dma(ab[0:96, self.A_Q:self.A_Q + 2048], AP(Q, h * 192 * NCOL, [[NCOL, 96], [1, 2048]]), ar, reads=[Qr], adds=[ar])
            self.dma(ab[0:96, QB:QB + 2048], AP(Q, (h * 192 + 96) * NCOL, [[NCOL, 96], [1, 2048]]), ar, reads=[Qr], adds=[ar])
            self.dma(ab[0:96, self.A_K:self.A_K + 2048], AP(Kt, h * 192 * NCOL, [[NCOL, 96], [1, 2048]]), ar, reads=[Kr], adds=[ar])
            self.dma(ab[0:96, KB:KB + 2048], AP(Kt, (h * 192 + 96) * NCOL, [[NCOL, 96], [1, 2048]]), ar, reads=[Kr], adds=[ar])
            G, Gr = self.gate_s
            gv = ab[:, self.A_G:self.A_G + 2176].rearrange("p (t c) -> p t c", t=TT)
            self.dma(gv[:, 0:16, :], AP(G, h * 128, [[2048, 128], [128 * 2048, 16], [1, 128]]), ar, reads=[Gr], adds=[ar])
            vv = ab[:, self.A_V:self.A_V + TT * 129].rearrange("p (t c) -> p t c", t=TT)
            self.dma(vv[:, 0:16, :], AP(self.v_s[0], h * TT * 128 * 129, [[129, 128], [128 * 129, 16], [1, 129]]), ar,
                     reads=[self.v_s[1]], adds=[ar])
            qa = ab[0:96, self.A_Q:self.A_Q + NCOL]
            qb = ab[0:96, QB:QB + NCOL]
            ka = ab[0:96, self.A_K:self.A_K + 2048]
            kb = ab[0:96, KB:KB + 2048]
            for t in range(16):
                kts = []
                for jt in range(t + 1):
                    kts.append(dict(qk=[(ka[:, jt * 128:(jt + 1) * 128], qa[:, t * 128:(t + 1) * 128]),
                                        (kb[:, jt * 128:(jt + 1) * 128], qb[:, t * 128:(t + 1) * 128])], nk=128,
                                    v=vv[:, jt, :], bias=None, diag=(jt == t), reads=[ar]))
                self.attn(kts, 128, sc, None, gv[:, t, :], [ar], self.actT[:, h, t * 128:(t + 1) * 128], self.act_r[t])
            self.flush()
        KA0, KB0, V0, QA0, QB0, G0 = 0, 4224, 8448, 12708, 12724, 12740
        for h in range(12):
            ai = self.rot("arena", 2)
            ar = self.arena_r[ai]
            ab = self.arena_bf(ai)
            Q, Qr = self.q192_s
            Kt, Kr = self.k192s_s
            ka = ab[0:96, KA0:KA0 + 4224]
            kb = ab[0:96, KB0:KB0 + 4224]
            vv = ab[:, V0:V0 + 33 * 129].rearrange("p (t c) -> p t c", t=33)
            qa = ab[0:96, QA0:QA0 + 16]
            qb = ab[0:96, QB0:QB0 + 16]
            gt = ab[0:16, G0:G0 + 128]
            self.dma(ka[:, 0:4112], AP(Kt, h * 192 * 4224, [[4224, 96], [1, 4112]]), ar, reads=[Kr], adds=[ar])
            self.dma(kb[:, 0:4112], AP(Kt, (h * 192 + 96) * 4224, [[4224, 96], [1, 4112]]), ar, reads=[Kr], adds=[ar])
            self.dma(vv[:, 0:32, :], AP(self.vs_s[0], h * 33 * 128 * 129, [[129, 128], [128 * 129, 32], [1, 129]]), ar,
                     reads=[self.vs_s[1]], adds=[ar])
            self.dma(vv[0:16, 32, :], AP(self.vs_s[0], (h * 33 + 32) * 128 * 129, [[129, 16], [1, 129]]), ar,
                     reads=[self.vs_s[1]], adds=[ar])
            self.dma(qa, AP(Q, h * 192 * NCOL + 2048, [[NCOL, 96], [1, 16]]), ar, reads=[Qr], adds=[ar])
            self.dma(qb, AP(Q, (h * 192 + 96) * NCOL + 2048, [[NCOL, 96], [1, 16]]), ar, reads=[Qr], adds=[ar])
            self.dma(gt, AP(self.gate_s[0], 2048 * 2048 + h * 128, [[2048, 16], [1, 128]]), ar, reads=[self.gate_s[1]], adds=[ar])
            s0, s0r = self.nb("s")
            s1, s1r = self.nb("s")
            for jc in range(32):
                self.op("pe", "matmul", reads=[ar], writes=[s0r], out=s0[:, jc * 16:(jc + 1) * 16],
                        lhsT=ka[:, jc * 128:(jc + 1) * 128], rhs=qa, start=True, stop=False)
                self.op("pe", "matmul", reads=[ar], writes=[s0r], out=s0[:, jc * 16:(jc + 1) * 16],
                        lhsT=kb[:, jc * 128:(jc + 1) * 128], rhs=qb, start=False, stop=True)
            self.op("pe", "matmul", reads=[ar], writes=[s1r], out=s1[0:16, 0:16], lhsT=ka[:, 4096:4112], rhs=qa,
                    start=True, stop=False)
            self.op("pe", "matmul", reads=[ar], writes=[s1r], out=s1[0:16, 0:16], lhsT=kb[:, 4096:4112], rhs=qb,
                    start=False, stop=True)
            p0, p0r = self.pexp[0], self.pexp_r[0]
            p1, p1r = self.pexp[1], self.pexp_r[1]
            self.op("act", "activation", reads=[s0r], writes=[p0r], out=p0[:, 0:512], in_=s0[:, 0:512], func=AF.Exp, scale=sc)
            self.op("act", "activation", reads=[s1r], writes=[p1r], out=p1[0:16, 0:16], in_=s1[0:16, 0:16], func=AF.Exp, scale=sc)
            ob, orr = self.nb("o")
            for jc in range(32):
                self.op("pe", "matmul", reads=[p0r, ar], writes=[orr], out=ob[0:16, 0:129], lhsT=p0[:, jc * 16:(jc + 1) * 16],
                        rhs=vv[:, jc, :], start=(jc == 0), stop=False)
            self.op("pe", "matmul", reads=[p1r, ar], writes=[orr], out=ob[0:16, 0:129], lhsT=p1[0:16, 0:16], rhs=vv[0:16, 32, :],
                    start=False, stop=True)
            self.attn_post(ob, orr, 16, None, gt, [ar], self.actT[:, h, 2048:2064], self.act_r[16])
            self.flush()

    def proj_kv(self, W, tiles, cache_tiles, actkv, epkv):
        din = self.din
        wh, woff = W
        ai = self.rot("arena", 2)
        ar = self.arena_r[ai]
        wv = self.arena_bf(ai)[:, 0:2 * 3072].rearrange("p (k w) -> p k w", k=2)
        for c in range(6):
            src = AP(wh, woff + c * 512, [[3072, 128], [128 * 3072, 2], [1, 512]])
            self.dma(wv[:, :, c * 512:(c + 1) * 512], src, ar, adds=[ar], q="pool")
        ck = din["cache_b_ckv"]
        ckr = din["cache_b_krope"]

        def prep(idx):
            b = idx % 2
            i = self.rot("ld", 2)
            ld = self.ld[i]
            lr = self.ld_r[i]
            self.dma(ld[:, 0:256], AP(ck, idx * 128 * 256, [[256, 128], [1, 256]]), lr, writes=[lr])
            b3 = idx % 3
            self.dma(self.krt[b3][:], AP(ckr, idx * 128 * 64, [[64, 128], [1, 64]]), self.krt_r[b3], writes=[self.krt_r[b3]])
            tpb, tpr = self.nb("tp")
            tpv = tpb.rearrange("p (a b) -> p a b", a=4)
            for c in range(2):
                self.op("pe", "transpose", reads=[lr, self.ident_r], writes=[tpr], out=tpv[:, c, :],
                        in_=ld[:, c * 128:(c + 1) * 128], identity=self.ident[:])
            self.op("act", "activation", reads=[tpr], writes=[self.ckvt_r[b]], out=self.ckvt[b][:], in_=tpv[:, 0:2, :],
                    func=AF.Copy)
            self.op("act", "activation", reads=[self.krt_r[b3]], writes=[self.rp_r, self.krss_c_r], out=self.rp[:, 0, :],
                    in_=self.krt[b3][:], func=AF.Square, accum_out=self.krss_c[:, idx:idx + 1])

        allt = tiles + cache_tiles
        for ti, (key, rows, rd) in enumerate(allt):
            kind_, idx = key
            if kind_ == "c" and idx == 0:
                prep(0)
            if ti + 1 < len(allt) and allt[ti + 1][0][0] == "c" and allt[ti + 1][0][1] > 0:
                prep(allt[ti + 1][0][1])
            for bi in range(6):
                acc, accr = self.nb("acc")
                for k in range(2):
                    self.op("pe", "matmul", reads=[ar] + rd, writes=[accr], out=acc[0:rows, 0:512], lhsT=actkv(key, k),
                            rhs=wv[:, k, bi * 512:(bi + 1) * 512], start=(k == 0), stop=(k == 1))
                self.step()
                epkv(key, rows, bi, acc[0:rows, 0:512], accr)
        self.flush()

    def mem_heads(self, L):
        sc = 128 ** -0.5
        for hm in range(4):
            ai = self.rot("arena", 2)
            ar = self.arena_r[ai]
            ab, gv = self.load_head(ai, (self.xqT_s[0], hm * 128 * NCOL, self.xqT_s[1]), (12 + hm) * 128, None, None,
                                    with_kv=False)
            q = ab[:, self.A_Q:self.A_Q + NCOL]
            for t in range(16):
                kts = [dict(qk=[(self.mkT[:, hm, jt * 128:(jt + 1) * 128], q[:, t * 128:(t + 1) * 128])], nk=128,
                            v=self.mv[:, jt, hm, :], bias=None, reads=[ar, self.mkT_r, self.mv_r]) for jt in range(2)]
                self.attn(kts, 128, sc, None, gv[:, t, :], [ar], self.actT[:, 12 + hm, t * 128:(t + 1) * 128], self.act_r[t])
            qs = q[:, 2048:2064]
            kts = [dict(qk=[(self.mkTs[:, hm, jt * 128:(jt + 1) * 128], qs)], nk=128, v=self.mvs[:, jt, hm, :], bias=None,
                        reads=[ar, self.mkTs_r, self.mvs_r]) for jt in range(2)]
            self.attn(kts, 16, sc, None, gv[0:16, 16, :], [ar], self.actT[:, 12 + hm, 2048:2064], self.act_r[16])
            self.flush()

    def out_phase(self, L):
        din = self.din
        last = (L == self.n_layers - 1)
        xnew, xnew_r = self.xres[L % 2]
        srcs = self._xsrc
        if L > 0:
            xold_r = [self.xres[(L - 1) % 2][1]]
        else:
            xold_r = []

        pend = {}
        order = [(tt, bi) for bi in range(4) for tt in range(TT)]

        def issue(n):
            tt, bi = order[n]
            rows = rows_of(tt)
            i = self.rot("ld", 2)
            ld = self.ld[i]
            lr = self.ld_r[i]
            s = srcs[tt][0]
            src = AP(s.tensor, s.offset + bi * 512, [[D, rows], [1, 512]])
            self.dma(ld[0:rows, :], src, lr, reads=xold_r, writes=[lr])
            pend[(tt, bi)] = (ld, lr)

        issue(0)

        def ep(tt, rows, bi, acc, accr):
            n = order.index((tt, bi))
            if n + 1 < len(order):
                issue(n + 1)
            ld, lr = pend.pop((tt, bi))
            j = self.rot("of", 2)
            of = self.of[j]
            orr = self.of_r[j]
            self.op("dve", "tensor_tensor", reads=[accr, lr], writes=[orr], out=of[0:rows, :], in0=acc, in1=ld[0:rows, :],
                    op=ALU.add)
            if last:
                if tt < 16:
                    dst = AP(self.dout["y_prompt"], tt * 128 * D + bi * 512, [[D, 128], [1, 512]])
                else:
                    dst = AP(self.dout["y_sample"], bi * 512, [[D, 16], [1, 512]])
                self.dma(dst, of[0:rows, :], orr, reads=[orr], is_output=True)
            else:
                dst = AP(xnew, tt * 128 * D + bi * 512, [[D, rows], [1, 512]])
                self.dma(dst, of[0:rows, :], orr, reads=[orr], adds=[xnew_r])

        self.proj((din["w_out"], L * 2048 * 2048), 2048, 16, [(c * 512, 512) for c in range(4)], self.x_tiles(),
                  self.act_ap, ep)


def _consts():
    ident = np.eye(128, dtype=np.float32)
    rel = np.arange(384) - 128
    half, max_exact = 16, 8
    ret = np.where(rel < 0, half, 0)
    n = np.abs(rel)
    nf = np.maximum(n, 1).astype(np.float32)
    large = max_exact + (np.log(nf / max_exact) / np.float32(np.log(128 / max_exact)) * (half - max_exact)).astype(np.int32)
    large = np.minimum(large, half - 1)
    bucket = ret + np.where(n < max_exact, n, large)
    oh = np.zeros((32, 384), np.float32)
    oh[bucket, np.arange(384)] = 1.0
    pos = np.zeros((128, 17), np.float32)
    for tt in range(16):
        pos[:, tt] = tt * 128 + np.arange(128)
    pos[:, 16] = 4096 + np.arange(128)
    inv = (np.float32(10000.0) ** (-np.arange(32, dtype=np.float32) / np.float32(32))).astype(np.float32)
    ang = (pos[:, :, None] * inv[None, None, :]).astype(np.float32)
    return ident, oh, np.cos(ang).astype(np.float32), np.sin(ang).astype(np.float32)


_PROG = {}


def kernel(**inputs):
    n = 8
    if "p" not in _PROG:
        _PROG["p"] = Prog()
    prog = _PROG["p"]
    ident, oh, cs, sn = _consts()
    per_batch = {"x_prompt": 0, "x_sample": 0, "mem_prompt": 0, "cache_a_k": 1, "cache_a_v": 1, "cache_b_ckv": 1,
                 "cache_b_krope": 1, "cache_c_k": 1, "cache_c_v": 1, "cache_mem_k": 1, "cache_mem_v": 1}
    shapes = dict(IN_SPECS)
    in_maps = []
    for b in range(n):
        m = {}
        for name, shp in IN_SPECS:
            if name.startswith("k_"):
                continue
            a = np.asarray(inputs[name], dtype=np.float32)
            if name in per_batch:
                a = np.take(a, b, axis=per_batch[name])
                if name in ("cache_b_ckv", "cache_b_krope", "cache_c_k", "cache_c_v"):
                    a = a[0]
            m[name] = np.ascontiguousarray(a).reshape(shp)
        m["k_ident"], m["k_t5oh"], m["k_cos"], m["k_sin"] = ident, oh, cs, sn
        in_maps.append(m)
    res = run_bass_kernel_spmd(prog.nc, in_maps, core_ids=list(range(n)))
    r = res.results

    def st(name, shape_fn):
        return np.stack([shape_fn(r[b][name]) for b in range(n)])

    y_p = st("y_prompt", lambda a: a)
    y_s = st("y_sample", lambda a: a)
    def lay(name, nl, rows, kvh):
        return np.stack([r[b][name].reshape(nl, rows, kvh, 128) for b in range(n)], axis=1)
    outs = (
        y_p, y_s,
        lay("a_k_prompt", 2, 128, 4), lay("a_v_prompt", 2, 128, 4), lay("a_k_sample", 2, 16, 4), lay("a_v_sample", 2, 16, 4),
        st("b_ckv_prompt", lambda a: a)[None], st("b_krope_prompt", lambda a: a)[None],
        st("b_ckv_sample", lambda a: a)[None], st("b_krope_sample", lambda a: a)[None],
        lay("c_k_prompt", 1, 512, 12), lay("c_v_prompt", 1, 512, 12), lay("c_k_sample", 1, 16, 12), lay("c_v_sample", 1, 16, 12),
        lay("mem_k_prompt", 4, 256, 4), lay("mem_v_prompt", 4, 256, 4),
    )
    return tuple(np.ascontiguousarray(o, dtype=np.float32) for o in outs)
```
